# Optimizing a Trainium2 kernel written in Bass

```python
import math
import jax, jax.numpy as jnp
from jax import lax
import numpy as np

D_MODEL = 1024
BATCH = 8
SEQ = 4096
DEPTH = 2

N_HEADS_A = 8
HEAD_DIM_A = 128
CONV_A = 4
CHUNK = 64
N_HEADS_B = 16
N_KV_B = 4
HEAD_DIM_B = 64
WINDOW = 128
D_FF = 2816
FFN_CONV = 3
PLE_DIM = 256
EPS = 1e-6
N_A_LAYERS = (DEPTH + 1) // 2
N_B_LAYERS = DEPTH // 2

kernel_name = "hybrid_deltanet_swa_convffn_ple"


def rmsnorm(x, w):
    x32 = x.astype(jnp.float32)
    y = x32 * lax.rsqrt(jnp.mean(x32 * x32, axis=-1, keepdims=True) + EPS)
    return (y * w.astype(jnp.float32)).astype(x.dtype)


def l2norm(x):
    x32 = x.astype(jnp.float32)
    return x32 * lax.rsqrt(jnp.sum(x32 * x32, axis=-1, keepdims=True) + EPS)


def causal_dwconv(x, w):
    k_w = w.shape[0]
    t = x.shape[1]
    xp = jnp.pad(x, ((0, 0), (k_w - 1, 0), (0, 0)))
    y = w[0] * xp[:, 0:t]
    for i in range(1, k_w):
        y = y + w[i] * xp[:, i:i + t]
    return y


def gated_delta_rule(q, k, v, g, beta):
    b_, t_, h_, dk = q.shape
    dv = v.shape[-1]
    n_ch = t_ // CHUNK
    f32 = jnp.float32

    def chunks(a):
        return a.astype(f32).reshape(b_, n_ch, CHUNK, h_, a.shape[-1]).transpose(1, 0, 3, 2, 4)

    qc = chunks(q) * (dk ** -0.5)
    kc = chunks(k)
    vc = chunks(v)
    gcs = jnp.cumsum(g.astype(f32).reshape(b_, n_ch, CHUNK, h_).transpose(1, 0, 3, 2), axis=-1)
    bc = beta.astype(f32).reshape(b_, n_ch, CHUNK, h_).transpose(1, 0, 3, 2)

    idx = jnp.arange(CHUNK)
    incl = idx[:, None] >= idx[None, :]
    strict = idx[:, None] > idx[None, :]
    diff = gcs[..., :, None] - gcs[..., None, :]
    decay = jnp.exp(jnp.where(incl, diff, -jnp.inf))

    kk = jnp.einsum('nbhik,nbhjk->nbhij', kc, kc)
    lmat = jnp.where(strict, bc[..., :, None] * kk * decay, 0.0)
    eye_l = jnp.eye(CHUNK, dtype=f32) + lmat
    rhs = jnp.concatenate([bc[..., None] * vc,
                           (bc * jnp.exp(gcs))[..., None] * kc], axis=-1)
    sol = lax.linalg.triangular_solve(eye_l, rhs, left_side=True, lower=True, unit_diagonal=True)
    u0 = sol[..., :dv]
    wk = sol[..., dv:]

    qk = jnp.einsum('nbhik,nbhjk->nbhij', qc, kc) * decay
    q_dec = qc * jnp.exp(gcs)[..., None]
    k_dec = kc * jnp.exp(gcs[..., -1:] - gcs)[..., None]
    g_last = jnp.exp(gcs[..., -1])

    def step(s, inp):
        qd, kd, qkc, u0c, wc, gl = inp
        u = u0c - jnp.einsum('bhck,bhkv->bhcv', wc, s)
        o = jnp.einsum('bhck,bhkv->bhcv', qd, s) + jnp.einsum('bhij,bhjv->bhiv', qkc, u)
        s = gl[..., None, None] * s + jnp.einsum('bhck,bhcv->bhkv', kd, u)
        return s, o

    s0 = jnp.zeros((b_, h_, dk, dv), f32)
    _, o = lax.scan(step, s0, (q_dec, k_dec, qk, u0, wk, g_last))
    return o.transpose(1, 0, 3, 2, 4).reshape(b_, t_, h_, dv)


def deltanet_mixer(x, w_in, conv_w, a_log, dt_bias, norm_w, w_out):
    b_, t_, _ = x.shape
    hk = N_HEADS_A * HEAD_DIM_A
    proj = x @ w_in
    qkv = jax.nn.silu(causal_dwconv(proj[..., :3 * hk], conv_w))
    z = proj[..., 3 * hk:4 * hk].reshape(b_, t_, N_HEADS_A, HEAD_DIM_A)
    b_logit = proj[..., 4 * hk:4 * hk + N_HEADS_A]
    a_in = proj[..., 4 * hk + N_HEADS_A:]
    q = l2norm(qkv[..., :hk].reshape(b_, t_, N_HEADS_A, HEAD_DIM_A))
    k = l2norm(qkv[..., hk:2 * hk].reshape(b_, t_, N_HEADS_A, HEAD_DIM_A))
    v = qkv[..., 2 * hk:].reshape(b_, t_, N_HEADS_A, HEAD_DIM_A)
    beta = jax.nn.sigmoid(b_logit.astype(jnp.float32))
    g = -jnp.exp(a_log.astype(jnp.float32)) * jax.nn.softplus(a_in.astype(jnp.float32) + dt_bias.astype(jnp.float32))
    o = gated_delta_rule(q, k, v, g, beta).astype(x.dtype)
    o = rmsnorm(o, norm_w) * jax.nn.silu(z)
    return o.reshape(b_, t_, hk) @ w_out


def alibi_slopes(n_heads):
    return 2.0 ** (-8.0 * jnp.arange(1, n_heads + 1, dtype=jnp.float32) / n_heads)


def swa_mixer(x, w_in, sinks, w_out):
    b_, t_, _ = x.shape
    n_blk = t_ // WINDOW
    grp = N_HEADS_B // N_KV_B
    qd = N_HEADS_B * HEAD_DIM_B
    kd = N_KV_B * HEAD_DIM_B
    proj = x @ w_in
    q = proj[..., :qd].reshape(b_, n_blk, WINDOW, N_KV_B, grp, HEAD_DIM_B)
    k = proj[..., qd:qd + kd].reshape(b_, n_blk, WINDOW, N_KV_B, HEAD_DIM_B)
    v = proj[..., qd + kd:].reshape(b_, n_blk, WINDOW, N_KV_B, HEAD_DIM_B)

    def with_prev(a):
        prev = jnp.pad(a[:, :-1], ((0, 0), (1, 0), (0, 0), (0, 0), (0, 0)))
        return jnp.concatenate([prev, a], axis=2)

    kb, vb = with_prev(k), with_prev(v)
    scores = jnp.einsum('bnqhgd,bnkhd->bhgnqk', q, kb).astype(jnp.float32) * (HEAD_DIM_B ** -0.5)
    qi = jnp.arange(WINDOW)[:, None]
    kj = jnp.arange(2 * WINDOW)[None, :]
    dist = qi + WINDOW - kj
    blk = jnp.arange(n_blk)[:, None, None]
    valid = (dist >= 0) & (dist < WINDOW) & (blk * WINDOW - WINDOW + kj >= 0)
    slopes = alibi_slopes(N_HEADS_B).reshape(N_KV_B, grp)
    logits = scores - slopes[None, :, :, None, None, None] * dist.astype(jnp.float32)
    logits = jnp.where(valid, logits, -jnp.inf)
    sink = sinks.astype(jnp.float32).reshape(1, N_KV_B, grp, 1, 1, 1)
    m = jnp.maximum(jnp.max(logits, axis=-1, keepdims=True), sink)
    e = jnp.exp(logits - m)
    probs = e / (jnp.sum(e, axis=-1, keepdims=True) + jnp.exp(sink - m))
    out = jnp.einsum('bhgnqk,bnkhd->bnqhgd', probs.astype(x.dtype), vb).reshape(b_, t_, qd)
    return out @ w_out


def conv_ffn(x, w_up, conv_w, w_down):
    u = causal_dwconv(x @ w_up, conv_w)
    gate, val = u[..., :D_FF], u[..., D_FF:]
    return (jax.nn.silu(gate) * val) @ w_down


def setup_inputs(seed: int = 0) -> dict:
    key = jax.random.key(seed)
    ks = jax.random.split(key, 24)
    f32 = jnp.float32
    hk = N_HEADS_A * HEAD_DIM_A

    def dense(k, shape, fan_in):
        return jax.random.normal(k, shape, f32) * (fan_in ** -0.5)

    def gain(k, shape):
        return 1.0 + 0.02 * jax.random.normal(k, shape, f32)

    x = jax.random.normal(ks[0], (BATCH, SEQ, D_MODEL), f32)
    p = jax.random.normal(ks[1], (DEPTH, BATCH, SEQ, PLE_DIM), f32)
    norm_mix = gain(ks[2], (DEPTH, D_MODEL))
    norm_ffn = gain(ks[3], (DEPTH, D_MODEL))
    norm_ple = gain(ks[4], (DEPTH, D_MODEL))
    norm_final = gain(ks[5], (D_MODEL,))

    a_w_in = dense(ks[6], (N_A_LAYERS, D_MODEL, 4 * hk + 2 * N_HEADS_A), D_MODEL)
    a_conv = jax.random.normal(ks[7], (N_A_LAYERS, CONV_A, 3 * hk), f32) * (CONV_A ** -0.5)
    a_log = jnp.log(jax.random.uniform(ks[8], (N_A_LAYERS, N_HEADS_A), f32, 1.0, 16.0))
    dt = jnp.exp(jax.random.uniform(ks[9], (N_A_LAYERS, N_HEADS_A), f32, math.log(1e-3), math.log(1e-1)))
    a_dt_bias = dt + jnp.log(-jnp.expm1(-dt))
    a_norm = gain(ks[10], (N_A_LAYERS, HEAD_DIM_A))
    a_w_out = dense(ks[11], (N_A_LAYERS, hk, D_MODEL), hk)

    qd = N_HEADS_B * HEAD_DIM_B
    kd = N_KV_B * HEAD_DIM_B
    b_w_in = dense(ks[12], (N_B_LAYERS, D_MODEL, qd + 2 * kd), D_MODEL)
    b_sinks = jax.random.normal(ks[13], (N_B_LAYERS, N_HEADS_B), f32)
    b_w_out = dense(ks[14], (N_B_LAYERS, qd, D_MODEL), qd)

    f_w_up = dense(ks[15], (DEPTH, D_MODEL, 2 * D_FF), D_MODEL)
    f_conv = jax.random.normal(ks[16], (DEPTH, FFN_CONV, 2 * D_FF), f32) * (FFN_CONV ** -0.5)
    f_w_down = dense(ks[17], (DEPTH, D_FF, D_MODEL), D_FF)

    ple_w_proj = dense(ks[18], (DEPTH, PLE_DIM, D_MODEL), PLE_DIM)
    ple_w_gate = dense(ks[19], (DEPTH, D_MODEL, D_MODEL), D_MODEL)

    return {"x": x, "p": p, "norm_mix": norm_mix, "norm_ffn": norm_ffn,
            "norm_ple": norm_ple, "norm_final": norm_final,
            "a_w_in": a_w_in, "a_conv": a_conv, "a_log": a_log, "a_dt_bias": a_dt_bias,
            "a_norm": a_norm, "a_w_out": a_w_out,
            "b_w_in": b_w_in, "b_sinks": b_sinks, "b_w_out": b_w_out,
            "f_w_up": f_w_up, "f_conv": f_conv, "f_w_down": f_w_down,
            "ple_w_proj": ple_w_proj, "ple_w_gate": ple_w_gate}


def reference(x, p, norm_mix, norm_ffn, norm_ple, norm_final,
              a_w_in, a_conv, a_log, a_dt_bias, a_norm, a_w_out,
              b_w_in, b_sinks, b_w_out,
              f_w_up, f_conv, f_w_down,
              ple_w_proj, ple_w_gate):
    h = x
    for i in range(DEPTH):
        hn = rmsnorm(h, norm_mix[i])
        j = i // 2
        if i % 2 == 0:
            mix = deltanet_mixer(hn, a_w_in[j], a_conv[j], a_log[j], a_dt_bias[j], a_norm[j], a_w_out[j])
        else:
            mix = swa_mixer(hn, b_w_in[j], b_sinks[j], b_w_out[j])
        h = h + mix
        h = h + conv_ffn(rmsnorm(h, norm_ffn[i]), f_w_up[i], f_conv[i], f_w_down[i])
        gate = jax.nn.sigmoid(rmsnorm(h, norm_ple[i]) @ ple_w_gate[i])
        h = h + gate * (p[i] @ ple_w_proj[i])
    return rmsnorm(h, norm_final)
```

```python
import contextlib
import numpy as np
import concourse.bass as bass
import concourse.mybir as mybir
from concourse.bass_utils import run_bass_kernel_spmd
from concourse.alu_op_type import AluOpType as ALU

F32 = mybir.dt.float32
BF16 = mybir.dt.bfloat16
F32R = mybir.dt.float32r
AF = mybir.ActivationFunctionType

T_SEQ = 4096
D = 1024
TT = 512
NB = TT // 128
NT = T_SEQ // TT
DFF = 2816
NFF = DFF // 128
EPS = 1e-6
BIG = 30000.0
import os
DBG = float(os.environ.get('K_DBG', '99'))
STAGES = os.environ.get('K_STAGES', 'delta,ffn0,ple0,swa,ffn1,ple1').split(',')


class Buf:
    __slots__ = ("name", "w", "r", "excl")

    def __init__(self, name="", excl=False):
        self.name = name
        self.w = None
        self.r = []
        self.excl = excl


class Sched:
    ENGS = ("pe", "act", "dve", "pool", "sp")

    def __init__(self, nc, n_dma_sems=32, same_engine_sync=True):
        self.nc = nc
        self.ops = {e: [] for e in self.ENGS}
        self.count = {e: 0 for e in self.ENGS}
        self.waited = {e: {} for e in self.ENGS}
        self.pending = {e: [] for e in self.ENGS}
        self.n_dma = n_dma_sems
        self.dma_uses = [0] * n_dma_sems
        self.dma_next = 0
        self.same_engine_sync = same_engine_sync
        self.final_tokens = []

    def _deps(self, reads, writes, eng=None):
        deps = []
        for b in reads:
            if b.w is not None:
                deps.append(b.w)
            if b.excl:
                deps.extend(t for t in b.r if t[0] != eng)
        for b in writes:
            if b.w is not None:
                deps.append(b.w)
            deps.extend(b.r)
        return deps

    def _commit(self, tok, reads, writes):
        for b in reads:
            b.r.append(tok)
            if len(b.r) > 48:
                best = {}
                for k, v in b.r:
                    if best.get(k, 0) < v:
                        best[k] = v
                b.r = list(best.items())
        for b in writes:
            b.w = tok
            b.r = []

    def _waits(self, eng, deps):
        out = []
        wd = self.waited[eng]
        best = {}
        for k, v in deps:
            if best.get(k, 0) < v:
                best[k] = v
        for k, v in best.items():
            if k == eng and (eng == "pe" or not self.same_engine_sync):
                continue
            if wd.get(k, 0) >= v:
                continue
            wd[k] = v
            out.append((k, v))
        return out

    def op(self, eng, fn, reads=(), writes=()):
        deps = self._deps(reads, writes, eng)
        if self.pending[eng]:
            deps = deps + self.pending[eng]
            self.pending[eng] = []
        waits = self._waits(eng, deps)
        self.count[eng] += 1
        tok = (eng, self.count[eng])
        self.ops[eng].append((waits, fn, eng, 1))
        self._commit(tok, reads, writes)
        return tok

    def dma(self, fn, reads=(), writes=(), queue="sp", final=False):
        deps = self._deps(reads, writes)
        if self.pending[queue]:
            deps = deps + self.pending[queue]
            self.pending[queue] = []
        slot = self.dma_next
        self.dma_next = (self.dma_next + 1) % self.n_dma
        key = ("dma", slot)
        if self.dma_uses[slot] > 0:
            deps.append((key, 16 * self.dma_uses[slot]))
        waits = self._waits(queue, deps)
        self.dma_uses[slot] += 1
        tok = (key, 16 * self.dma_uses[slot])
        self.ops[queue].append((waits, fn, key, 16))
        self._commit(tok, reads, writes)
        if final:
            self.final_tokens.append(tok)
        return tok

    def barrier(self):
        toks = [(e, self.count[e]) for e in self.ENGS if self.count[e] > 0 and e != "sp"]
        for i in range(self.n_dma):
            if self.dma_uses[i] > 0:
                toks.append((("dma", i), 16 * self.dma_uses[i]))
        for e in self.ENGS:
            self.pending[e] = self.pending[e] + toks

    def emit(self):
        nc = self.nc
        with contextlib.ExitStack() as st:
            sems = {}
            for e in self.ENGS:
                sems[e] = st.enter_context(nc.semaphore("s_" + e))
            for i in range(self.n_dma):
                sems[("dma", i)] = st.enter_context(nc.semaphore("s_dma%d" % i))
            fw = self._waits("sp", list(self.final_tokens))
            block = st.enter_context(nc.Block())

            def run(engname, engobj, extra_waits=()):
                for waits, fn, inc_key, inc_val in self.ops[engname]:
                    for k, v in waits:
                        engobj.wait_ge(sems[k], v)
                    ins = fn(engobj)
                    ins.then_inc(sems[inc_key], inc_val)
                for k, v in extra_waits:
                    engobj.wait_ge(sems[k], v)

            @block.tensor
            def _(e):
                run("pe", e)

            @block.scalar
            def _(e):
                run("act", e)

            @block.vector
            def _(e):
                run("dve", e)

            @block.gpsimd
            def _(e):
                run("pool", e)

            @block.sync
            def _(e):
                run("sp", e, fw)


class Arena:
    def __init__(self, ap, ncols):
        self.ap = ap
        self.n = ncols
        self.pos = 0

    def reset(self):
        self.pos = 0

    def alloc(self, cols):
        cols_al = (cols + 7) // 8 * 8
        assert self.pos + cols_al <= self.n, ("arena overflow", self.pos, cols, self.n)
        a = self.ap[:, self.pos:self.pos + cols]
        self.pos += cols_al
        return a


def bc3(ap2, n):
    return ap2.rearrange("p (h o) -> p h o", o=1).to_broadcast([ap2.shape[0], ap2.shape[1], n])


def v3(ap, inner):
    return ap.rearrange("p (a b) -> p a b", b=inner)


def build_program(stop_after=None, n_tiles=NT, mode="fused"):
    nc = bass.Bass("TRN2", target_bir_lowering=False)
    if mode == "tile":
        n_tiles = 1
    T_LOC = T_SEQ if mode == "fused" else TT

    def din(name, shape, dt=F32):
        return nc.dram_tensor(name, shape, dt, kind="ExternalInput").ap()

    def dout(name, shape, dt=F32):
        return nc.dram_tensor(name, shape, dt, kind="ExternalOutput").ap()

    if mode != "prep":
        x_d = din("x", [T_LOC, D])
        p_d = din("p", [2, T_LOC, 256])
        norm_final = din("norm_final", [D])
        a_conv = din("a_conv", [4, 3072])
        a_log = din("a_log", [8])
        a_dt_bias = din("a_dt_bias", [8])
        b_sinks = din("b_sinks", [16])
        f_conv = din("f_conv", [2, 3, 2 * DFF])
        out_d = dout("out", [T_LOC, D])
    if mode != "tile":
        norm_mix = din("norm_mix", [2, D])
        norm_ffn = din("norm_ffn", [2, D])
        norm_ple = din("norm_ple", [2, D])
        a_w_in = din("a_w_in", [D, 4112])
        a_norm = din("a_norm", [128])
        a_w_out = din("a_w_out", [D, D])
        b_w_in = din("b_w_in", [D, 1536])
        b_w_out = din("b_w_out", [D, D])
        f_w_up = din("f_w_up", [2, D, 2 * DFF])
        f_w_down = din("f_w_down", [2, DFF, D])
        ple_w_proj = din("ple_w_proj", [2, 256, D])
        ple_w_gate = din("ple_w_gate", [2, D, D])

    def dscr(name, shape):
        if mode == "prep":
            return dout(name, shape, BF16)
        if mode == "tile":
            return din(name, shape, BF16)
        return nc.dram_tensor(name, shape, BF16).ap()

    Wa_in = dscr("Wa_in", [D, 4096])
    Wa_ba = dscr("Wa_ba", [D, 16])
    Wa_out = dscr("Wa_out", [D, D])
    Wb_in = dscr("Wb_in", [D, 1024 + 512 + 256])
    Wb_out = dscr("Wb_out", [D, D])
    Wf_up = [dscr("Wf_up%d" % l, [D, 2 * DFF]) for l in range(2)]
    Wf_dn = [dscr("Wf_dn%d" % l, [DFF, D]) for l in range(2)]
    Wp_pr = [dscr("Wp_pr%d" % l, [256, D]) for l in range(2)]
    Wp_gt = [dscr("Wp_gt%d" % l, [D, D]) for l in range(2)]
    if mode == "tile":
        S_in = din("S_in", [128, 8, 128]); S_out = dout("S_out", [128, 8, 128])
        ha_in = din("ha_in", [128, 24, 3]); ha_out = dout("ha_out", [128, 24, 3])
        hf_in = din("hf_in", [128, 2, 44, 2]); hf_out = dout("hf_out", [128, 2, 44, 2])
        kp_in = din("kp_in", [128, 2, 4, 128], BF16); kp_out = dout("kp_out", [128, 2, 4, 128], BF16)
        vp_in = din("vp_in", [128, 4, 72], BF16); vp_out = dout("vp_out", [128, 4, 72], BF16)
        flag_in = din("flag_in", [128, 1])

    S = Sched(nc)
    st = contextlib.ExitStack()
    with st:
        def sb(name, shape, dt):
            return st.enter_context(nc.sbuf_tensor(name, shape, dt))

        h_t = sb("h", [128, NB, D], F32)
        hnT = sb("hnT", [128, 8, TT], BF16)
        identb = sb("identb", [128, 128], BF16)
        identf = sb("identf", [128, 128], F32)
        onesf = sb("onesf", [128, 128], F32)
        triU = sb("triU", [128, 128], F32)
        mposS = sb("mposS", [128, 128], F32)
        mnegI = sb("mnegI", [128, 128], F32)
        nrm = sb("nrm", [128, 7, 8], F32)
        anorm_c = sb("anorm_c", [128, 1], F32)
        nfin_bc = sb("nfin_bc", [128, D], F32)
        aconv_c = sb("aconv_c", [128, 24, 4], F32)
        fconv_c = sb("fconv_c", [128, 2, 44, 3], F32)
        negA_bc = sb("negA_bc", [128, 8], F32)
        dtb_bc = sb("dtb_bc", [128, 8], F32)
        esink_bc = sb("esink_bc", [128, 16], F32)
        halo_a = sb("halo_a", [128, 24, 3], F32)
        halo_f = sb("halo_f", [128, 2, 44, 2], F32)
        S_st = sb("S_st", [128, 8, 128], F32)
        Sb_st = sb("Sb_st", [128, 8, 128], BF16)
        kT_prev = sb("kT_prev", [128, 2, 4, 128], BF16)
        vx_prev = sb("vx_prev", [128, 4, 72], BF16)
        swab = sb("swab", [128, 4, 2, 512], F32)
        wst = [sb("wst%d" % i, [128, 8, 256], BF16) for i in range(3)]
        wout_t = sb("wout_t", [128, 8, D], BF16)
        smalls = sb("smalls", [128, 512], F32)
        arena_f_t = sb("arena_f", [128, 8960], F32)
        neu_t = sb("neu", [128, 4, 6, 128], F32R)
        arena_b_t = sb("arena_b", [128, 31488], BF16)
        AFa = Arena(arena_f_t[:], 8960)
        ABa = Arena(arena_b_t[:], 31488)

        pbank = [st.enter_context(nc.psum_tensor("pb%d" % i, [128, 512], F32)) for i in range(6)]
        pbankh = [st.enter_context(nc.psum_tensor("pbh%d" % i, [128, 1024], BF16)) for i in range(2)]
        PB = [Buf("pb%d" % i, excl=True) for i in range(6)]
        PBH = [Buf("pbh%d" % i, excl=True) for i in range(2)]

        B_h = [Buf("h%d" % b) for b in range(NB)]
        B_hnT = Buf("hnT")
        B_const = Buf("const")
        B_wst = [Buf("wst%d" % i) for i in range(3)]
        B_wout = Buf("wout")
        B_S = [Buf("S%d" % i) for i in range(8)]
        B_Sb = [Buf("Sb%d" % i) for i in range(8)]
        B_halo_a = [Buf("haloa%d" % i) for i in range(24)]
        B_halo_f = [[Buf("halof%d_%d" % (l, i)) for i in range(44)] for l in range(2)]
        B_kprev = Buf("kprev")
        B_vprev = Buf("vprev")
        wst_rr = [0]

        flag_t = sb("flag_t", [128, 1], F32)

        def const_setup():
            rd, wr = [B_const], [B_const]

            def ld(out_ap, in_ap):
                S.dma(lambda e: e.dma_start(out=out_ap, in_=in_ap, allow_slow_non_contiguous=True), writes=wr)
            if mode != "tile":
                for i, (src, l) in enumerate([(norm_mix, 0), (norm_mix, 1), (norm_ffn, 0), (norm_ffn, 1),
                                              (norm_ple, 0), (norm_ple, 1)]):
                    ld(nrm[:, i, :], src[l].rearrange("(k p) -> p k", p=128))
                ld(anorm_c[:], a_norm.rearrange("(p o) -> p o", o=1))
            if mode == "prep":
                return
            S.op("pool", lambda e: e.memset(identf[:], 0.0), writes=wr)
            S.op("pool", lambda e: e.affine_select(out=identf[:], in_=identf[:], pattern=[[-1, 128]],
                                                   compare_op=ALU.not_equal, fill=1.0, base=0, channel_multiplier=1),
                 reads=rd, writes=wr)
            S.op("pool", lambda e: e.tensor_copy(out=identb[:], in_=identf[:]), reads=rd, writes=wr)
            S.op("pool", lambda e: e.memset(onesf[:], 1.0), writes=wr)
            S.op("pool", lambda e: e.affine_select(out=triU[:], in_=onesf[:], pattern=[[1, 128]],
                                                   compare_op=ALU.is_ge, fill=0.0, base=0, channel_multiplier=-1),
                 reads=rd, writes=wr)
            S.op("pool", lambda e: e.memset(mposS[:], 0.0), writes=wr)
            S.op("pool", lambda e: e.affine_select(out=mposS[:], in_=mposS[:], pattern=[[-1, 128]],
                                                   compare_op=ALU.is_gt, fill=BIG, base=0, channel_multiplier=1),
                 reads=rd, writes=wr)
            S.op("pool", lambda e: e.memset(mnegI[:], 0.0), writes=wr)
            S.op("pool", lambda e: e.affine_select(out=mnegI[:], in_=mnegI[:], pattern=[[1, 128]],
                                                   compare_op=ALU.is_ge, fill=-BIG, base=0, channel_multiplier=-1),
                 reads=rd, writes=wr)
            if mode == "tile":
                for (t_, src_) in ((halo_a, ha_in), (halo_f, hf_in), (S_st, S_in), (kT_prev, kp_in), (vx_prev, vp_in),
                                   (flag_t, flag_in)):
                    S.dma(lambda e, t_=t_, src_=src_: e.dma_start(out=t_[:], in_=src_), writes=wr)
                S.op("pool", lambda e: e.tensor_copy(out=Sb_st[:], in_=S_st[:]), reads=rd, writes=wr)
            else:
                for t_ in (halo_a, halo_f, S_st):
                    S.op("pool", lambda e, t_=t_: e.memset(t_[:], 0.0), writes=wr)
                S.op("pool", lambda e: e.memset(Sb_st[:], 0.0), writes=wr)
                S.op("pool", lambda e: e.memset(kT_prev[:], 0.0), writes=wr)
                S.op("pool", lambda e: e.memset(vx_prev[:], 0.0), writes=wr)
                S.op("pool", lambda e: e.memset(flag_t[:], 0.0), writes=wr)
            for g in range(4):
                for hh in range(4):
                    head = 4 * g + hh
                    slope = 2.0 ** (-8.0 * (head + 1) / 16.0)
                    for pc in range(2):
                        dst = swab[:, g, pc, hh * 128:(hh + 1) * 128]
                        S.op("pool", lambda e, dst=dst, pc=pc: e.iota(dst, [[1, 128]], base=(128 if pc == 0 else 0),
                                                                      channel_multiplier=-1,
                                                                      allow_small_or_imprecise_dtypes=True),
                             reads=rd, writes=wr)
                        S.op("pool", lambda e, dst=dst, slope=slope: e.tensor_scalar(out=dst, in0=dst, scalar1=-slope,
                                                                                    scalar2=None, op0=ALU.mult),
                             reads=rd, writes=wr)
                        if pc == 0:
                            S.op("pool", lambda e, dst=dst: e.affine_select(out=dst, in_=dst, pattern=[[-1, 128]],
                                                                            compare_op=ALU.is_gt, fill=-BIG, base=0,
                                                                            channel_multiplier=1),
                                 reads=rd, writes=wr)
                        else:
                            S.op("pool", lambda e, dst=dst: e.affine_select(out=dst, in_=dst, pattern=[[1, 128]],
                                                                            compare_op=ALU.is_ge, fill=-BIG, base=0,
                                                                            channel_multiplier=-1),
                                 reads=rd, writes=wr)
            ld(nfin_bc[:], bass.AP(norm_final.tensor, 0, [[0, 128], [1, D]]))
            for i in range(4):
                ld(aconv_c[:, :, i], a_conv[i].rearrange("(c p) -> p c", p=128))
            for l in range(2):
                for i in range(3):
                    ld(fconv_c[:, l, :, i], f_conv[l, i].rearrange("(c p) -> p c", p=128))
            ld(negA_bc[:], bass.AP(a_log.tensor, 0, [[0, 128], [1, 8]]))
            ld(dtb_bc[:], bass.AP(a_dt_bias.tensor, 0, [[0, 128], [1, 8]]))
            ld(esink_bc[:], bass.AP(b_sinks.tensor, 0, [[0, 128], [1, 16]]))
            S.op("act", lambda e: e.activation(out=negA_bc[:], in_=negA_bc[:], func=AF.Exp), reads=rd, writes=wr)
            S.op("dve", lambda e: e.tensor_scalar(out=negA_bc[:], in0=negA_bc[:], scalar1=-1.0, scalar2=None,
                                                  op0=ALU.mult), reads=rd, writes=wr)
            S.op("act", lambda e: e.activation(out=esink_bc[:], in_=esink_bc[:], func=AF.Exp), reads=rd, writes=wr)

        def weight_prep():
            AFa.reset(); ABa.reset()
            nst = 4
            PW = 2048
            stg_f = [AFa.alloc(PW) for _ in range(nst)]
            stg_b = [ABa.alloc(PW) for _ in range(nst)]
            Bf = [Buf("stgf%d" % i) for i in range(nst)]
            Bb = [Buf("stgb%d" % i) for i in range(nst)]
            engs = ["dve", "act"]
            pieces = []

            def conv_mat(src, dst, nrows, c0, ncols, d0, scale_fn):
                for k in range(nrows // 128):
                    for cc in range(0, ncols, PW):
                        n = min(PW, ncols - cc)
                        pieces.append((src[k * 128:(k + 1) * 128, c0 + cc:c0 + cc + n],
                                       dst[k * 128:(k + 1) * 128, d0 + cc:d0 + cc + n], n,
                                       scale_fn(k) if scale_fn else None))

            def emit_pieces():
                LA = nst - 1
                npc = len(pieces)
                for t in range(npc + LA):
                    if t < npc:
                        src_ap, _, ncols, _ = pieces[t]
                        i = t % nst
                        S.dma(lambda e, i=i, ncols=ncols, src_ap=src_ap: e.dma_start(out=stg_f[i][:, 0:ncols], in_=src_ap),
                              writes=[Bf[i]])
                    u = t - LA
                    if u >= 0:
                        _, dst_ap, ncols, scale_ap = pieces[u]
                        i = u % nst
                        sf, sbb = stg_f[i][:, 0:ncols], stg_b[i][:, 0:ncols]
                        eng = engs[u % 2]
                        rds = [Bf[i]] + ([B_const] if scale_ap is not None else [])
                        if eng == "act":
                            if scale_ap is None:
                                S.op("act", lambda e, sf=sf, sbb=sbb: e.activation(out=sbb, in_=sf, func=AF.Copy), reads=rds, writes=[Bb[i]])
                            else:
                                S.op("act", lambda e, sf=sf, sbb=sbb, scale_ap=scale_ap: e.activation(out=sbb, in_=sf, func=AF.Copy, scale=scale_ap),
                                     reads=rds, writes=[Bb[i]])
                        else:
                            if scale_ap is None:
                                S.op("dve", lambda e, sf=sf, sbb=sbb: e.tensor_copy(out=sbb, in_=sf), reads=rds, writes=[Bb[i]])
                            else:
                                S.op("dve", lambda e, sf=sf, sbb=sbb, scale_ap=scale_ap: e.tensor_scalar(out=sbb, in0=sf, scalar1=scale_ap,
                                                                                                      scalar2=None, op0=ALU.mult),
                                     reads=rds, writes=[Bb[i]])
                        S.dma(lambda e, dst_ap=dst_ap, sbb=sbb: e.dma_start(out=dst_ap, in_=sbb), reads=[Bb[i]], final=(mode == "prep"))

            conv_mat(a_w_in, Wa_in, D, 0, 4096, 0, lambda k: nrm[:, 0, k:k + 1])
            conv_mat(a_w_in, Wa_ba, D, 4096, 16, 0, lambda k: nrm[:, 0, k:k + 1])
            conv_mat(a_w_out, Wa_out, D, 0, D, 0, lambda k: anorm_c[:, 0:1])
            conv_mat(b_w_in, Wb_in, D, 0, 1024, 0, lambda k: nrm[:, 1, k:k + 1])
            for g in range(4):
                for dup in range(2):
                    conv_mat(b_w_in, Wb_in, D, 1024 + g * 64, 64, 1024 + g * 128 + dup * 64, lambda k: nrm[:, 1, k:k + 1])
            conv_mat(b_w_in, Wb_in, D, 1280, 256, 1536, lambda k: nrm[:, 1, k:k + 1])
            conv_mat(b_w_out, Wb_out, D, 0, D, 0, None)
            for l in range(2):
                conv_mat(f_w_up[l], Wf_up[l], D, 0, 2 * DFF, 0, lambda k, l=l: nrm[:, 2 + l, k:k + 1])
                conv_mat(f_w_down[l], Wf_dn[l], DFF, 0, D, 0, None)
                conv_mat(ple_w_proj[l], Wp_pr[l], 256, 0, D, 0, None)
                conv_mat(ple_w_gate[l], Wp_gt[l], D, 0, D, 0, lambda k, l=l: nrm[:, 4 + l, k:k + 1])
            emit_pieces()
            S.barrier()

        def load_wst(Wd, c0, ncols):
            i = wst_rr[0] % 3
            wst_rr[0] += 1
            dst = wst[i][:, :, 0:ncols]
            src = Wd[:, c0:c0 + ncols].rearrange("(k p) n -> p k n", p=128)
            S.dma(lambda e: e.dma_start(out=dst, in_=src), writes=[B_wst[i]])
            return i

        def norm_transpose():
            ss = smalls[:, 0:NB]
            rstd = smalls[:, 8:8 + NB]
            Bss = Buf("ss")
            junk = ABa.alloc(D)
            Bj = Buf("junk")
            xs = [ABa.alloc(D) for _ in range(2)]
            Bxs = [Buf("xs0"), Buf("xs1")]
            for b in range(NB):
                S.op("act", lambda e, b=b: e.activation(out=junk, in_=h_t[:, b, :], func=AF.Square,
                                                        accum_out=ss[:, b:b + 1]),
                     reads=[B_h[b]], writes=[Bj, Bss])
            S.op("act", lambda e: e.activation(out=rstd, in_=ss, func=AF.Ln, scale=1.0 / D, bias=EPS),
                 reads=[Bss], writes=[Bss])
            S.op("act", lambda e: e.activation(out=rstd, in_=rstd, func=AF.Exp, scale=-0.5), reads=[Bss], writes=[Bss])
            for b in range(NB):
                x_ = xs[b % 2]
                S.op("dve", lambda e, b=b, x_=x_: e.tensor_scalar(out=x_, in0=h_t[:, b, :], scalar1=rstd[:, b:b + 1],
                                                                  scalar2=None, op0=ALU.mult),
                     reads=[B_h[b], Bss], writes=[Bxs[b % 2]])
                pt = pbankh[b % 2]
                for k in range(8):
                    S.op("pe", lambda e, k=k, x_=x_, pt=pt: e.transpose(out=pt[:, k * 128:(k + 1) * 128],
                                                                        in_=x_[:, k * 128:(k + 1) * 128],
                                                                        identity=identb[:]),
                         reads=[Bxs[b % 2], B_const], writes=[PBH[b % 2]])
                eng = "dve" if b % 2 == 0 else "act"
                dst = hnT[:, :, b * 128:(b + 1) * 128]
                src = v3(pt[:, :], 128)
                if eng == "dve":
                    S.op("dve", lambda e, dst=dst, src=src: e.tensor_copy(out=dst, in_=src),
                         reads=[PBH[b % 2]], writes=[B_hnT])
                else:
                    S.op("act", lambda e, dst=dst, src=src: e.activation(out=dst, in_=src, func=AF.Copy),
                         reads=[PBH[b % 2]], writes=[B_hnT])

        def proj_fm(Wd, c0, nchunks, consume, banks=(0, 1, 2, 3)):
            per = 2
            bi = 0
            for cb in range(0, nchunks, per):
                n = min(per, nchunks - cb)
                wi = load_wst(Wd, c0 + cb * 128, n * 128)
                for j in range(n):
                    bk = banks[bi % len(banks)]
                    bi += 1
                    for k in range(8):
                        S.op("pe", lambda e, k=k, j=j, wi=wi, bk=bk: e.matmul(
                            pbank[bk][:, :], lhsT=wst[wi][:, k, j * 128:(j + 1) * 128], rhs=hnT[:, k, :],
                            start=(k == 0), stop=(k == 7)),
                            reads=[B_wst[wi], B_hnT], writes=[PB[bk]])
                    consume(cb + j, pbank[bk], PB[bk])

        def resid_proj(lhs_fn, nk, w_fn, w_bufs, lhs_bufs, banks=(4, 5)):
            for b in range(NB):
                for half in range(2):
                    bk = banks[half]
                    for k in range(nk):
                        S.op("pe", lambda e, b=b, k=k, half=half, bk=bk: e.matmul(
                            pbank[bk][:, :], lhsT=lhs_fn(k, b), rhs=w_fn(k, half), start=(k == 0), stop=(k == nk - 1)),
                            reads=list(w_bufs) + list(lhs_bufs), writes=[PB[bk]])
                    hs = h_t[:, b, half * 512:(half + 1) * 512]
                    S.op("dve", lambda e, hs=hs, bk=bk: e.tensor_tensor(out=hs, in0=pbank[bk][:, :], in1=hs, op=ALU.add),
                         reads=[PB[bk], B_h[b]], writes=[B_h[b]])

        def conv_from_psum(items, KW, act_fix=False):
            for it in items:
                S.op("act", lambda e, it=it: e.activation(out=it["ac"], in_=it["ps"][:, 0:TT], func=AF.Copy, scale=it["w"](KW - 1)),
                     reads=[it["Bps"], B_const], writes=[it["Bac"]])
            if act_fix:
                for j in range(KW - 1):
                    s_ = KW - 1 - j
                    for col in range(s_):
                        for it in items:
                            S.op("act", lambda e, it=it, j=j, col=col: e.activation(
                                out=it["ac"][:, col:col + 1], in_=it["halo"][:, j + col:j + col + 1], func=AF.Identity,
                                scale=it["w"](j), bias=it["ac"][:, col:col + 1]),
                                reads=[it["Bhalo"], it["Bac"], B_const], writes=[it["Bac"]])
                for it in items:
                    S.op("act", lambda e, it=it: e.activation(out=it["halo"], in_=it["ps"][:, TT - (KW - 1):TT], func=AF.Copy),
                         reads=[it["Bps"]], writes=[it["Bhalo"]])
            for j in range(KW - 1):
                s_ = KW - 1 - j
                for it in items:
                    S.op("dve", lambda e, it=it, j=j, s_=s_: e.scalar_tensor_tensor(
                        out=it["ac"][:, s_:TT], in0=it["ps"][:, 0:TT - s_], scalar=it["w"](j), in1=it["ac"][:, s_:TT],
                        op0=ALU.mult, op1=ALU.add), reads=[it["Bps"], it["Bac"], B_const], writes=[it["Bac"]])
            if act_fix:
                return
            for j in range(KW - 1):
                s_ = KW - 1 - j
                for it in items:
                    S.op("dve", lambda e, it=it, j=j, s_=s_: e.scalar_tensor_tensor(
                        out=it["ac"][:, 0:s_], in0=it["halo"][:, j:KW - 1], scalar=it["w"](j), in1=it["ac"][:, 0:s_],
                        op0=ALU.mult, op1=ALU.add), reads=[it["Bhalo"], it["Bac"], B_const], writes=[it["Bac"]])
            for it in items:
                S.op("dve", lambda e, it=it: e.tensor_copy(out=it["halo"], in_=it["ps"][:, TT - (KW - 1):TT]),
                     reads=[it["Bps"]], writes=[it["Bhalo"]])

        def deltanet(ti):
            AFa.reset(); ABa.reset()
            norm_transpose()
            qkvT = v3(ABa.alloc(24 * TT), TT)
            zsT = v3(ABa.alloc(8 * TT), TT)
            ogT = v3(ABa.alloc(8 * TT), TT)
            B_qkv = [Buf("qkv%d" % c) for c in range(24)]
            B_zs = Buf("zs")
            B_og = [Buf("og%d" % b) for b in range(NB)]
            ba = v3(AFa.alloc(NB * 16), 16)
            B_ba = Buf("ba")
            pre = [AFa.alloc(TT + 3) for _ in range(3)]
            acc = [AFa.alloc(TT) for _ in range(3)]
            Bpre = [Buf("pre%d" % i) for i in range(3)]
            Bacc = [Buf("acc%d" % i) for i in range(3)]

            S.dma(lambda e: e.dma_start(out=wout_t[:], in_=Wa_out.rearrange("(k p) n -> p k n", p=128)), writes=[B_wout])

            def consume(c, ps, Bps):
                if c < 24:
                    i = c % 3
                    ac = acc[i]
                    conv_from_psum([dict(ps=ps, Bps=Bps, w=(lambda j, c=c: aconv_c[:, c, j:j + 1]), halo=halo_a[:, c, :],
                                         Bhalo=B_halo_a[c], ac=ac, Bac=Bacc[i])], 4)
                    S.op("act", lambda e: e.activation(out=qkvT[:, c, :], in_=ac, func=AF.Silu),
                         reads=[Bacc[i]], writes=[B_qkv[c]])
                else:
                    S.op("act", lambda e: e.activation(out=zsT[:, c - 24, :], in_=ps[:, :], func=AF.Silu),
                         reads=[Bps], writes=[B_zs])

            proj_fm(Wa_in, 0, 32, consume)
            if DBG <= 1:
                S.barrier(); return

            wba = ABa.alloc(8 * 16)
            wba3 = v3(wba, 16)
            B_wba = Buf("wba")
            S.dma(lambda e: e.dma_start(out=wba3, in_=Wa_ba.rearrange("(k p) n -> p k n", p=128)), writes=[B_wba])
            for b in range(NB):
                for k in range(8):
                    S.op("pe", lambda e, b=b, k=k: e.matmul(pbank[4][:, b * 16:(b + 1) * 16],
                                                            lhsT=hnT[:, k, b * 128:(b + 1) * 128], rhs=wba3[:, k, :],
                                                            start=(k == 0), stop=(k == 7)),
                         reads=[B_hnT, B_wba], writes=[PB[4]])
            S.op("dve", lambda e: e.tensor_copy(out=ba, in_=v3(pbank[4][:, 0:NB * 16], 16)), reads=[PB[4]], writes=[B_ba])

            if DBG <= 2:
                S.barrier(); return

            def hb(n, dt):
                return [(AFa if dt == F32 else ABa).alloc(n) for _ in range(4)]
            Kn = hb(128, BF16); kb = hb(128, BF16); kd = hb(128, BF16)
            Qn = hb(128, BF16); qd = hb(128, BF16); bv = hb(128, BF16)
            KQT = hb(384, BF16)
            qkTm = hb(128, BF16); TTb = hb(128, BF16); nwkT = hb(128, BF16); usb = hb(128, BF16)
            dg = hb(128, F32); tL = hb(128, F32); tU = hb(128, F32); EL = hb(128, F32); EU = hb(128, F32)
            PTa_r, Pa_r, PTb_r, Pb_r, Xa_r, Xb_r = [[neu_t[:, i_, kk_, :] for i_ in range(4)] for kk_ in range(6)]
            PPa_r = [neu_t[:, i_, 0:2, :] for i_ in range(4)]
            PPb_r = [neu_t[:, i_, 2:4, :] for i_ in range(4)]
            PTa = [t_.bitcast(F32) for t_ in PTa_r]
            names = "Kn kb kd Qn qd bv KQT qkTm TTb nwkT usb dg tL tU EL EU PTa PTb Pa Pb Xa Xb".split()
            BH = [{n: Buf(n + str(i)) for n in names} for i in range(4)]
            sq = v3(AFa.alloc(16 * 128), 128)
            B_sq = Buf("sq")
            on_all = v3(ABa.alloc(8 * 128), 128)
            B_on = Buf("on")
            junk2 = AFa.alloc(128)
            B_j2 = Buf("junk2")
            B_g = Buf("gate")
            B_sso = Buf("sso")
            NG = NB * 8
            Gall = AFa.alloc(14 * NG + NB * 16)

            def gq(i):
                return Gall[:, i * NG:(i + 1) * NG]
            (beta_a, nbeta_a, g_a, gcs_a, ngcs_a, Eg_a, Ekd_a, Egl_a, c_kb_a, c_kd_a, c_qn_a, c_qd_a, tmp1, tmp2) = [
                gq(i) for i in range(14)]
            rn_a = Gall[:, 14 * NG:14 * NG + NB * 16]
            rn3 = v3(rn_a, 16)
            rnq3, rnk3 = rn3[:, :, 0:8], rn3[:, :, 8:16]

            def q3(x):
                return v3(x, 8)
            sso = smalls[:, 64:72]
            rso = smalls[:, 72:80]
            scale_q = 128.0 ** -0.5
            gr, gw = [B_g], [B_g]
            dtb3 = dtb_bc[:].rearrange("p (o h) -> p o h", o=1).to_broadcast([128, NB, 8])
            negA3 = negA_bc[:].rearrange("p (o h) -> p o h", o=1).to_broadcast([128, NB, 8])
            S.op("act", lambda e: e.activation(out=q3(tmp1), in_=ba[:, :, 0:8], func=AF.Exp, scale=-1.0), reads=[B_ba], writes=gw)
            S.op("dve", lambda e: e.tensor_scalar(out=tmp1, in0=tmp1, scalar1=1.0, scalar2=None, op0=ALU.add), reads=gr, writes=gw)
            S.op("dve", lambda e: e.reciprocal(out=beta_a, in_=tmp1), reads=gr, writes=gw)
            S.op("dve", lambda e: e.tensor_scalar(out=nbeta_a, in0=beta_a, scalar1=-1.0, scalar2=None, op0=ALU.mult), reads=gr, writes=gw)
            S.op("dve", lambda e: e.tensor_tensor(out=q3(tmp2), in0=ba[:, :, 8:16], in1=dtb3, op=ALU.add),
                 reads=[B_ba, B_const] + gr, writes=gw)
            S.op("act", lambda e: e.activation(out=tmp2, in_=tmp2, func=AF.Exp), reads=gr, writes=gw)
            S.op("act", lambda e: e.activation(out=tmp2, in_=tmp2, func=AF.Ln, bias=1.0), reads=gr, writes=gw)
            S.op("dve", lambda e: e.tensor_tensor(out=q3(g_a), in0=q3(tmp2), in1=negA3, op=ALU.mult), reads=[B_const] + gr, writes=gw)
            for b in range(NB):
                S.op("pe", lambda e, b=b: e.matmul(pbank[5][:, b * 8:(b + 1) * 8], lhsT=triU[:], rhs=g_a[:, b * 8:(b + 1) * 8],
                                                   start=True, stop=True), reads=[B_const] + gr, writes=[PB[5]])
                S.op("pe", lambda e, b=b: e.matmul(pbank[5][:, NG + b * 8:NG + (b + 1) * 8], lhsT=onesf[:],
                                                   rhs=g_a[:, b * 8:(b + 1) * 8], start=True, stop=True),
                     reads=[B_const] + gr, writes=[PB[5]])
            for b in range(NB):
                blk = slice(b * 128, (b + 1) * 128)
                S.op("act", lambda e, blk=blk: e.activation(out=sq, in_=qkvT[:, 0:16, blk], func=AF.Square),
                     reads=B_qkv[0:16], writes=[B_sq])
                for c in range(16):
                    S.op("pe", lambda e, c=c, b=b: e.matmul(pbank[5][:, 2 * NG + b * 16 + c:2 * NG + b * 16 + c + 1], lhsT=sq[:, c, :],
                                                            rhs=onesf[:, 0:1], start=True, stop=True),
                         reads=[B_sq, B_const], writes=[PB[5]])
            psg, psl, pss = pbank[5][:, 0:NG], pbank[5][:, NG:2 * NG], pbank[5][:, 2 * NG:2 * NG + NB * 16]
            S.op("act", lambda e: e.activation(out=gcs_a, in_=psg, func=AF.Copy), reads=[PB[5]] + gr, writes=gw)
            S.op("act", lambda e: e.activation(out=ngcs_a, in_=psg, func=AF.Copy, scale=-1.0), reads=[PB[5]] + gr, writes=gw)
            S.op("act", lambda e: e.activation(out=Eg_a, in_=psg, func=AF.Exp), reads=[PB[5]] + gr, writes=gw)
            S.op("act", lambda e: e.activation(out=Egl_a, in_=psl, func=AF.Exp), reads=[PB[5]] + gr, writes=gw)
            S.op("act", lambda e: e.activation(out=rn_a, in_=pss, func=AF.Ln, bias=EPS), reads=[PB[5]] + gr, writes=gw)
            S.op("act", lambda e: e.activation(out=rn_a, in_=rn_a, func=AF.Exp, scale=-0.5), reads=gr, writes=gw)
            S.op("dve", lambda e: e.tensor_tensor(out=tmp1, in0=psl, in1=gcs_a, op=ALU.subtract), reads=[PB[5]] + gr, writes=gw)
            S.op("act", lambda e: e.activation(out=Ekd_a, in_=tmp1, func=AF.Exp), reads=gr, writes=gw)
            S.op("dve", lambda e: e.tensor_tensor(out=q3(tmp2), in0=rnk3, in1=q3(beta_a), op=ALU.mult), reads=gr, writes=gw)
            S.op("dve", lambda e: e.tensor_tensor(out=c_kb_a, in0=tmp2, in1=Eg_a, op=ALU.mult), reads=gr, writes=gw)
            S.op("dve", lambda e: e.tensor_tensor(out=q3(c_kd_a), in0=rnk3, in1=q3(Ekd_a), op=ALU.mult), reads=gr, writes=gw)
            S.op("dve", lambda e: e.tensor_scalar(out=q3(c_qn_a), in0=rnq3, scalar1=scale_q, scalar2=None, op0=ALU.mult),
                 reads=gr, writes=gw)
            S.op("dve", lambda e: e.tensor_tensor(out=c_qd_a, in0=c_qn_a, in1=Eg_a, op=ALU.mult), reads=gr, writes=gw)

            def do_block(b):
                blk = slice(b * 128, (b + 1) * 128)
                beta_, nbeta, gcs, ngcs, Egl, c_kb, c_kd, c_qn, c_qd = [x[:, b * 8:(b + 1) * 8] for x in (
                    beta_a, nbeta_a, gcs_a, ngcs_a, Egl_a, c_kb_a, c_kd_a, c_qn_a, c_qd_a)]
                rnk = rn_a[:, b * 16 + 8:b * 16 + 16]
                if DBG <= 3:
                    S.barrier(); return
                for grp in range(2):
                    heads = [grp * 4 + i for i in range(4)]
                    ob = 4 + grp
                    for i, hh in enumerate(heads):
                        S.op("act", lambda e, i=i, hh=hh: e.activation(out=dg[i], in_=identf[:], func=AF.Copy, scale=gcs[:, hh:hh + 1]),
                             reads=[B_const] + gr, writes=[BH[i]["dg"]])
                    for i, hh in enumerate(heads):
                        pt = pbankh[i // 2]
                        for j, c in enumerate((8 + hh, hh, 16 + hh)):
                            o_ = ((i % 2) * 3 + j) * 128
                            S.op("pe", lambda e, pt=pt, o_=o_, c=c, blk=blk: e.transpose(
                                out=pt[:, o_:o_ + 128], in_=qkvT[:, c, blk], identity=identb[:]),
                                reads=[B_qkv[c], B_const], writes=[PBH[i // 2]])
                    if DBG <= 3.3:
                        S.barrier(); return
                    for i, hh in enumerate(heads):
                        pt = pbankh[i // 2]
                        o_ = (i % 2) * 384
                        Kt, Qt, Vt = pt[:, o_:o_ + 128], pt[:, o_ + 128:o_ + 256], pt[:, o_ + 256:o_ + 384]
                        B = BH[i]
                        beng = "act" if i // 2 == 0 else "dve"
                        for (dst, src, sc, nm, eng) in ((Kn[i], Kt, rnk, "Kn", beng), (kb[i], Kt, c_kb, "kb", beng),
                                                        (kd[i], Kt, c_kd, "kd", beng), (Qn[i], Qt, c_qn, "Qn", beng),
                                                        (qd[i], Qt, c_qd, "qd", beng), (bv[i], Vt, beta_, "bv", beng)):
                            sc1 = sc[:, hh:hh + 1]
                            if eng == "act":
                                S.op("act", lambda e, dst=dst, src=src, sc1=sc1: e.activation(out=dst, in_=src, func=AF.Copy,
                                                                                             scale=sc1),
                                     reads=[PBH[i // 2]] + gr, writes=[B[nm]])
                            else:
                                S.op("dve", lambda e, dst=dst, src=src, sc1=sc1: e.tensor_scalar(
                                    out=dst, in0=src, scalar1=sc1, scalar2=None, op0=ALU.mult),
                                    reads=[PBH[i // 2]] + gr, writes=[B[nm]])
                    if DBG <= 3.6:
                        S.barrier(); return
                    for i, hh in enumerate(heads):
                        pt = pbankh[i // 2]
                        B = BH[i]
                        for j, (src, nm) in enumerate(((Kn[i], "Kn"), (Qn[i], "Qn"), (qd[i], "qd"))):
                            o_ = ((i % 2) * 3 + j) * 128
                            S.op("pe", lambda e, pt=pt, o_=o_, src=src: e.transpose(out=pt[:, o_:o_ + 128], in_=src,
                                                                                    identity=identb[:]),
                                 reads=[B[nm], B_const], writes=[PBH[i // 2]])
                    for i, hh in enumerate(heads):
                        pt = pbankh[i // 2]
                        o_ = (i % 2) * 384
                        S.op("dve" if i // 2 == 1 else "act",
                             (lambda e, i=i, pt=pt, o_=o_: e.tensor_copy(out=KQT[i], in_=pt[:, o_:o_ + 384])) if i // 2 == 1 else
                             (lambda e, i=i, pt=pt, o_=o_: e.activation(out=KQT[i], in_=pt[:, o_:o_ + 384], func=AF.Copy)),
                             reads=[PBH[i // 2]], writes=[BH[i]["KQT"]])
                    if DBG <= 4:
                        S.barrier(); return
                    for i, hh in enumerate(heads):
                        B = BH[i]
                        KnT, QnT = KQT[i][:, 0:128], KQT[i][:, 128:256]
                        S.op("pe", lambda e, i=i, KnT=KnT: e.matmul(pbank[i][:, 0:128], lhsT=KnT, rhs=KnT, start=True, stop=True),
                             reads=[B["KQT"]], writes=[PB[i]])
                        S.op("pe", lambda e, i=i, KnT=KnT, QnT=QnT: e.matmul(pbank[i][:, 128:256], lhsT=KnT, rhs=QnT,
                                                                           start=True, stop=True),
                             reads=[B["KQT"]], writes=[PB[i]])
                        S.op("pe", lambda e, i=i: e.matmul(pbank[i][:, 256:384], lhsT=onesf[:], rhs=dg[i], start=True, stop=True),
                             reads=[B["dg"], B_const], writes=[PB[i]])
                    for i, hh in enumerate(heads):
                        B = BH[i]
                        S.op("dve", lambda e, i=i: e.tensor_tensor(out=tL[i], in0=pbank[i][:, 256:384], in1=mposS[:], op=ALU.add),
                             reads=[PB[i], B_const], writes=[B["tL"]])
                        S.op("dve", lambda e, i=i: e.tensor_tensor(out=tU[i], in0=pbank[i][:, 256:384], in1=mnegI[:], op=ALU.add),
                             reads=[PB[i], B_const], writes=[B["tU"]])
                        S.op("act", lambda e, i=i, hh=hh: e.activation(out=EL[i], in_=tL[i], func=AF.Exp, scale=-1.0,
                                                                       bias=gcs[:, hh:hh + 1]),
                             reads=[B["tL"]] + gr, writes=[B["EL"]])
                        S.op("act", lambda e, i=i, hh=hh: e.activation(out=EU[i], in_=tU[i], func=AF.Exp,
                                                                       bias=ngcs[:, hh:hh + 1]),
                             reads=[B["tU"]] + gr, writes=[B["EU"]])
                    for i, hh in enumerate(heads):
                        B = BH[i]
                        S.op("dve", lambda e, i=i, hh=hh: e.scalar_tensor_tensor(
                            out=PTa_r[i], in0=pbank[i][:, 0:128], scalar=nbeta[:, hh:hh + 1], in1=EL[i],
                            op0=ALU.mult, op1=ALU.mult), reads=[PB[i], B["EL"]] + gr, writes=[B["PTa"]])
                        S.op("dve", lambda e, i=i: e.tensor_tensor(out=qkTm[i], in0=pbank[i][:, 128:256], in1=EU[i], op=ALU.mult),
                             reads=[PB[i], B["EU"]], writes=[B["qkTm"]])
                    if DBG <= 5:
                        S.barrier(); return
                    for i, hh in enumerate(heads):
                        B = BH[i]
                        S.op("pe", lambda e, i=i: e.transpose(out=pbank[i][:, 0:128], in_=PTa[i], identity=identf[:]),
                             reads=[B["PTa"], B_const], writes=[PB[i]])
                        S.op("dve", lambda e, i=i: e.tensor_copy(out=Pa_r[i], in_=pbank[i][:, 0:128]),
                             reads=[PB[i]], writes=[B["Pa"]])
                        S.op("dve", lambda e, i=i: e.tensor_tensor(out=Xa_r[i], in0=pbank[i][:, 0:128], in1=identf[:], op=ALU.add),
                             reads=[PB[i], B_const], writes=[B["Xa"]])
                    cur = {"P": (Pa_r, "Pa"), "PT": (PTa_r, "PTa"), "X": (Xa_r, "Xa"), "PP": PPa_r}
                    alt = {"P": (Pb_r, "Pb"), "PT": (PTb_r, "PTb"), "X": (Xb_r, "Xb"), "PP": PPb_r}
                    for kk_ in range(1, 5):
                        last = (kk_ == 4)
                        (Pc, Pcn), (PTc, PTcn), (Xc, Xcn) = cur["P"], cur["PT"], cur["X"]
                        (Pn, Pnn), (PTn, PTnn), (Xn, Xnn) = alt["P"], alt["PT"], alt["X"]
                        PPn = alt["PP"]
                        for i, hh in enumerate(heads):
                            B = BH[i]
                            S.op("pe", lambda e, i=i, Pc=Pc, PTc=PTc: e.matmul(pbank[i][:, 0:128], lhsT=Pc[i], rhs=PTc[i],
                                                                             start=True, stop=True),
                                 reads=[B[Pcn], B[PTcn]], writes=[PB[i]])
                            if not last:
                                S.op("pe", lambda e, i=i, Pc=Pc, PTc=PTc: e.matmul(pbank[i][:, 128:256], lhsT=PTc[i], rhs=Pc[i],
                                                                                 start=True, stop=True),
                                     reads=[B[Pcn], B[PTcn]], writes=[PB[i]])
                        for i, hh in enumerate(heads):
                            B = BH[i]
                            if not last:
                                S.op("dve", lambda e, i=i, PPn=PPn: e.tensor_copy(out=PPn[i], in_=v3(pbank[i][:, 0:256], 128)),
                                     reads=[PB[i]], writes=[B[PTnn], B[Pnn]])
                            else:
                                S.op("dve", lambda e, i=i, PTn=PTn: e.tensor_copy(out=PTn[i], in_=pbank[i][:, 0:128]),
                                     reads=[PB[i]], writes=[B[PTnn]])
                        for i, hh in enumerate(heads):
                            B = BH[i]
                            S.op("pe", lambda e, i=i, PTn=PTn, Xc=Xc, ob=ob: e.matmul(pbank[ob][:, i * 128:(i + 1) * 128], lhsT=PTn[i],
                                                                                    rhs=Xc[i], start=True, stop=True),
                                 reads=[B[PTnn], B[Xcn]], writes=[PB[ob]])
                        for i, hh in enumerate(heads):
                            B = BH[i]
                            if not last:
                                S.op("dve", lambda e, i=i, Xn=Xn, Xc=Xc, ob=ob: e.tensor_tensor(
                                    out=Xn[i], in0=pbank[ob][:, i * 128:(i + 1) * 128], in1=Xc[i].bitcast(F32), op=ALU.add),
                                    reads=[PB[ob], B[Xcn]], writes=[B[Xnn]])
                            else:
                                S.op("dve", lambda e, i=i, Xc=Xc, ob=ob: e.tensor_tensor(
                                    out=TTb[i], in0=pbank[ob][:, i * 128:(i + 1) * 128], in1=Xc[i].bitcast(F32), op=ALU.add),
                                    reads=[PB[ob], B[Xcn]], writes=[B["TTb"]])
                        cur, alt = alt, cur
                    if DBG <= 6:
                        S.barrier(); return
                    for i, hh in enumerate(heads):
                        B = BH[i]
                        S.op("pe", lambda e, i=i: e.matmul(pbank[i][:, 0:128], lhsT=kb[i], rhs=TTb[i], start=True, stop=True),
                             reads=[B["kb"], B["TTb"]], writes=[PB[i]])
                        S.op("dve", lambda e, i=i: e.tensor_scalar(out=nwkT[i], in0=pbank[i][:, 0:128], scalar1=-1.0, scalar2=None,
                                                                    op0=ALU.mult),
                             reads=[PB[i]], writes=[B["nwkT"]])
                    for i, hh in enumerate(heads):
                        B = BH[i]
                        S.op("pe", lambda e, i=i: e.matmul(pbank[i][:, 128:256], lhsT=TTb[i], rhs=bv[i], start=True, stop=False),
                             reads=[B["TTb"], B["bv"]], writes=[PB[i]])
                        S.op("pe", lambda e, i=i, hh=hh: e.matmul(pbank[i][:, 128:256], lhsT=nwkT[i], rhs=Sb_st[:, hh, :],
                                                                 start=False, stop=True),
                             reads=[B["nwkT"], B_Sb[hh]], writes=[PB[i]])
                        S.op("dve", lambda e, i=i: e.tensor_copy(out=usb[i], in_=pbank[i][:, 128:256]),
                             reads=[PB[i]], writes=[B["usb"]])
                    for i, hh in enumerate(heads):
                        B = BH[i]
                        qdT = KQT[i][:, 256:384]
                        oc = slice(i * 128, (i + 1) * 128)
                        S.op("pe", lambda e, qdT=qdT, hh=hh, oc=oc, ob=ob: e.matmul(pbank[ob][:, oc], lhsT=qdT, rhs=Sb_st[:, hh, :],
                                                                                  start=True, stop=False),
                             reads=[B["KQT"], B_Sb[hh]], writes=[PB[ob]])
                        S.op("pe", lambda e, i=i, oc=oc, ob=ob: e.matmul(pbank[ob][:, oc], lhsT=qkTm[i], rhs=usb[i],
                                                                       start=False, stop=True),
                             reads=[B["qkTm"], B["usb"]], writes=[PB[ob]])
                        S.op("pe", lambda e, i=i: e.matmul(pbank[i][:, 256:384], lhsT=kd[i], rhs=usb[i], start=True, stop=True),
                             reads=[B["kd"], B["usb"]], writes=[PB[i]])
                        S.op("dve", lambda e, i=i, hh=hh: e.scalar_tensor_tensor(
                            out=S_st[:, hh, :], in0=S_st[:, hh, :], scalar=Egl[:, hh:hh + 1], in1=pbank[i][:, 256:384],
                            op0=ALU.mult, op1=ALU.add), reads=[PB[i], B_S[hh]] + gr, writes=[B_S[hh]])
                        S.op("act", lambda e, hh=hh: e.activation(out=Sb_st[:, hh, :], in_=S_st[:, hh, :], func=AF.Copy),
                             reads=[B_S[hh]], writes=[B_Sb[hh]])
                    for i, hh in enumerate(heads):
                        oc = slice(i * 128, (i + 1) * 128)
                        S.op("act", lambda e, oc=oc, ob=ob, hh=hh: e.activation(out=junk2, in_=pbank[ob][:, oc], func=AF.Square,
                                                                               accum_out=sso[:, hh:hh + 1]),
                             reads=[PB[ob]], writes=[B_j2, B_sso])
                if DBG <= 7:
                    S.barrier(); return
                S.op("act", lambda e: e.activation(out=rso, in_=sso, func=AF.Ln, scale=1.0 / 128, bias=EPS), reads=[B_sso], writes=[B_sso])
                S.op("act", lambda e: e.activation(out=rso, in_=rso, func=AF.Exp, scale=-0.5), reads=[B_sso], writes=[B_sso])
                for grp in range(2):
                    ob = 4 + grp
                    S.op("dve", lambda e, grp=grp, ob=ob: e.tensor_tensor(
                        out=on_all[:, grp * 4:(grp + 1) * 4, :], in0=v3(pbank[ob][:, :], 128),
                        in1=bc3(rso[:, grp * 4:(grp + 1) * 4], 128), op=ALU.mult),
                        reads=[PB[ob], B_sso], writes=[B_on])
                for hh in range(8):
                    S.op("pe", lambda e, hh=hh: e.transpose(out=pbankh[0][:, hh * 128:(hh + 1) * 128], in_=on_all[:, hh, :],
                                                            identity=identb[:]),
                         reads=[B_on, B_const], writes=[PBH[0]])
                S.op("dve", lambda e, blk=blk: e.tensor_tensor(out=ogT[:, :, blk], in0=v3(pbankh[0][:, :], 128),
                                                               in1=zsT[:, :, blk], op=ALU.mult),
                     reads=[PBH[0], B_zs], writes=[B_og[b]])
            for b in range(NB):
                do_block(b)
            resid_proj(lambda k, b: ogT[:, k, b * 128:(b + 1) * 128], 8,
                       lambda k, half: wout_t[:, k, half * 512:(half + 1) * 512], [B_wout], B_og)
            S.barrier()

        def conv_ffn(l):
            AFa.reset(); ABa.reset()
            norm_transpose()
            aT = v3(ABa.alloc(NFF * TT), TT)
            B_aT = [Buf("aT%d" % c) for c in range(NFF)]
            wdn = v3(ABa.alloc(NFF * 512), 512)
            B_wdn = Buf("wdn")

            def load_wdn(half):
                for (k0, k1) in ((0, 11), (11, 22)):
                    S.dma(lambda e, k0=k0, k1=k1: e.dma_start(
                        out=wdn[:, k0:k1, :],
                        in_=Wf_dn[l][k0 * 128:k1 * 128, half * 512:(half + 1) * 512].rearrange("(k p) n -> p k n", p=128)),
                        writes=[B_wdn])
            load_wdn(0)
            npre = 8
            acc = [AFa.alloc(TT) for _ in range(npre)]
            sg = [AFa.alloc(TT) for _ in range(4)]
            Bacc = [Buf("facc%d" % i) for i in range(npre)]
            Bsg = [Buf("sg%d" % i) for i in range(4)]
            cnt = [0]
            per = 2
            bi = 0
            for cb in range(0, NFF, per):
                n = min(per, NFF - cb)
                wg = load_wst(Wf_up[l], cb * 128, n * 128)
                wv = load_wst(Wf_up[l], DFF + cb * 128, n * 128)
                for j in range(n):
                    c = cb + j
                    items = []
                    for (wi, cg) in ((wg, c), (wv, NFF + c)):
                        bk = bi % 4
                        bi += 1
                        for k in range(8):
                            S.op("pe", lambda e, k=k, j=j, wi=wi, bk=bk: e.matmul(
                                pbank[bk][:, :], lhsT=wst[wi][:, k, j * 128:(j + 1) * 128], rhs=hnT[:, k, :],
                                start=(k == 0), stop=(k == 7)), reads=[B_wst[wi], B_hnT], writes=[PB[bk]])
                        i = cnt[0] % npre
                        cnt[0] += 1
                        items.append(dict(ps=pbank[bk], Bps=PB[bk], w=(lambda jj, cg=cg: fconv_c[:, l, cg, jj:jj + 1]),
                                          halo=halo_f[:, l, cg, :], Bhalo=B_halo_f[l][cg], ac=acc[i], Bac=Bacc[i]))
                    conv_from_psum(items, 3, act_fix=True)
                    ag, Bag, av, Bav = items[0]["ac"], items[0]["Bac"], items[1]["ac"], items[1]["Bac"]
                    s_ = sg[c % 4]
                    S.op("act", lambda e, ag=ag, s_=s_: e.activation(out=s_, in_=ag, func=AF.Silu), reads=[Bag], writes=[Bsg[c % 4]])
                    S.op("pool", lambda e, c=c, av=av, s_=s_: e.tensor_tensor(out=aT[:, c, :], in0=s_, in1=av, op=ALU.mult),
                         reads=[Bsg[c % 4], Bav], writes=[B_aT[c]])
            for half in range(2):
                if half == 1:
                    load_wdn(1)
                for b in range(NB):
                    bk = 4 + (b % 2)
                    for k in range(NFF):
                        S.op("pe", lambda e, b=b, k=k, bk=bk: e.matmul(
                            pbank[bk][:, :], lhsT=aT[:, k, b * 128:(b + 1) * 128], rhs=wdn[:, k, :],
                            start=(k == 0), stop=(k == NFF - 1)), reads=[B_wdn, B_aT[k]], writes=[PB[bk]])
                    hs = h_t[:, b, half * 512:(half + 1) * 512]
                    S.op("dve", lambda e, hs=hs, bk=bk: e.tensor_tensor(out=hs, in0=pbank[bk][:, :], in1=hs, op=ALU.add),
                         reads=[PB[bk], B_h[b]], writes=[B_h[b]])
            S.barrier()

        def ple(l, ti):
            AFa.reset(); ABa.reset()
            norm_transpose()
            wpr = v3(ABa.alloc(2 * D), D)
            B_wpr = Buf("wpr")
            S.dma(lambda e: e.dma_start(out=wpr, in_=Wp_pr[l].rearrange("(k p) n -> p k n", p=128)), writes=[B_wpr])
            S.dma(lambda e: e.dma_start(out=wout_t[:], in_=Wp_gt[l].rearrange("(k p) n -> p k n", p=128)), writes=[B_wout])
            pin = [AFa.alloc(256) for _ in range(2)]
            pT = [ABa.alloc(256) for _ in range(2)]
            gate = [AFa.alloc(512) for _ in range(2)]
            Bpin = [Buf("pin0"), Buf("pin1")]
            BpT = [Buf("pT0"), Buf("pT1")]
            Bgate = [Buf("gate0"), Buf("gate1")]
            gi = 0
            for b in range(NB):
                i = b % 2
                t0 = ti * TT + b * 128
                S.dma(lambda e, i=i, t0=t0: e.dma_start(out=pin[i], in_=p_d[l, t0:t0 + 128, :]), writes=[Bpin[i]])
                for k in range(2):
                    S.op("pe", lambda e, i=i, k=k: e.transpose(out=pbank[4][:, k * 128:(k + 1) * 128],
                                                               in_=pin[i][:, k * 128:(k + 1) * 128], identity=identf[:]),
                         reads=[Bpin[i], B_const], writes=[PB[4]])
                S.op("act", lambda e, i=i: e.activation(out=pT[i], in_=pbank[4][:, 0:256], func=AF.Copy), reads=[PB[4]], writes=[BpT[i]])
                for half in range(2):
                    hs = slice(half * 512, (half + 1) * 512)
                    gbk = half
                    pbk = 2 + half
                    for k in range(8):
                        S.op("pe", lambda e, b=b, k=k, hs=hs, gbk=gbk: e.matmul(
                            pbank[gbk][:, :], lhsT=hnT[:, k, b * 128:(b + 1) * 128], rhs=wout_t[:, k, hs],
                            start=(k == 0), stop=(k == 7)), reads=[B_hnT, B_wout], writes=[PB[gbk]])
                    for k in range(2):
                        S.op("pe", lambda e, i=i, k=k, hs=hs, pbk=pbk: e.matmul(
                            pbank[pbk][:, :], lhsT=pT[i][:, k * 128:(k + 1) * 128], rhs=wpr[:, k, hs],
                            start=(k == 0), stop=(k == 1)), reads=[BpT[i], B_wpr], writes=[PB[pbk]])
                    gt = gate[gi % 2]
                    Bg = Bgate[gi % 2]
                    gi += 1
                    S.op("act", lambda e, gt=gt, gbk=gbk: e.activation(out=gt, in_=pbank[gbk][:, :], func=AF.Sigmoid),
                         reads=[PB[gbk]], writes=[Bg])
                    S.op("dve", lambda e, gt=gt, pbk=pbk: e.tensor_tensor(out=gt, in0=pbank[pbk][:, :], in1=gt, op=ALU.mult),
                         reads=[PB[pbk], Bg], writes=[Bg])
                    S.op("dve", lambda e, gt=gt, b=b, hs=hs: e.tensor_tensor(out=h_t[:, b, hs], in0=h_t[:, b, hs], in1=gt, op=ALU.add),
                         reads=[Bg, B_h[b]], writes=[B_h[b]])
            S.barrier()

        def swa(ti):
            AFa.reset(); ABa.reset()
            norm_transpose()
            qT = v3(ABa.alloc(8 * TT), TT)
            kT0 = v3(ABa.alloc(4 * (TT + 128)), TT + 128)
            kT1 = v3(ABa.alloc(4 * (TT + 128)), TT + 128)
            kTz = (kT0, kT1)
            vx = ABa.alloc((NB + 1) * 4 * 72).rearrange("p (b g d) -> p b g d", g=4, d=72)
            aoT = v3(ABa.alloc(8 * TT), TT)
            B_qT = [Buf("qT%d" % c) for c in range(8)]
            B_kT = Buf("kT")
            B_vx = [Buf("vx%d" % b) for b in range(NB + 1)]
            B_ao = [Buf("ao%d" % b) for b in range(NB)]
            S.dma(lambda e: e.dma_start(out=wout_t[:], in_=Wb_out.rearrange("(k p) n -> p k n", p=128)), writes=[B_wout])
            S.op("pool", lambda e: e.memset(kT0, 0.0), writes=[B_kT])
            S.op("pool", lambda e: e.memset(kT1, 0.0), writes=[B_kT])
            S.op("pool", lambda e: e.tensor_copy(out=kT0[:, :, 0:128], in_=kT_prev[:, 0, :, :]), reads=[B_kprev], writes=[B_kT])
            S.op("pool", lambda e: e.tensor_copy(out=kT1[:, :, 0:128], in_=kT_prev[:, 1, :, :]), reads=[B_kprev], writes=[B_kT])
            S.op("pool", lambda e: e.tensor_copy(out=vx[:, 0, :, :], in_=vx_prev[:]), reads=[B_vprev], writes=[B_vx[0]])

            def consume(c, ps, Bps):
                if c < 8:
                    S.op("act", lambda e: e.activation(out=qT[:, c, :], in_=ps[:, :], func=AF.Copy), reads=[Bps], writes=[B_qT[c]])
                else:
                    g = c - 8
                    S.op("dve", lambda e: e.tensor_copy(out=kT0[0:64, g, 128:128 + TT], in_=ps[0:64, :]), reads=[Bps], writes=[B_kT])
                    S.op("dve", lambda e: e.tensor_copy(out=kT1[64:128, g, 128:128 + TT], in_=ps[64:128, :]), reads=[Bps], writes=[B_kT])
            proj_fm(Wb_in, 0, 12, consume)
            wv = ABa.alloc(8 * 256)
            wv3 = v3(wv, 256)
            B_wv = Buf("wv")
            S.dma(lambda e: e.dma_start(out=wv3, in_=Wb_in[:, 1536:1792].rearrange("(k p) n -> p k n", p=128)), writes=[B_wv])
            for b in range(NB):
                for k in range(8):
                    S.op("pe", lambda e, b=b, k=k: e.matmul(pbank[4][:, 0:256], lhsT=hnT[:, k, b * 128:(b + 1) * 128],
                                                            rhs=wv3[:, k, :], start=(k == 0), stop=(k == 7)),
                         reads=[B_hnT, B_wv], writes=[PB[4]])
                S.op("pool", lambda e, b=b: e.memset(vx[:, b + 1, :, :], 1.0), writes=[B_vx[b + 1]])
                S.op("act", lambda e, b=b: e.activation(out=vx[:, b + 1, :, 0:64], in_=v3(pbank[4][:, 0:256], 64), func=AF.Copy),
                     reads=[PB[4]], writes=[B_vx[b + 1]])
            S.op("pool", lambda e: e.tensor_copy(out=kT_prev[:, 0, :, :], in_=kT0[:, :, TT:TT + 128]), reads=[B_kT], writes=[B_kprev])
            S.op("pool", lambda e: e.tensor_copy(out=kT_prev[:, 1, :, :], in_=kT1[:, :, TT:TT + 128]), reads=[B_kT], writes=[B_kprev])
            S.op("pool", lambda e: e.tensor_copy(out=vx_prev[:], in_=vx[:, NB, :, :]), reads=[B_vx[NB]], writes=[B_vprev])

            tmp = [AFa.alloc(512) for _ in range(4)]
            ee = [ABa.alloc(512) for _ in range(4)]
            Btmp = [Buf("stmp%d" % i) for i in range(4)]
            Bee = [Buf("see%d" % i) for i in range(4)]
            ao = [ABa.alloc(D) for _ in range(2)]
            Bao_t = [Buf("aot0"), Buf("aot1")]
            den = smalls[:, 300:332]
            B_den = Buf("den")
            ri = [0]

            def swa_scores(b, g):
                first = (mode == "fused" and ti == 0 and b == 0)
                use_flag = (mode == "tile" and b == 0)
                parts = ([] if first else [0]) + [1]
                ebufs = {}
                for pc in parts:
                    bk = (2 * g + pc) % 4
                    r = ri[0] % 4
                    ri[0] += 1
                    kofs = b * 128 + pc * 128
                    for hh in range(4):
                        head = 4 * g + hh
                        c, half = head // 2, head % 2
                        kz = kTz[half]
                        S.op("pe", lambda e, bk=bk, kz=kz, g=g, kofs=kofs, c=c, b=b, hh=hh: e.matmul(
                            pbank[bk][:, hh * 128:(hh + 1) * 128], lhsT=kz[:, g, kofs:kofs + 128],
                            rhs=qT[:, c, b * 128:(b + 1) * 128], start=True, stop=True),
                            reads=[B_kT, B_qT[c]], writes=[PB[bk]])
                    S.op("dve", lambda e, bk=bk, r=r, g=g, pc=pc: e.scalar_tensor_tensor(
                        out=tmp[r], in0=pbank[bk][:, :], scalar=0.125, in1=swab[:, g, pc, :], op0=ALU.mult, op1=ALU.add),
                        reads=[PB[bk], B_const], writes=[Btmp[r]])
                    if use_flag and pc == 0:
                        S.op("act", lambda e, r=r: e.activation(out=ee[r], in_=tmp[r], func=AF.Exp, bias=flag_t[:, 0:1]),
                             reads=[Btmp[r], B_const], writes=[Bee[r]])
                    else:
                        S.op("act", lambda e, r=r: e.activation(out=ee[r], in_=tmp[r], func=AF.Exp), reads=[Btmp[r]], writes=[Bee[r]])
                    ebufs[pc] = r
                return parts, ebufs

            def swa_pv(b, g, parts, ebufs):
                ao_b = ao[b % 2]
                ps_o = pbank[4 + (g % 2)]
                Bpo = PB[4 + (g % 2)]
                for hh in range(4):
                    for n_, pc in enumerate(parts):
                        r = ebufs[pc]
                        S.op("pe", lambda e, ps_o=ps_o, hh=hh, r=r, b=b, pc=pc, g=g, n_=n_, np_=len(parts): e.matmul(
                            ps_o[:, hh * 72:hh * 72 + 66], lhsT=ee[r][:, hh * 128:(hh + 1) * 128], rhs=vx[:, b + pc, g, 0:66],
                            start=(n_ == 0), stop=(n_ == np_ - 1)),
                            reads=[Bee[r], B_vx[b + pc]], writes=[Bpo])
                dn = den[:, g * 4:(g + 1) * 4]
                po3 = ps_o[:, 0:288].rearrange("p (h d) -> p h d", d=72)
                S.op("dve", lambda e, dn=dn, po3=po3, g=g: e.tensor_tensor(
                    out=dn, in0=po3[:, :, 64], in1=esink_bc[:, g * 4:(g + 1) * 4], op=ALU.add),
                    reads=[Bpo, B_const], writes=[B_den])
                S.op("dve", lambda e, dn=dn: e.reciprocal(out=dn, in_=dn), reads=[B_den], writes=[B_den])
                S.op("dve", lambda e, dn=dn, po3=po3, g=g, ao_b=ao_b: e.tensor_tensor(
                    out=ao_b[:, g * 256:(g + 1) * 256].rearrange("p (h d) -> p h d", d=64), in0=po3[:, :, 0:64],
                    in1=bc3(dn, 64), op=ALU.mult),
                    reads=[Bpo, B_den], writes=[Bao_t[b % 2]])

            def swa_finish(b):
                ao_b = ao[b % 2]
                for k in range(8):
                    S.op("pe", lambda e, k=k, ao_b=ao_b: e.transpose(out=pbankh[1][:, k * 128:(k + 1) * 128],
                                                                     in_=ao_b[:, k * 128:(k + 1) * 128], identity=identb[:]),
                         reads=[Bao_t[b % 2], B_const], writes=[PBH[1]])
                S.op("act", lambda e, b=b: e.activation(out=aoT[:, :, b * 128:(b + 1) * 128], in_=v3(pbankh[1][:, :], 128), func=AF.Copy),
                     reads=[PBH[1]], writes=[B_ao[b]])

            items_ = [(b, g) for b in range(NB) for g in range(4)]
            pend = swa_scores(*items_[0])
            for k_, (b, g) in enumerate(items_):
                nxt = swa_scores(*items_[k_ + 1]) if k_ + 1 < len(items_) else None
                swa_pv(b, g, *pend)
                if g == 3:
                    swa_finish(b)
                pend = nxt
            resid_proj(lambda k, b: aoT[:, k, b * 128:(b + 1) * 128], 8,
                       lambda k, half: wout_t[:, k, half * 512:(half + 1) * 512], [B_wout], B_ao, banks=(0, 1))
            S.barrier()

        def final_out(ti, do_norm=True):
            AFa.reset(); ABa.reset()
            ss = smalls[:, 0:NB]
            rstd = smalls[:, 8:8 + NB]
            Bss = Buf("fss")
            junk = ABa.alloc(D)
            Bj = Buf("fjunk")
            ot = [AFa.alloc(D) for _ in range(2)]
            Bot = [Buf("ot0"), Buf("ot1")]
            if do_norm:
                for b in range(NB):
                    S.op("act", lambda e, b=b: e.activation(out=junk, in_=h_t[:, b, :], func=AF.Square, accum_out=ss[:, b:b + 1]),
                         reads=[B_h[b]], writes=[Bj, Bss])
                S.op("act", lambda e: e.activation(out=rstd, in_=ss, func=AF.Ln, scale=1.0 / D, bias=EPS), reads=[Bss], writes=[Bss])
                S.op("act", lambda e: e.activation(out=rstd, in_=rstd, func=AF.Exp, scale=-0.5), reads=[Bss], writes=[Bss])
            for b in range(NB):
                t0 = ti * TT + b * 128
                if do_norm:
                    o_ = ot[b % 2]
                    S.op("dve", lambda e, b=b, o_=o_: e.scalar_tensor_tensor(out=o_, in0=h_t[:, b, :], scalar=rstd[:, b:b + 1],
                                                                            in1=nfin_bc[:], op0=ALU.mult, op1=ALU.mult),
                         reads=[B_h[b], Bss, B_const], writes=[Bot[b % 2]])
                    S.dma(lambda e, o_=o_, t0=t0: e.dma_start(out=out_d[t0:t0 + 128, :], in_=o_), reads=[Bot[b % 2]], final=True)
                else:
                    S.dma(lambda e, b=b, t0=t0: e.dma_start(out=out_d[t0:t0 + 128, :], in_=h_t[:, b, :]), reads=[B_h[b]], final=True)
            S.barrier()

        const_setup()
        S.barrier()
        if mode != "tile":
            weight_prep()
        if mode != "prep":
            stages = [s_ for s_ in ["delta", "ffn0", "ple0", "swa", "ffn1", "ple1"] if s_ in STAGES]
            for ti in range(n_tiles):
                for b in range(NB):
                    t0 = ti * TT + b * 128
                    S.dma(lambda e, b=b, t0=t0: e.dma_start(out=h_t[:, b, :], in_=x_d[t0:t0 + 128, :]), writes=[B_h[b]])
                done = False
                for sname in stages:
                    if stop_after == "prep":
                        done = True
                        break
                    if sname == "delta":
                        deltanet(ti)
                    elif sname == "ffn0":
                        conv_ffn(0)
                    elif sname == "ple0":
                        ple(0, ti)
                    elif sname == "swa":
                        swa(ti)
                    elif sname == "ffn1":
                        conv_ffn(1)
                    elif sname == "ple1":
                        ple(1, ti)
                    if stop_after == sname:
                        done = True
                        break
                final_out(ti, do_norm=not done)
            if mode == "tile":
                S.barrier()
                for (t_, dst_) in ((halo_a, ha_out), (halo_f, hf_out), (S_st, S_out), (kT_prev, kp_out), (vx_prev, vp_out)):
                    S.dma(lambda e, t_=t_, dst_=dst_: e.dma_start(out=dst_, in_=t_[:]), final=True)
        S.emit()
    return nc


_PREP_IN = ["norm_mix", "norm_ffn", "norm_ple", "a_w_in", "a_norm", "a_w_out", "b_w_in", "b_w_out",
            "f_w_up", "f_w_down", "ple_w_proj", "ple_w_gate"]
_TILE_PARAMS = ["norm_final", "a_conv", "a_log", "a_dt_bias", "b_sinks", "f_conv"]
_SQUEEZE = {"a_w_in", "a_conv", "a_log", "a_dt_bias", "a_norm", "a_w_out", "b_w_in", "b_sinks", "b_w_out"}
_WB = ["Wa_in", "Wa_ba", "Wa_out", "Wb_in", "Wb_out", "Wf_up0", "Wf_up1", "Wf_dn0", "Wf_dn1",
       "Wp_pr0", "Wp_pr1", "Wp_gt0", "Wp_gt1"]


def _prm(inputs, n):
    a = np.asarray(inputs[n], dtype=np.float32)
    if n in _SQUEEZE:
        a = a[0]
    return np.ascontiguousarray(a)


def make_in_maps(inputs, n_cores=8):
    shared = {n: _prm(inputs, n) for n in _PREP_IN + _TILE_PARAMS}
    x = np.asarray(inputs["x"], dtype=np.float32)
    p = np.asarray(inputs["p"], dtype=np.float32)
    maps = []
    for c in range(n_cores):
        m = dict(shared)
        m["x"] = np.ascontiguousarray(x[c])
        m["p"] = np.ascontiguousarray(p[:, c])
        maps.append(m)
    return maps


def kernel_tiles(inputs, n_cores=8, n_tiles=NT):
    import ml_dtypes
    bf = ml_dtypes.bfloat16
    nc_p = build_program(mode="prep")
    res = run_bass_kernel_spmd(nc_p, [{n: _prm(inputs, n) for n in _PREP_IN}], core_ids=[0])
    wb = {n: np.asarray(res.results[0][n]) for n in _WB}
    params = {n: _prm(inputs, n) for n in _TILE_PARAMS}
    x = np.asarray(inputs["x"], dtype=np.float32)
    p = np.asarray(inputs["p"], dtype=np.float32)
    nc_t = build_program(mode="tile")
    st = [{"S_in": np.zeros((128, 8, 128), np.float32), "ha_in": np.zeros((128, 24, 3), np.float32),
           "hf_in": np.zeros((128, 2, 44, 2), np.float32), "kp_in": np.zeros((128, 2, 4, 128), bf),
           "vp_in": np.zeros((128, 4, 72), bf)} for _ in range(n_cores)]
    out = np.zeros((n_cores, n_tiles * TT, D), np.float32)
    for ti in range(n_tiles):
        flag = np.full((128, 1), -BIG if ti == 0 else 0.0, np.float32)
        maps = []
        for c in range(n_cores):
            m = dict(wb)
            m.update(params)
            m.update(st[c])
            m["flag_in"] = flag
            m["x"] = np.ascontiguousarray(x[c, ti * TT:(ti + 1) * TT])
            m["p"] = np.ascontiguousarray(p[:, c, ti * TT:(ti + 1) * TT])
            maps.append(m)
        r = run_bass_kernel_spmd(nc_t, maps, core_ids=list(range(n_cores)))
        for c in range(n_cores):
            rc = r.results[c]
            out[c, ti * TT:(ti + 1) * TT] = np.asarray(rc["out"], dtype=np.float32)
            st[c] = {"S_in": np.asarray(rc["S_out"]), "ha_in": np.asarray(rc["ha_out"]), "hf_in": np.asarray(rc["hf_out"]),
                     "kp_in": np.asarray(rc["kp_out"]), "vp_in": np.asarray(rc["vp_out"])}
    return out


def kernel(**inputs):
    nc = build_program(mode="fused")
    maps = make_in_maps(inputs)
    res = run_bass_kernel_spmd(nc, maps, core_ids=list(range(8)))
    return np.stack([np.asarray(r["out"], dtype=np.float32) for r in res.results], axis=0)
```

```python
import contextlib
import numpy as np
import concourse.bass as bass
import concourse.mybir as mybir
from concourse.bass_utils import run_bass_kernel_spmd
from concourse.alu_op_type import AluOpType as ALU

F32 = mybir.dt.float32
BF16 = mybir.dt.bfloat16
F32R = mybir.dt.float32r
AF = mybir.ActivationFunctionType

T_SEQ = 4096
D = 1024
TT = 512
NB = TT // 128
NT = T_SEQ // TT
DFF = 2816
NFF = DFF // 128
EPS = 1e-6
BIG = 30000.0
import os
DBG = float(os.environ.get('K_DBG', '99'))
STAGES = os.environ.get('K_STAGES', 'delta,ffn0,ple0,swa,ffn1,ple1').split(',')


class Buf:
    __slots__ = ("name", "w", "r", "excl")

    def __init__(self, name="", excl=False):
        self.name = name
        self.w = None
        self.r = []
        self.excl = excl


class Sched:
    ENGS = ("pe", "act", "dve", "pool", "sp")

    def __init__(self, nc, n_dma_sems=32, same_engine_sync=True):
        self.nc = nc
        self.ops = {e: [] for e in self.ENGS}
        self.count = {e: 0 for e in self.ENGS}
        self.waited = {e: {} for e in self.ENGS}
        self.pending = {e: [] for e in self.ENGS}
        self.n_dma = n_dma_sems
        self.dma_uses = [0] * n_dma_sems
        self.dma_next = 0
        self.same_engine_sync = same_engine_sync
        self.final_tokens = []

    def _deps(self, reads, writes, eng=None):
        deps = []
        for b in reads:
            if b.w is not None:
                deps.append(b.w)
            if b.excl:
                deps.extend(t for t in b.r if t[0] != eng)
        for b in writes:
            if b.w is not None:
                deps.append(b.w)
            deps.extend(b.r)
        return deps

    def _commit(self, tok, reads, writes):
        for b in reads:
            b.r.append(tok)
            if len(b.r) > 48:
                best = {}
                for k, v in b.r:
                    if best.get(k, 0) < v:
                        best[k] = v
                b.r = list(best.items())
        for b in writes:
            b.w = tok
            b.r = []

    def _waits(self, eng, deps):
        out = []
        wd = self.waited[eng]
        best = {}
        for k, v in deps:
            if best.get(k, 0) < v:
                best[k] = v
        for k, v in best.items():
            if k == eng and (eng == "pe" or not self.same_engine_sync):
                continue
            if wd.get(k, 0) >= v:
                continue
            wd[k] = v
            out.append((k, v))
        return out

    def op(self, eng, fn, reads=(), writes=()):
        deps = self._deps(reads, writes, eng)
        if self.pending[eng]:
            deps = deps + self.pending[eng]
            self.pending[eng] = []
        waits = self._waits(eng, deps)
        self.count[eng] += 1
        tok = (eng, self.count[eng])
        self.ops[eng].append((waits, fn, eng, 1))
        self._commit(tok, reads, writes)
        return tok

    def dma(self, fn, reads=(), writes=(), queue="sp", final=False):
        deps = self._deps(reads, writes)
        if self.pending[queue]:
            deps = deps + self.pending[queue]
            self.pending[queue] = []
        slot = self.dma_next
        self.dma_next = (self.dma_next + 1) % self.n_dma
        key = ("dma", slot)
        if self.dma_uses[slot] > 0:
            deps.append((key, 16 * self.dma_uses[slot]))
        waits = self._waits(queue, deps)
        self.dma_uses[slot] += 1
        tok = (key, 16 * self.dma_uses[slot])
        self.ops[queue].append((waits, fn, key, 16))
        self._commit(tok, reads, writes)
        if final:
            self.final_tokens.append(tok)
        return tok

    def barrier(self):
        toks = [(e, self.count[e]) for e in self.ENGS if self.count[e] > 0 and e != "sp"]
        for i in range(self.n_dma):
            if self.dma_uses[i] > 0:
                toks.append((("dma", i), 16 * self.dma_uses[i]))
        for e in self.ENGS:
            self.pending[e] = self.pending[e] + toks

    def emit(self):
        nc = self.nc
        with contextlib.ExitStack() as st:
            sems = {}
            for e in self.ENGS:
                sems[e] = st.enter_context(nc.semaphore("s_" + e))
            for i in range(self.n_dma):
                sems[("dma", i)] = st.enter_context(nc.semaphore("s_dma%d" % i))
            fw = self._waits("sp", list(self.final_tokens))
            block = st.enter_context(nc.Block())

            def run(engname, engobj, extra_waits=()):
                for waits, fn, inc_key, inc_val in self.ops[engname]:
                    for k, v in waits:
                        engobj.wait_ge(sems[k], v)
                    ins = fn(engobj)
                    ins.then_inc(sems[inc_key], inc_val)
                for k, v in extra_waits:
                    engobj.wait_ge(sems[k], v)

            @block.tensor
            def _(e):
                run("pe", e)

            @block.scalar
            def _(e):
                run("act", e)

            @block.vector
            def _(e):
                run("dve", e)

            @block.gpsimd
            def _(e):
                run("pool", e)

            @block.sync
            def _(e):
                run("sp", e, fw)


class Arena:
    def __init__(self, ap, ncols):
        self.ap = ap
        self.n = ncols
        self.pos = 0

    def reset(self):
        self.pos = 0

    def alloc(self, cols):
        cols_al = (cols + 7) // 8 * 8
        assert self.pos + cols_al <= self.n, ("arena overflow", self.pos, cols, self.n)
        a = self.ap[:, self.pos:self.pos + cols]
        self.pos += cols_al
        return a


def bc3(ap2, n):
    return ap2.rearrange("p (h o) -> p h o", o=1).to_broadcast([ap2.shape[0], ap2.shape[1], n])


def v3(ap, inner):
    return ap.rearrange("p (a b) -> p a b", b=inner)


def build_program(stop_after=None, n_tiles=NT, mode="fused"):
    nc = bass.Bass("TRN2", target_bir_lowering=False)
    if mode == "tile":
        n_tiles = 1
    T_LOC = T_SEQ if mode == "fused" else TT

    def din(name, shape, dt=F32):
        return nc.dram_tensor(name, shape, dt, kind="ExternalInput").ap()

    def dout(name, shape, dt=F32):
        return nc.dram_tensor(name, shape, dt, kind="ExternalOutput").ap()

    if mode != "prep":
        x_d = din("x", [T_LOC, D])
        p_d = din("p", [2, T_LOC, 256])
        norm_final = din("norm_final", [D])
        a_conv = din("a_conv", [4, 3072])
        a_log = din("a_log", [8])
        a_dt_bias = din("a_dt_bias", [8])
        b_sinks = din("b_sinks", [16])
        f_conv = din("f_conv", [2, 3, 2 * DFF])
        out_d = dout("out", [T_LOC, D])
    if mode != "tile":
        norm_mix = din("norm_mix", [2, D])
        norm_ffn = din("norm_ffn", [2, D])
        norm_ple = din("norm_ple", [2, D])
        a_w_in = din("a_w_in", [D, 4112])
        a_norm = din("a_norm", [128])
        a_w_out = din("a_w_out", [D, D])
        b_w_in = din("b_w_in", [D, 1536])
        b_w_out = din("b_w_out", [D, D])
        f_w_up = din("f_w_up", [2, D, 2 * DFF])
        f_w_down = din("f_w_down", [2, DFF, D])
        ple_w_proj = din("ple_w_proj", [2, 256, D])
        ple_w_gate = din("ple_w_gate", [2, D, D])

    def dscr(name, shape):
        if mode == "prep":
            return dout(name, shape, BF16)
        if mode == "tile":
            return din(name, shape, BF16)
        return nc.dram_tensor(name, shape, BF16).ap()

    Wa_in = dscr("Wa_in", [D, 4096])
    Wa_ba = dscr("Wa_ba", [D, 16])
    Wa_out = dscr("Wa_out", [D, D])
    Wb_in = dscr("Wb_in", [D, 1024 + 512 + 256])
    Wb_out = dscr("Wb_out", [D, D])
    Wf_up = [dscr("Wf_up%d" % l, [D, 2 * DFF]) for l in range(2)]
    Wf_dn = [dscr("Wf_dn%d" % l, [DFF, D]) for l in range(2)]
    Wp_pr = [dscr("Wp_pr%d" % l, [256, D]) for l in range(2)]
    Wp_gt = [dscr("Wp_gt%d" % l, [D, D]) for l in range(2)]
    if mode == "tile":
        S_in = din("S_in", [128, 8, 128]); S_out = dout("S_out", [128, 8, 128])
        ha_in = din("ha_in", [128, 24, 3]); ha_out = dout("ha_out", [128, 24, 3])
        hf_in = din("hf_in", [128, 2, 44, 2]); hf_out = dout("hf_out", [128, 2, 44, 2])
        kp_in = din("kp_in", [128, 2, 4, 128], BF16); kp_out = dout("kp_out", [128, 2, 4, 128], BF16)
        vp_in = din("vp_in", [128, 4, 72], BF16); vp_out = dout("vp_out", [128, 4, 72], BF16)
        flag_in = din("flag_in", [128, 1])

    S = Sched(nc)
    st = contextlib.ExitStack()
    with st:
        def sb(name, shape, dt):
            return st.enter_context(nc.sbuf_tensor(name, shape, dt))

        h_t = sb("h", [128, NB, D], F32)
        hnT = sb("hnT", [128, 8, TT], BF16)
        identb = sb("identb", [128, 128], BF16)
        identf = sb("identf", [128, 128], F32)
        onesf = sb("onesf", [128, 128], F32)
        triU = sb("triU", [128, 128], F32)
        mposS = sb("mposS", [128, 128], F32)
        mnegI = sb("mnegI", [128, 128], F32)
        nrm = sb("nrm", [128, 7, 8], F32)
        anorm_c = sb("anorm_c", [128, 1], F32)
        nfin_bc = sb("nfin_bc", [128, D], F32)
        aconv_c = sb("aconv_c", [128, 24, 4], F32)
        fconv_c = sb("fconv_c", [128, 2, 44, 3], F32)
        negA_bc = sb("negA_bc", [128, 8], F32)
        dtb_bc = sb("dtb_bc", [128, 8], F32)
        esink_bc = sb("esink_bc", [128, 16], F32)
        halo_a = sb("halo_a", [128, 24, 3], F32)
        halo_f = sb("halo_f", [128, 2, 44, 2], F32)
        S_st = sb("S_st", [128, 8, 128], F32)
        Sb_st = sb("Sb_st", [128, 8, 128], BF16)
        kT_prev = sb("kT_prev", [128, 2, 4, 128], BF16)
        vx_prev = sb("vx_prev", [128, 4, 72], BF16)
        swab = sb("swab", [128, 4, 2, 512], F32)
        wst = [sb("wst%d" % i, [128, 8, 256], BF16) for i in range(3)]
        wout_t = sb("wout_t", [128, 8, D], BF16)
        smalls = sb("smalls", [128, 512], F32)
        arena_f_t = sb("arena_f", [128, 8960], F32)
        neu_t = sb("neu", [128, 4, 6, 128], F32R)
        arena_b_t = sb("arena_b", [128, 31488], BF16)
        AFa = Arena(arena_f_t[:], 8960)
        ABa = Arena(arena_b_t[:], 31488)

        pbank = [st.enter_context(nc.psum_tensor("pb%d" % i, [128, 512], F32)) for i in range(6)]
        pbankh = [st.enter_context(nc.psum_tensor("pbh%d" % i, [128, 1024], BF16)) for i in range(2)]
        PB = [Buf("pb%d" % i, excl=True) for i in range(6)]
        PBH = [Buf("pbh%d" % i, excl=True) for i in range(2)]

        B_h = [Buf("h%d" % b) for b in range(NB)]
        B_hnT = Buf("hnT")
        B_const = Buf("const")
        B_wst = [Buf("wst%d" % i) for i in range(3)]
        B_wout = Buf("wout")
        B_S = [Buf("S%d" % i) for i in range(8)]
        B_Sb = [Buf("Sb%d" % i) for i in range(8)]
        B_halo_a = [Buf("haloa%d" % i) for i in range(24)]
        B_halo_f = [[Buf("halof%d_%d" % (l, i)) for i in range(44)] for l in range(2)]
        B_kprev = Buf("kprev")
        B_vprev = Buf("vprev")
        wst_rr = [0]

        flag_t = sb("flag_t", [128, 1], F32)

        def const_setup():
            rd, wr = [B_const], [B_const]

            def ld(out_ap, in_ap):
                S.dma(lambda e: e.dma_start(out=out_ap, in_=in_ap, allow_slow_non_contiguous=True), writes=wr)
            if mode != "tile":
                for i, (src, l) in enumerate([(norm_mix, 0), (norm_mix, 1), (norm_ffn, 0), (norm_ffn, 1),
                                              (norm_ple, 0), (norm_ple, 1)]):
                    ld(nrm[:, i, :], src[l].rearrange("(k p) -> p k", p=128))
                ld(anorm_c[:], a_norm.rearrange("(p o) -> p o", o=1))
            if mode == "prep":
                return
            S.op("pool", lambda e: e.memset(identf[:], 0.0), writes=wr)
            S.op("pool", lambda e: e.affine_select(out=identf[:], in_=identf[:], pattern=[[-1, 128]],
                                                   compare_op=ALU.not_equal, fill=1.0, base=0, channel_multiplier=1),
                 reads=rd, writes=wr)
            S.op("pool", lambda e: e.tensor_copy(out=identb[:], in_=identf[:]), reads=rd, writes=wr)
            S.op("pool", lambda e: e.memset(onesf[:], 1.0), writes=wr)
            S.op("pool", lambda e: e.affine_select(out=triU[:], in_=onesf[:], pattern=[[1, 128]],
                                                   compare_op=ALU.is_ge, fill=0.0, base=0, channel_multiplier=-1),
                 reads=rd, writes=wr)
            S.op("pool", lambda e: e.memset(mposS[:], 0.0), writes=wr)
            S.op("pool", lambda e: e.affine_select(out=mposS[:], in_=mposS[:], pattern=[[-1, 128]],
                                                   compare_op=ALU.is_gt, fill=BIG, base=0, channel_multiplier=1),
                 reads=rd, writes=wr)
            S.op("pool", lambda e: e.memset(mnegI[:], 0.0), writes=wr)
            S.op("pool", lambda e: e.affine_select(out=mnegI[:], in_=mnegI[:], pattern=[[1, 128]],
                                                   compare_op=ALU.is_ge, fill=-BIG, base=0, channel_multiplier=-1),
                 reads=rd, writes=wr)
            if mode == "tile":
                for (t_, src_) in ((halo_a, ha_in), (halo_f, hf_in), (S_st, S_in), (kT_prev, kp_in), (vx_prev, vp_in),
                                   (flag_t, flag_in)):
                    S.dma(lambda e, t_=t_, src_=src_: e.dma_start(out=t_[:], in_=src_), writes=wr)
                S.op("pool", lambda e: e.tensor_copy(out=Sb_st[:], in_=S_st[:]), reads=rd, writes=wr)
            else:
                for t_ in (halo_a, halo_f, S_st):
                    S.op("pool", lambda e, t_=t_: e.memset(t_[:], 0.0), writes=wr)
                S.op("pool", lambda e: e.memset(Sb_st[:], 0.0), writes=wr)
                S.op("pool", lambda e: e.memset(kT_prev[:], 0.0), writes=wr)
                S.op("pool", lambda e: e.memset(vx_prev[:], 0.0), writes=wr)
                S.op("pool", lambda e: e.memset(flag_t[:], 0.0), writes=wr)
            for g in range(4):
                for hh in range(4):
                    head = 4 * g + hh
                    slope = 2.0 ** (-8.0 * (head + 1) / 16.0)
                    for pc in range(2):
                        dst = swab[:, g, pc, hh * 128:(hh + 1) * 128]
                        S.op("pool", lambda e, dst=dst, pc=pc: e.iota(dst, [[1, 128]], base=(128 if pc == 0 else 0),
                                                                      channel_multiplier=-1,
                                                                      allow_small_or_imprecise_dtypes=True),
                             reads=rd, writes=wr)
                        S.op("pool", lambda e, dst=dst, slope=slope: e.tensor_scalar(out=dst, in0=dst, scalar1=-slope,
                                                                                    scalar2=None, op0=ALU.mult),
                             reads=rd, writes=wr)
                        if pc == 0:
                            S.op("pool", lambda e, dst=dst: e.affine_select(out=dst, in_=dst, pattern=[[-1, 128]],
                                                                            compare_op=ALU.is_gt, fill=-BIG, base=0,
                                                                            channel_multiplier=1),
                                 reads=rd, writes=wr)
                        else:
                            S.op("pool", lambda e, dst=dst: e.affine_select(out=dst, in_=dst, pattern=[[1, 128]],
                                                                            compare_op=ALU.is_ge, fill=-BIG, base=0,
                                                                            channel_multiplier=-1),
                                 reads=rd, writes=wr)
            ld(nfin_bc[:], bass.AP(norm_final.tensor, 0, [[0, 128], [1, D]]))
            for i in range(4):
                ld(aconv_c[:, :, i], a_conv[i].rearrange("(c p) -> p c", p=128))
            for l in range(2):
                for i in range(3):
                    ld(fconv_c[:, l, :, i], f_conv[l, i].rearrange("(c p) -> p c", p=128))
            ld(negA_bc[:], bass.AP(a_log.tensor, 0, [[0, 128], [1, 8]]))
            ld(dtb_bc[:], bass.AP(a_dt_bias.tensor, 0, [[0, 128], [1, 8]]))
            ld(esink_bc[:], bass.AP(b_sinks.tensor, 0, [[0, 128], [1, 16]]))
            S.op("act", lambda e: e.activation(out=negA_bc[:], in_=negA_bc[:], func=AF.Exp), reads=rd, writes=wr)
            S.op("dve", lambda e: e.tensor_scalar(out=negA_bc[:], in0=negA_bc[:], scalar1=-1.0, scalar2=None,
                                                  op0=ALU.mult), reads=rd, writes=wr)
            S.op("act", lambda e: e.activation(out=esink_bc[:], in_=esink_bc[:], func=AF.Exp), reads=rd, writes=wr)

        def weight_prep():
            AFa.reset(); ABa.reset()
            nst = 4
            PW = 2048
            stg_f = [AFa.alloc(PW) for _ in range(nst)]
            stg_b = [ABa.alloc(PW) for _ in range(nst)]
            Bf = [Buf("stgf%d" % i) for i in range(nst)]
            Bb = [Buf("stgb%d" % i) for i in range(nst)]
            engs = ["dve", "act"]
            pieces = []

            def conv_mat(src, dst, nrows, c0, ncols, d0, scale_fn):
                for k in range(nrows // 128):
                    for cc in range(0, ncols, PW):
                        n = min(PW, ncols - cc)
                        pieces.append((src[k * 128:(k + 1) * 128, c0 + cc:c0 + cc + n],
                                       dst[k * 128:(k + 1) * 128, d0 + cc:d0 + cc + n], n,
                                       scale_fn(k) if scale_fn else None))

            def emit_pieces():
                LA = nst - 1
                npc = len(pieces)
                for t in range(npc + LA):
                    if t < npc:
                        src_ap, _, ncols, _ = pieces[t]
                        i = t % nst
                        S.dma(lambda e, i=i, ncols=ncols, src_ap=src_ap: e.dma_start(out=stg_f[i][:, 0:ncols], in_=src_ap),
                              writes=[Bf[i]])
                    u = t - LA
                    if u >= 0:
                        _, dst_ap, ncols, scale_ap = pieces[u]
                        i = u % nst
                        sf, sbb = stg_f[i][:, 0:ncols], stg_b[i][:, 0:ncols]
                        eng = engs[u % 2]
                        rds = [Bf[i]] + ([B_const] if scale_ap is not None else [])
                        if eng == "act":
                            if scale_ap is None:
                                S.op("act", lambda e, sf=sf, sbb=sbb: e.activation(out=sbb, in_=sf, func=AF.Copy), reads=rds, writes=[Bb[i]])
                            else:
                                S.op("act", lambda e, sf=sf, sbb=sbb, scale_ap=scale_ap: e.activation(out=sbb, in_=sf, func=AF.Copy, scale=scale_ap),
                                     reads=rds, writes=[Bb[i]])
                        else:
                            if scale_ap is None:
                                S.op("dve", lambda e, sf=sf, sbb=sbb: e.tensor_copy(out=sbb, in_=sf), reads=rds, writes=[Bb[i]])
                            else:
                                S.op("dve", lambda e, sf=sf, sbb=sbb, scale_ap=scale_ap: e.tensor_scalar(out=sbb, in0=sf, scalar1=scale_ap,
                                                                                                      scalar2=None, op0=ALU.mult),
                                     reads=rds, writes=[Bb[i]])
                        S.dma(lambda e, dst_ap=dst_ap, sbb=sbb: e.dma_start(out=dst_ap, in_=sbb), reads=[Bb[i]], final=(mode == "prep"))

            conv_mat(a_w_in, Wa_in, D, 0, 4096, 0, lambda k: nrm[:, 0, k:k + 1])
            conv_mat(a_w_in, Wa_ba, D, 4096, 16, 0, lambda k: nrm[:, 0, k:k + 1])
            conv_mat(a_w_out, Wa_out, D, 0, D, 0, lambda k: anorm_c[:, 0:1])
            conv_mat(b_w_in, Wb_in, D, 0, 1024, 0, lambda k: nrm[:, 1, k:k + 1])
            for g in range(4):
                for dup in range(2):
                    conv_mat(b_w_in, Wb_in, D, 1024 + g * 64, 64, 1024 + g * 128 + dup * 64, lambda k: nrm[:, 1, k:k + 1])
            conv_mat(b_w_in, Wb_in, D, 1280, 256, 1536, lambda k: nrm[:, 1, k:k + 1])
            conv_mat(b_w_out, Wb_out, D, 0, D, 0, None)
            for l in range(2):
                conv_mat(f_w_up[l], Wf_up[l], D, 0, 2 * DFF, 0, lambda k, l=l: nrm[:, 2 + l, k:k + 1])
                conv_mat(f_w_down[l], Wf_dn[l], DFF, 0, D, 0, None)
                conv_mat(ple_w_proj[l], Wp_pr[l], 256, 0, D, 0, None)
                conv_mat(ple_w_gate[l], Wp_gt[l], D, 0, D, 0, lambda k, l=l: nrm[:, 4 + l, k:k + 1])
            emit_pieces()
            S.barrier()

        def load_wst(Wd, c0, ncols):
            i = wst_rr[0] % 3
            wst_rr[0] += 1
            dst = wst[i][:, :, 0:ncols]
            src = Wd[:, c0:c0 + ncols].rearrange("(k p) n -> p k n", p=128)
            S.dma(lambda e: e.dma_start(out=dst, in_=src), writes=[B_wst[i]])
            return i

        def norm_transpose():
            ss = smalls[:, 0:NB]
            rstd = smalls[:, 8:8 + NB]
            Bss = Buf("ss")
            junk = ABa.alloc(D)
            Bj = Buf("junk")
            xs = [ABa.alloc(D) for _ in range(2)]
            Bxs = [Buf("xs0"), Buf("xs1")]
            for b in range(NB):
                S.op("act", lambda e, b=b: e.activation(out=junk, in_=h_t[:, b, :], func=AF.Square,
                                                        accum_out=ss[:, b:b + 1]),
                     reads=[B_h[b]], writes=[Bj, Bss])
            S.op("act", lambda e: e.activation(out=rstd, in_=ss, func=AF.Ln, scale=1.0 / D, bias=EPS),
                 reads=[Bss], writes=[Bss])
            S.op("act", lambda e: e.activation(out=rstd, in_=rstd, func=AF.Exp, scale=-0.5), reads=[Bss], writes=[Bss])
            for b in range(NB):
                x_ = xs[b % 2]
                S.op("dve", lambda e, b=b, x_=x_: e.tensor_scalar(out=x_, in0=h_t[:, b, :], scalar1=rstd[:, b:b + 1],
                                                                  scalar2=None, op0=ALU.mult),
                     reads=[B_h[b], Bss], writes=[Bxs[b % 2]])
                pt = pbankh[b % 2]
                for k in range(8):
                    S.op("pe", lambda e, k=k, x_=x_, pt=pt: e.transpose(out=pt[:, k * 128:(k + 1) * 128],
                                                                        in_=x_[:, k * 128:(k + 1) * 128],
                                                                        identity=identb[:]),
                         reads=[Bxs[b % 2], B_const], writes=[PBH[b % 2]])
                eng = "dve" if b % 2 == 0 else "act"
                dst = hnT[:, :, b * 128:(b + 1) * 128]
                src = v3(pt[:, :], 128)
                if eng == "dve":
                    S.op("dve", lambda e, dst=dst, src=src: e.tensor_copy(out=dst, in_=src),
                         reads=[PBH[b % 2]], writes=[B_hnT])
                else:
                    S.op("act", lambda e, dst=dst, src=src: e.activation(out=dst, in_=src, func=AF.Copy),
                         reads=[PBH[b % 2]], writes=[B_hnT])

        def proj_fm(Wd, c0, nchunks, consume, banks=(0, 1, 2, 3)):
            per = 2
            bi = 0
            for cb in range(0, nchunks, per):
                n = min(per, nchunks - cb)
                wi = load_wst(Wd, c0 + cb * 128, n * 128)
                for j in range(n):
                    bk = banks[bi % len(banks)]
                    bi += 1
                    for k in range(8):
                        S.op("pe", lambda e, k=k, j=j, wi=wi, bk=bk: e.matmul(
                            pbank[bk][:, :], lhsT=wst[wi][:, k, j * 128:(j + 1) * 128], rhs=hnT[:, k, :],
                            start=(k == 0), stop=(k == 7)),
                            reads=[B_wst[wi], B_hnT], writes=[PB[bk]])
                    consume(cb + j, pbank[bk], PB[bk])

        def resid_proj(lhs_fn, nk, w_fn, w_bufs, lhs_bufs, banks=(4, 5)):
            for b in range(NB):
                for half in range(2):
                    bk = banks[half]
                    for k in range(nk):
                        S.op("pe", lambda e, b=b, k=k, half=half, bk=bk: e.matmul(
                            pbank[bk][:, :], lhsT=lhs_fn(k, b), rhs=w_fn(k, half), start=(k == 0), stop=(k == nk - 1)),
                            reads=list(w_bufs) + list(lhs_bufs), writes=[PB[bk]])
                    hs = h_t[:, b, half * 512:(half + 1) * 512]
                    S.op("dve", lambda e, hs=hs, bk=bk: e.tensor_tensor(out=hs, in0=pbank[bk][:, :], in1=hs, op=ALU.add),
                         reads=[PB[bk], B_h[b]], writes=[B_h[b]])

        def conv_from_psum(items, KW, act_fix=False):
            for it in items:
                S.op("act", lambda e, it=it: e.activation(out=it["ac"], in_=it["ps"][:, 0:TT], func=AF.Copy, scale=it["w"](KW - 1)),
                     reads=[it["Bps"], B_const], writes=[it["Bac"]])
            if act_fix:
                for j in range(KW - 1):
                    s_ = KW - 1 - j
                    for col in range(s_):
                        for it in items:
                            S.op("act", lambda e, it=it, j=j, col=col: e.activation(
                                out=it["ac"][:, col:col + 1], in_=it["halo"][:, j + col:j + col + 1], func=AF.Identity,
                                scale=it["w"](j), bias=it["ac"][:, col:col + 1]),
                                reads=[it["Bhalo"], it["Bac"], B_const], writes=[it["Bac"]])
                for it in items:
                    S.op("act", lambda e, it=it: e.activation(out=it["halo"], in_=it["ps"][:, TT - (KW - 1):TT], func=AF.Copy),
                         reads=[it["Bps"]], writes=[it["Bhalo"]])
            for j in range(KW - 1):
                s_ = KW - 1 - j
                for it in items:
                    S.op("dve", lambda e, it=it, j=j, s_=s_: e.scalar_tensor_tensor(
                        out=it["ac"][:, s_:TT], in0=it["ps"][:, 0:TT - s_], scalar=it["w"](j), in1=it["ac"][:, s_:TT],
                        op0=ALU.mult, op1=ALU.add), reads=[it["Bps"], it["Bac"], B_const], writes=[it["Bac"]])
            if act_fix:
                return
            for j in range(KW - 1):
                s_ = KW - 1 - j
                for it in items:
                    S.op("dve", lambda e, it=it, j=j, s_=s_: e.scalar_tensor_tensor(
                        out=it["ac"][:, 0:s_], in0=it["halo"][:, j:KW - 1], scalar=it["w"](j), in1=it["ac"][:, 0:s_],
                        op0=ALU.mult, op1=ALU.add), reads=[it["Bhalo"], it["Bac"], B_const], writes=[it["Bac"]])
            for it in items:
                S.op("dve", lambda e, it=it: e.tensor_copy(out=it["halo"], in_=it["ps"][:, TT - (KW - 1):TT]),
                     reads=[it["Bps"]], writes=[it["Bhalo"]])

        def deltanet(ti):
            AFa.reset(); ABa.reset()
            norm_transpose()
            qkvT = v3(ABa.alloc(24 * TT), TT)
            zsT = v3(ABa.alloc(8 * TT), TT)
            ogT = v3(ABa.alloc(8 * TT), TT)
            B_qkv = [Buf("qkv%d" % c) for c in range(24)]
            B_zs = Buf("zs")
            B_og = [Buf("og%d" % b) for b in range(NB)]
            ba = v3(AFa.alloc(NB * 16), 16)
            B_ba = Buf("ba")
            pre = [AFa.alloc(TT + 3) for _ in range(3)]
            acc = [AFa.alloc(TT) for _ in range(3)]
            Bpre = [Buf("pre%d" % i) for i in range(3)]
            Bacc = [Buf("acc%d" % i) for i in range(3)]

            S.dma(lambda e: e.dma_start(out=wout_t[:], in_=Wa_out.rearrange("(k p) n -> p k n", p=128)), writes=[B_wout])

            def consume(c, ps, Bps):
                if c < 24:
                    i = c % 3
                    ac = acc[i]
                    conv_from_psum([dict(ps=ps, Bps=Bps, w=(lambda j, c=c: aconv_c[:, c, j:j + 1]), halo=halo_a[:, c, :],
                                         Bhalo=B_halo_a[c], ac=ac, Bac=Bacc[i])], 4)
                    S.op("act", lambda e: e.activation(out=qkvT[:, c, :], in_=ac, func=AF.Silu),
                         reads=[Bacc[i]], writes=[B_qkv[c]])
                else:
                    S.op("act", lambda e: e.activation(out=zsT[:, c - 24, :], in_=ps[:, :], func=AF.Silu),
                         reads=[Bps], writes=[B_zs])

            proj_fm(Wa_in, 0, 32, consume)
            if DBG <= 1:
                S.barrier(); return

            wba = ABa.alloc(8 * 16)
            wba3 = v3(wba, 16)
            B_wba = Buf("wba")
            S.dma(lambda e: e.dma_start(out=wba3, in_=Wa_ba.rearrange("(k p) n -> p k n", p=128)), writes=[B_wba])
            for b in range(NB):
                for k in range(8):
                    S.op("pe", lambda e, b=b, k=k: e.matmul(pbank[4][:, b * 16:(b + 1) * 16],
                                                            lhsT=hnT[:, k, b * 128:(b + 1) * 128], rhs=wba3[:, k, :],
                                                            start=(k == 0), stop=(k == 7)),
                         reads=[B_hnT, B_wba], writes=[PB[4]])
            S.op("dve", lambda e: e.tensor_copy(out=ba, in_=v3(pbank[4][:, 0:NB * 16], 16)), reads=[PB[4]], writes=[B_ba])

            if DBG <= 2:
                S.barrier(); return

            def hb(n, dt):
                return [(AFa if dt == F32 else ABa).alloc(n) for _ in range(4)]
            Kn = hb(128, BF16); kb = hb(128, BF16); kd = hb(128, BF16)
            Qn = hb(128, BF16); qd = hb(128, BF16); bv = hb(128, BF16)
            KQT = hb(384, BF16)
            qkTm = hb(128, BF16); TTb = hb(128, BF16); nwkT = hb(128, BF16); usb = hb(128, BF16)
            dg = hb(128, F32); tL = hb(128, F32); tU = hb(128, F32); EL = hb(128, F32); EU = hb(128, F32)
            PTa_r, Pa_r, PTb_r, Pb_r, Xa_r, Xb_r = [[neu_t[:, i_, kk_, :] for i_ in range(4)] for kk_ in range(6)]
            PPa_r = [neu_t[:, i_, 0:2, :] for i_ in range(4)]
            PPb_r = [neu_t[:, i_, 2:4, :] for i_ in range(4)]
            PTa = [t_.bitcast(F32) for t_ in PTa_r]
            names = "Kn kb kd Qn qd bv KQT qkTm TTb nwkT usb dg tL tU EL EU PTa PTb Pa Pb Xa Xb".split()
            BH = [{n: Buf(n + str(i)) for n in names} for i in range(4)]
            sq = v3(AFa.alloc(16 * 128), 128)
            B_sq = Buf("sq")
            on_all = v3(ABa.alloc(8 * 128), 128)
            B_on = Buf("on")
            junk2 = AFa.alloc(128)
            B_j2 = Buf("junk2")
            B_g = Buf("gate")
            B_sso = Buf("sso")
            NG = NB * 8
            Gall = AFa.alloc(14 * NG + NB * 16)

            def gq(i):
                return Gall[:, i * NG:(i + 1) * NG]
            (beta_a, nbeta_a, g_a, gcs_a, ngcs_a, Eg_a, Ekd_a, Egl_a, c_kb_a, c_kd_a, c_qn_a, c_qd_a, tmp1, tmp2) = [
                gq(i) for i in range(14)]
            rn_a = Gall[:, 14 * NG:14 * NG + NB * 16]
            rn3 = v3(rn_a, 16)
            rnq3, rnk3 = rn3[:, :, 0:8], rn3[:, :, 8:16]

            def q3(x):
                return v3(x, 8)
            sso = smalls[:, 64:72]
            rso = smalls[:, 72:80]
            scale_q = 128.0 ** -0.5
            gr, gw = [B_g], [B_g]
            dtb3 = dtb_bc[:].rearrange("p (o h) -> p o h", o=1).to_broadcast([128, NB, 8])
            negA3 = negA_bc[:].rearrange("p (o h) -> p o h", o=1).to_broadcast([128, NB, 8])
            S.op("act", lambda e: e.activation(out=q3(tmp1), in_=ba[:, :, 0:8], func=AF.Exp, scale=-1.0), reads=[B_ba], writes=gw)
            S.op("dve", lambda e: e.tensor_scalar(out=tmp1, in0=tmp1, scalar1=1.0, scalar2=None, op0=ALU.add), reads=gr, writes=gw)
            S.op("dve", lambda e: e.reciprocal(out=beta_a, in_=tmp1), reads=gr, writes=gw)
            S.op("dve", lambda e: e.tensor_scalar(out=nbeta_a, in0=beta_a, scalar1=-1.0, scalar2=None, op0=ALU.mult), reads=gr, writes=gw)
            S.op("dve", lambda e: e.tensor_tensor(out=q3(tmp2), in0=ba[:, :, 8:16], in1=dtb3, op=ALU.add),
                 reads=[B_ba, B_const] + gr, writes=gw)
            S.op("act", lambda e: e.activation(out=tmp2, in_=tmp2, func=AF.Exp), reads=gr, writes=gw)
            S.op("act", lambda e: e.activation(out=tmp2, in_=tmp2, func=AF.Ln, bias=1.0), reads=gr, writes=gw)
            S.op("dve", lambda e: e.tensor_tensor(out=q3(g_a), in0=q3(tmp2), in1=negA3, op=ALU.mult), reads=[B_const] + gr, writes=gw)
            for b in range(NB):
                S.op("pe", lambda e, b=b: e.matmul(pbank[5][:, b * 8:(b + 1) * 8], lhsT=triU[:], rhs=g_a[:, b * 8:(b + 1) * 8],
                                                   start=True, stop=True), reads=[B_const] + gr, writes=[PB[5]])
                S.op("pe", lambda e, b=b: e.matmul(pbank[5][:, NG + b * 8:NG + (b + 1) * 8], lhsT=onesf[:],
                                                   rhs=g_a[:, b * 8:(b + 1) * 8], start=True, stop=True),
                     reads=[B_const] + gr, writes=[PB[5]])
            for b in range(NB):
                blk = slice(b * 128, (b + 1) * 128)
                S.op("act", lambda e, blk=blk: e.activation(out=sq, in_=qkvT[:, 0:16, blk], func=AF.Square),
                     reads=B_qkv[0:16], writes=[B_sq])
                for c in range(16):
                    S.op("pe", lambda e, c=c, b=b: e.matmul(pbank[5][:, 2 * NG + b * 16 + c:2 * NG + b * 16 + c + 1], lhsT=sq[:, c, :],
                                                            rhs=onesf[:, 0:1], start=True, stop=True),
                         reads=[B_sq, B_const], writes=[PB[5]])
            psg, psl, pss = pbank[5][:, 0:NG], pbank[5][:, NG:2 * NG], pbank[5][:, 2 * NG:2 * NG + NB * 16]
            S.op("act", lambda e: e.activation(out=gcs_a, in_=psg, func=AF.Copy), reads=[PB[5]] + gr, writes=gw)
            S.op("act", lambda e: e.activation(out=ngcs_a, in_=psg, func=AF.Copy, scale=-1.0), reads=[PB[5]] + gr, writes=gw)
            S.op("act", lambda e: e.activation(out=Eg_a, in_=psg, func=AF.Exp), reads=[PB[5]] + gr, writes=gw)
            S.op("act", lambda e: e.activation(out=Egl_a, in_=psl, func=AF.Exp), reads=[PB[5]] + gr, writes=gw)
            S.op("act", lambda e: e.activation(out=rn_a, in_=pss, func=AF.Ln, bias=EPS), reads=[PB[5]] + gr, writes=gw)
            S.op("act", lambda e: e.activation(out=rn_a, in_=rn_a, func=AF.Exp, scale=-0.5), reads=gr, writes=gw)
            S.op("dve", lambda e: e.tensor_tensor(out=tmp1, in0=psl, in1=gcs_a, op=ALU.subtract), reads=[PB[5]] + gr, writes=gw)
            S.op("act", lambda e: e.activation(out=Ekd_a, in_=tmp1, func=AF.Exp), reads=gr, writes=gw)
            S.op("dve", lambda e: e.tensor_tensor(out=q3(tmp2), in0=rnk3, in1=q3(beta_a), op=ALU.mult), reads=gr, writes=gw)
            S.op("dve", lambda e: e.tensor_tensor(out=c_kb_a, in0=tmp2, in1=Eg_a, op=ALU.mult), reads=gr, writes=gw)
            S.op("dve", lambda e: e.tensor_tensor(out=q3(c_kd_a), in0=rnk3, in1=q3(Ekd_a), op=ALU.mult), reads=gr, writes=gw)
            S.op("dve", lambda e: e.tensor_scalar(out=q3(c_qn_a), in0=rnq3, scalar1=scale_q, scalar2=None, op0=ALU.mult),
                 reads=gr, writes=gw)
            S.op("dve", lambda e: e.tensor_tensor(out=c_qd_a, in0=c_qn_a, in1=Eg_a, op=ALU.mult), reads=gr, writes=gw)

            def do_block(b):
                blk = slice(b * 128, (b + 1) * 128)
                beta_, nbeta, gcs, ngcs, Egl, c_kb, c_kd, c_qn, c_qd = [x[:, b * 8:(b + 1) * 8] for x in (
                    beta_a, nbeta_a, gcs_a, ngcs_a, Egl_a, c_kb_a, c_kd_a, c_qn_a, c_qd_a)]
                rnk = rn_a[:, b * 16 + 8:b * 16 + 16]
                if DBG <= 3:
                    S.barrier(); return
                for grp in range(2):
                    heads = [grp * 4 + i for i in range(4)]
                    ob = 4 + grp
                    for i, hh in enumerate(heads):
                        S.op("act", lambda e, i=i, hh=hh: e.activation(out=dg[i], in_=identf[:], func=AF.Copy, scale=gcs[:, hh:hh + 1]),
                             reads=[B_const] + gr, writes=[BH[i]["dg"]])
                    for i, hh in enumerate(heads):
                        pt = pbankh[i // 2]
                        for j, c in enumerate((8 + hh, hh, 16 + hh)):
                            o_ = ((i % 2) * 3 + j) * 128
                            S.op("pe", lambda e, pt=pt, o_=o_, c=c, blk=blk: e.transpose(
                                out=pt[:, o_:o_ + 128], in_=qkvT[:, c, blk], identity=identb[:]),
                                reads=[B_qkv[c], B_const], writes=[PBH[i // 2]])
                    if DBG <= 3.3:
                        S.barrier(); return
                    for i, hh in enumerate(heads):
                        pt = pbankh[i // 2]
                        o_ = (i % 2) * 384
                        Kt, Qt, Vt = pt[:, o_:o_ + 128], pt[:, o_ + 128:o_ + 256], pt[:, o_ + 256:o_ + 384]
                        B = BH[i]
                        beng = "act" if i // 2 == 0 else "dve"
                        for (dst, src, sc, nm, eng) in ((Kn[i], Kt, rnk, "Kn", beng), (kb[i], Kt, c_kb, "kb", beng),
                                                        (kd[i], Kt, c_kd, "kd", beng), (Qn[i], Qt, c_qn, "Qn", beng),
                                                        (qd[i], Qt, c_qd, "qd", beng), (bv[i], Vt, beta_, "bv", beng)):
                            sc1 = sc[:, hh:hh + 1]
                            if eng == "act":
                                S.op("act", lambda e, dst=dst, src=src, sc1=sc1: e.activation(out=dst, in_=src, func=AF.Copy,
                                                                                             scale=sc1),
                                     reads=[PBH[i // 2]] + gr, writes=[B[nm]])
                            else:
                                S.op("dve", lambda e, dst=dst, src=src, sc1=sc1: e.tensor_scalar(
                                    out=dst, in0=src, scalar1=sc1, scalar2=None, op0=ALU.mult),
                                    reads=[PBH[i // 2]] + gr, writes=[B[nm]])
                    if DBG <= 3.6:
                        S.barrier(); return
                    for i, hh in enumerate(heads):
                        pt = pbankh[i // 2]
                        B = BH[i]
                        for j, (src, nm) in enumerate(((Kn[i], "Kn"), (Qn[i], "Qn"), (qd[i], "qd"))):
                            o_ = ((i % 2) * 3 + j) * 128
                            S.op("pe", lambda e, pt=pt, o_=o_, src=src: e.transpose(out=pt[:, o_:o_ + 128], in_=src,
                                                                                    identity=identb[:]),
                                 reads=[B[nm], B_const], writes=[PBH[i // 2]])
                    for i, hh in enumerate(heads):
                        pt = pbankh[i // 2]
                        o_ = (i % 2) * 384
                        S.op("dve" if i // 2 == 1 else "act",
                             (lambda e, i=i, pt=pt, o_=o_: e.tensor_copy(out=KQT[i], in_=pt[:, o_:o_ + 384])) if i // 2 == 1 else
                             (lambda e, i=i, pt=pt, o_=o_: e.activation(out=KQT[i], in_=pt[:, o_:o_ + 384], func=AF.Copy)),
                             reads=[PBH[i // 2]], writes=[BH[i]["KQT"]])
                    if DBG <= 4:
                        S.barrier(); return
                    for i, hh in enumerate(heads):
                        B = BH[i]
                        KnT, QnT = KQT[i][:, 0:128], KQT[i][:, 128:256]
                        S.op("pe", lambda e, i=i, KnT=KnT: e.matmul(pbank[i][:, 0:128], lhsT=KnT, rhs=KnT, start=True, stop=True),
                             reads=[B["KQT"]], writes=[PB[i]])
                        S.op("pe", lambda e, i=i, KnT=KnT, QnT=QnT: e.matmul(pbank[i][:, 128:256], lhsT=KnT, rhs=QnT,
                                                                           start=True, stop=True),
                             reads=[B["KQT"]], writes=[PB[i]])
                        S.op("pe", lambda e, i=i: e.matmul(pbank[i][:, 256:384], lhsT=onesf[:], rhs=dg[i], start=True, stop=True),
                             reads=[B["dg"], B_const], writes=[PB[i]])
                    for i, hh in enumerate(heads):
                        B = BH[i]
                        S.op("dve", lambda e, i=i: e.tensor_tensor(out=tL[i], in0=pbank[i][:, 256:384], in1=mposS[:], op=ALU.add),
                             reads=[PB[i], B_const], writes=[B["tL"]])
                        S.op("dve", lambda e, i=i: e.tensor_tensor(out=tU[i], in0=pbank[i][:, 256:384], in1=mnegI[:], op=ALU.add),
                             reads=[PB[i], B_const], writes=[B["tU"]])
                        S.op("act", lambda e, i=i, hh=hh: e.activation(out=EL[i], in_=tL[i], func=AF.Exp, scale=-1.0,
                                                                       bias=gcs[:, hh:hh + 1]),
                             reads=[B["tL"]] + gr, writes=[B["EL"]])
                        S.op("act", lambda e, i=i, hh=hh: e.activation(out=EU[i], in_=tU[i], func=AF.Exp,
                                                                       bias=ngcs[:, hh:hh + 1]),
                             reads=[B["tU"]] + gr, writes=[B["EU"]])
                    for i, hh in enumerate(heads):
                        B = BH[i]
                        S.op("dve", lambda e, i=i, hh=hh: e.scalar_tensor_tensor(
                            out=PTa_r[i], in0=pbank[i][:, 0:128], scalar=nbeta[:, hh:hh + 1], in1=EL[i],
                            op0=ALU.mult, op1=ALU.mult), reads=[PB[i], B["EL"]] + gr, writes=[B["PTa"]])
                        S.op("dve", lambda e, i=i: e.tensor_tensor(out=qkTm[i], in0=pbank[i][:, 128:256], in1=EU[i], op=ALU.mult),
                             reads=[PB[i], B["EU"]], writes=[B["qkTm"]])
                    if DBG <= 5:
                        S.barrier(); return
                    for i, hh in enumerate(heads):
                        B = BH[i]
                        S.op("pe", lambda e, i=i: e.transpose(out=pbank[i][:, 0:128], in_=PTa[i], identity=identf[:]),
                             reads=[B["PTa"], B_const], writes=[PB[i]])
                        S.op("dve", lambda e, i=i: e.tensor_copy(out=Pa_r[i], in_=pbank[i][:, 0:128]),
                             reads=[PB[i]], writes=[B["Pa"]])
                        S.op("dve", lambda e, i=i: e.tensor_tensor(out=Xa_r[i], in0=pbank[i][:, 0:128], in1=identf[:], op=ALU.add),
                             reads=[PB[i], B_const], writes=[B["Xa"]])
                    cur = {"P": (Pa_r, "Pa"), "PT": (PTa_r, "PTa"), "X": (Xa_r, "Xa"), "PP": PPa_r}
                    alt = {"P": (Pb_r, "Pb"), "PT": (PTb_r, "PTb"), "X": (Xb_r, "Xb"), "PP": PPb_r}
                    for kk_ in range(1, 5):
                        last = (kk_ == 4)
                        (Pc, Pcn), (PTc, PTcn), (Xc, Xcn) = cur["P"], cur["PT"], cur["X"]
                        (Pn, Pnn), (PTn, PTnn), (Xn, Xnn) = alt["P"], alt["PT"], alt["X"]
                        PPn = alt["PP"]
                        for i, hh in enumerate(heads):
                            B = BH[i]
                            S.op("pe", lambda e, i=i, Pc=Pc, PTc=PTc: e.matmul(pbank[i][:, 0:128], lhsT=Pc[i], rhs=PTc[i],
                                                                             start=True, stop=True),
                                 reads=[B[Pcn], B[PTcn]], writes=[PB[i]])
                            if not last:
                                S.op("pe", lambda e, i=i, Pc=Pc, PTc=PTc: e.matmul(pbank[i][:, 128:256], lhsT=PTc[i], rhs=Pc[i],
                                                                                 start=True, stop=True),
                                     reads=[B[Pcn], B[PTcn]], writes=[PB[i]])
                        for i, hh in enumerate(heads):
                            B = BH[i]
                            if not last:
                                S.op("dve", lambda e, i=i, PPn=PPn: e.tensor_copy(out=PPn[i], in_=v3(pbank[i][:, 0:256], 128)),
                                     reads=[PB[i]], writes=[B[PTnn], B[Pnn]])
                            else:
                                S.op("dve", lambda e, i=i, PTn=PTn: e.tensor_copy(out=PTn[i], in_=pbank[i][:, 0:128]),
                                     reads=[PB[i]], writes=[B[PTnn]])
                        for i, hh in enumerate(heads):
                            B = BH[i]
                            S.op("pe", lambda e, i=i, PTn=PTn, Xc=Xc, ob=ob: e.matmul(pbank[ob][:, i * 128:(i + 1) * 128], lhsT=PTn[i],
                                                                                    rhs=Xc[i], start=True, stop=True),
                                 reads=[B[PTnn], B[Xcn]], writes=[PB[ob]])
                        for i, hh in enumerate(heads):
                            B = BH[i]
                            if not last:
                                S.op("dve", lambda e, i=i, Xn=Xn, Xc=Xc, ob=ob: e.tensor_tensor(
                                    out=Xn[i], in0=pbank[ob][:, i * 128:(i + 1) * 128], in1=Xc[i].bitcast(F32), op=ALU.add),
                                    reads=[PB[ob], B[Xcn]], writes=[B[Xnn]])
                            else:
                                S.op("dve", lambda e, i=i, Xc=Xc, ob=ob: e.tensor_tensor(
                                    out=TTb[i], in0=pbank[ob][:, i * 128:(i + 1) * 128], in1=Xc[i].bitcast(F32), op=ALU.add),
                                    reads=[PB[ob], B[Xcn]], writes=[B["TTb"]])
                        cur, alt = alt, cur
                    if DBG <= 6:
                        S.barrier(); return
                    for i, hh in enumerate(heads):
                        B = BH[i]
                        S.op("pe", lambda e, i=i: e.matmul(pbank[i][:, 0:128], lhsT=kb[i], rhs=TTb[i], start=True, stop=True),
                             reads=[B["kb"], B["TTb"]], writes=[PB[i]])
                        S.op("dve", lambda e, i=i: e.tensor_scalar(out=nwkT[i], in0=pbank[i][:, 0:128], scalar1=-1.0, scalar2=None,
                                                                    op0=ALU.mult),
                             reads=[PB[i]], writes=[B["nwkT"]])
                    for i, hh in enumerate(heads):
                        B = BH[i]
                        S.op("pe", lambda e, i=i: e.matmul(pbank[i][:, 128:256], lhsT=TTb[i], rhs=bv[i], start=True, stop=False),
                             reads=[B["TTb"], B["bv"]], writes=[PB[i]])
                        S.op("pe", lambda e, i=i, hh=hh: e.matmul(pbank[i][:, 128:256], lhsT=nwkT[i], rhs=Sb_st[:, hh, :],
                                                                 start=False, stop=True),
                             reads=[B["nwkT"], B_Sb[hh]], writes=[PB[i]])
                        S.op("dve", lambda e, i=i: e.tensor_copy(out=usb[i], in_=pbank[i][:, 128:256]),
                             reads=[PB[i]], writes=[B["usb"]])
                    for i, hh in enumerate(heads):
                        B = BH[i]
                        qdT = KQT[i][:, 256:384]
                        oc = slice(i * 128, (i + 1) * 128)
                        S.op("pe", lambda e, qdT=qdT, hh=hh, oc=oc, ob=ob: e.matmul(pbank[ob][:, oc], lhsT=qdT, rhs=Sb_st[:, hh, :],
                                                                                  start=True, stop=False),
                             reads=[B["KQT"], B_Sb[hh]], writes=[PB[ob]])
                        S.op("pe", lambda e, i=i, oc=oc, ob=ob: e.matmul(pbank[ob][:, oc], lhsT=qkTm[i], rhs=usb[i],
                                                                       start=False, stop=True),
                             reads=[B["qkTm"], B["usb"]], writes=[PB[ob]])
                        S.op("pe", lambda e, i=i: e.matmul(pbank[i][:, 256:384], lhsT=kd[i], rhs=usb[i], start=True, stop=True),
                             reads=[B["kd"], B["usb"]], writes=[PB[i]])
                        S.op("dve", lambda e, i=i, hh=hh: e.scalar_tensor_tensor(
                            out=S_st[:, hh, :], in0=S_st[:, hh, :], scalar=Egl[:, hh:hh + 1], in1=pbank[i][:, 256:384],
                            op0=ALU.mult, op1=ALU.add), reads=[PB[i], B_S[hh]] + gr, writes=[B_S[hh]])
                        S.op("act", lambda e, hh=hh: e.activation(out=Sb_st[:, hh, :], in_=S_st[:, hh, :], func=AF.Copy),
                             reads=[B_S[hh]], writes=[B_Sb[hh]])
                    for i, hh in enumerate(heads):
                        oc = slice(i * 128, (i + 1) * 128)
                        S.op("act", lambda e, oc=oc, ob=ob, hh=hh: e.activation(out=junk2, in_=pbank[ob][:, oc], func=AF.Square,
                                                                               accum_out=sso[:, hh:hh + 1]),
                             reads=[PB[ob]], writes=[B_j2, B_sso])
                if DBG <= 7:
                    S.barrier(); return
                S.op("act", lambda e: e.activation(out=rso, in_=sso, func=AF.Ln, scale=1.0 / 128, bias=EPS), reads=[B_sso], writes=[B_sso])
                S.op("act", lambda e: e.activation(out=rso, in_=rso, func=AF.Exp, scale=-0.5), reads=[B_sso], writes=[B_sso])
                for grp in range(2):
                    ob = 4 + grp
                    S.op("dve", lambda e, grp=grp, ob=ob: e.tensor_tensor(
                        out=on_all[:, grp * 4:(grp + 1) * 4, :], in0=v3(pbank[ob][:, :], 128),
                        in1=bc3(rso[:, grp * 4:(grp + 1) * 4], 128), op=ALU.mult),
                        reads=[PB[ob], B_sso], writes=[B_on])
                for hh in range(8):
                    S.op("pe", lambda e, hh=hh: e.transpose(out=pbankh[0][:, hh * 128:(hh + 1) * 128], in_=on_all[:, hh, :],
                                                            identity=identb[:]),
                         reads=[B_on, B_const], writes=[PBH[0]])
                S.op("dve", lambda e, blk=blk: e.tensor_tensor(out=ogT[:, :, blk], in0=v3(pbankh[0][:, :], 128),
                                                               in1=zsT[:, :, blk], op=ALU.mult),
                     reads=[PBH[0], B_zs], writes=[B_og[b]])
            for b in range(NB):
                do_block(b)
            resid_proj(lambda k, b: ogT[:, k, b * 128:(b + 1) * 128], 8,
                       lambda k, half: wout_t[:, k, half * 512:(half + 1) * 512], [B_wout], B_og)
            S.barrier()

        def conv_ffn(l):
            AFa.reset(); ABa.reset()
            norm_transpose()
            aT = v3(ABa.alloc(NFF * TT), TT)
            B_aT = [Buf("aT%d" % c) for c in range(NFF)]
            wdn = v3(ABa.alloc(NFF * 512), 512)
            B_wdn = Buf("wdn")

            def load_wdn(half):
                for (k0, k1) in ((0, 11), (11, 22)):
                    S.dma(lambda e, k0=k0, k1=k1: e.dma_start(
                        out=wdn[:, k0:k1, :],
                        in_=Wf_dn[l][k0 * 128:k1 * 128, half * 512:(half + 1) * 512].rearrange("(k p) n -> p k n", p=128)),
                        writes=[B_wdn])
            load_wdn(0)
            npre = 8
            acc = [AFa.alloc(TT) for _ in range(npre)]
            sg = [AFa.alloc(TT) for _ in range(4)]
            Bacc = [Buf("facc%d" % i) for i in range(npre)]
            Bsg = [Buf("sg%d" % i) for i in range(4)]
            cnt = [0]
            per = 2
            bi = 0
            for cb in range(0, NFF, per):
                n = min(per, NFF - cb)
                wg = load_wst(Wf_up[l], cb * 128, n * 128)
                wv = load_wst(Wf_up[l], DFF + cb * 128, n * 128)
                for j in range(n):
                    c = cb + j
                    items = []
                    for (wi, cg) in ((wg, c), (wv, NFF + c)):
                        bk = bi % 4
                        bi += 1
                        for k in range(8):
                            S.op("pe", lambda e, k=k, j=j, wi=wi, bk=bk: e.matmul(
                                pbank[bk][:, :], lhsT=wst[wi][:, k, j * 128:(j + 1) * 128], rhs=hnT[:, k, :],
                                start=(k == 0), stop=(k == 7)), reads=[B_wst[wi], B_hnT], writes=[PB[bk]])
                        i = cnt[0] % npre
                        cnt[0] += 1
                        items.append(dict(ps=pbank[bk], Bps=PB[bk], w=(lambda jj, cg=cg: fconv_c[:, l, cg, jj:jj + 1]),
                                          halo=halo_f[:, l, cg, :], Bhalo=B_halo_f[l][cg], ac=acc[i], Bac=Bacc[i]))
                    conv_from_psum(items, 3)
                    ag, Bag, av, Bav = items[0]["ac"], items[0]["Bac"], items[1]["ac"], items[1]["Bac"]
                    s_ = sg[c % 4]
                    S.op("act", lambda e, ag=ag, s_=s_: e.activation(out=s_, in_=ag, func=AF.Silu), reads=[Bag], writes=[Bsg[c % 4]])
                    S.op("pool", lambda e, c=c, av=av, s_=s_: e.tensor_tensor(out=aT[:, c, :], in0=s_, in1=av, op=ALU.mult),
                         reads=[Bsg[c % 4], Bav], writes=[B_aT[c]])
            for half in range(2):
                if half == 1:
                    load_wdn(1)
                for b in range(NB):
                    bk = 4 + (b % 2)
                    for k in range(NFF):
                        S.op("pe", lambda e, b=b, k=k, bk=bk: e.matmul(
                            pbank[bk][:, :], lhsT=aT[:, k, b * 128:(b + 1) * 128], rhs=wdn[:, k, :],
                            start=(k == 0), stop=(k == NFF - 1)), reads=[B_wdn, B_aT[k]], writes=[PB[bk]])
                    hs = h_t[:, b, half * 512:(half + 1) * 512]
                    S.op("dve", lambda e, hs=hs, bk=bk: e.tensor_tensor(out=hs, in0=pbank[bk][:, :], in1=hs, op=ALU.add),
                         reads=[PB[bk], B_h[b]], writes=[B_h[b]])
            S.barrier()

        def ple(l, ti):
            AFa.reset(); ABa.reset()
            norm_transpose()
            wpr = v3(ABa.alloc(2 * D), D)
            B_wpr = Buf("wpr")
            S.dma(lambda e: e.dma_start(out=wpr, in_=Wp_pr[l].rearrange("(k p) n -> p k n", p=128)), writes=[B_wpr])
            S.dma(lambda e: e.dma_start(out=wout_t[:], in_=Wp_gt[l].rearrange("(k p) n -> p k n", p=128)), writes=[B_wout])
            pin = [AFa.alloc(256) for _ in range(2)]
            pT = [ABa.alloc(256) for _ in range(2)]
            gate = [AFa.alloc(512) for _ in range(2)]
            Bpin = [Buf("pin0"), Buf("pin1")]
            BpT = [Buf("pT0"), Buf("pT1")]
            Bgate = [Buf("gate0"), Buf("gate1")]
            gi = 0
            for b in range(NB):
                i = b % 2
                t0 = ti * TT + b * 128
                S.dma(lambda e, i=i, t0=t0: e.dma_start(out=pin[i], in_=p_d[l, t0:t0 + 128, :]), writes=[Bpin[i]])
                for k in range(2):
                    S.op("pe", lambda e, i=i, k=k: e.transpose(out=pbank[4][:, k * 128:(k + 1) * 128],
                                                               in_=pin[i][:, k * 128:(k + 1) * 128], identity=identf[:]),
                         reads=[Bpin[i], B_const], writes=[PB[4]])
                S.op("act", lambda e, i=i: e.activation(out=pT[i], in_=pbank[4][:, 0:256], func=AF.Copy), reads=[PB[4]], writes=[BpT[i]])
                for half in range(2):
                    hs = slice(half * 512, (half + 1) * 512)
                    gbk = half
                    pbk = 2 + half
                    for k in range(8):
                        S.op("pe", lambda e, b=b, k=k, hs=hs, gbk=gbk: e.matmul(
                            pbank[gbk][:, :], lhsT=hnT[:, k, b * 128:(b + 1) * 128], rhs=wout_t[:, k, hs],
                            start=(k == 0), stop=(k == 7)), reads=[B_hnT, B_wout], writes=[PB[gbk]])
                    for k in range(2):
                        S.op("pe", lambda e, i=i, k=k, hs=hs, pbk=pbk: e.matmul(
                            pbank[pbk][:, :], lhsT=pT[i][:, k * 128:(k + 1) * 128], rhs=wpr[:, k, hs],
                            start=(k == 0), stop=(k == 1)), reads=[BpT[i], B_wpr], writes=[PB[pbk]])
                    gt = gate[gi % 2]
                    Bg = Bgate[gi % 2]
                    gi += 1
                    S.op("act", lambda e, gt=gt, gbk=gbk: e.activation(out=gt, in_=pbank[gbk][:, :], func=AF.Sigmoid),
                         reads=[PB[gbk]], writes=[Bg])
                    S.op("dve", lambda e, gt=gt, pbk=pbk: e.tensor_tensor(out=gt, in0=pbank[pbk][:, :], in1=gt, op=ALU.mult),
                         reads=[PB[pbk], Bg], writes=[Bg])
                    S.op("dve", lambda e, gt=gt, b=b, hs=hs: e.tensor_tensor(out=h_t[:, b, hs], in0=h_t[:, b, hs], in1=gt, op=ALU.add),
                         reads=[Bg, B_h[b]], writes=[B_h[b]])
            S.barrier()

        def swa(ti):
            AFa.reset(); ABa.reset()
            norm_transpose()
            qT = v3(ABa.alloc(8 * TT), TT)
            kT0 = v3(ABa.alloc(4 * (TT + 128)), TT + 128)
            kT1 = v3(ABa.alloc(4 * (TT + 128)), TT + 128)
            kTz = (kT0, kT1)
            vx = ABa.alloc((NB + 1) * 4 * 72).rearrange("p (b g d) -> p b g d", g=4, d=72)
            aoT = v3(ABa.alloc(8 * TT), TT)
            B_qT = [Buf("qT%d" % c) for c in range(8)]
            B_kT = Buf("kT")
            B_vx = [Buf("vx%d" % b) for b in range(NB + 1)]
            B_ao = [Buf("ao%d" % b) for b in range(NB)]
            S.dma(lambda e: e.dma_start(out=wout_t[:], in_=Wb_out.rearrange("(k p) n -> p k n", p=128)), writes=[B_wout])
            S.op("pool", lambda e: e.memset(kT0, 0.0), writes=[B_kT])
            S.op("pool", lambda e: e.memset(kT1, 0.0), writes=[B_kT])
            S.op("pool", lambda e: e.tensor_copy(out=kT0[:, :, 0:128], in_=kT_prev[:, 0, :, :]), reads=[B_kprev], writes=[B_kT])
            S.op("pool", lambda e: e.tensor_copy(out=kT1[:, :, 0:128], in_=kT_prev[:, 1, :, :]), reads=[B_kprev], writes=[B_kT])
            S.op("pool", lambda e: e.tensor_copy(out=vx[:, 0, :, :], in_=vx_prev[:]), reads=[B_vprev], writes=[B_vx[0]])

            def consume(c, ps, Bps):
                if c < 8:
                    S.op("act", lambda e: e.activation(out=qT[:, c, :], in_=ps[:, :], func=AF.Copy), reads=[Bps], writes=[B_qT[c]])
                else:
                    g = c - 8
                    S.op("dve", lambda e: e.tensor_copy(out=kT0[0:64, g, 128:128 + TT], in_=ps[0:64, :]), reads=[Bps], writes=[B_kT])
                    S.op("dve", lambda e: e.tensor_copy(out=kT1[64:128, g, 128:128 + TT], in_=ps[64:128, :]), reads=[Bps], writes=[B_kT])
            proj_fm(Wb_in, 0, 12, consume)
            wv = ABa.alloc(8 * 256)
            wv3 = v3(wv, 256)
            B_wv = Buf("wv")
            S.dma(lambda e: e.dma_start(out=wv3, in_=Wb_in[:, 1536:1792].rearrange("(k p) n -> p k n", p=128)), writes=[B_wv])
            for b in range(NB):
                for k in range(8):
                    S.op("pe", lambda e, b=b, k=k: e.matmul(pbank[4][:, 0:256], lhsT=hnT[:, k, b * 128:(b + 1) * 128],
                                                            rhs=wv3[:, k, :], start=(k == 0), stop=(k == 7)),
                         reads=[B_hnT, B_wv], writes=[PB[4]])
                S.op("pool", lambda e, b=b: e.memset(vx[:, b + 1, :, :], 1.0), writes=[B_vx[b + 1]])
                S.op("act", lambda e, b=b: e.activation(out=vx[:, b + 1, :, 0:64], in_=v3(pbank[4][:, 0:256], 64), func=AF.Copy),
                     reads=[PB[4]], writes=[B_vx[b + 1]])
            S.op("pool", lambda e: e.tensor_copy(out=kT_prev[:, 0, :, :], in_=kT0[:, :, TT:TT + 128]), reads=[B_kT], writes=[B_kprev])
            S.op("pool", lambda e: e.tensor_copy(out=kT_prev[:, 1, :, :], in_=kT1[:, :, TT:TT + 128]), reads=[B_kT], writes=[B_kprev])
            S.op("pool", lambda e: e.tensor_copy(out=vx_prev[:], in_=vx[:, NB, :, :]), reads=[B_vx[NB]], writes=[B_vprev])

            tmp = [AFa.alloc(512) for _ in range(4)]
            ee = [ABa.alloc(512) for _ in range(4)]
            Btmp = [Buf("stmp%d" % i) for i in range(4)]
            Bee = [Buf("see%d" % i) for i in range(4)]
            ao = [ABa.alloc(D) for _ in range(2)]
            Bao_t = [Buf("aot0"), Buf("aot1")]
            den = smalls[:, 300:332]
            B_den = Buf("den")
            ri = [0]

            def swa_scores(b, g):
                first = (mode == "fused" and ti == 0 and b == 0)
                use_flag = (mode == "tile" and b == 0)
                parts = ([] if first else [0]) + [1]
                ebufs = {}
                for pc in parts:
                    bk = (2 * g + pc) % 4
                    r = ri[0] % 4
                    ri[0] += 1
                    kofs = b * 128 + pc * 128
                    for hh in range(4):
                        head = 4 * g + hh
                        c, half = head // 2, head % 2
                        kz = kTz[half]
                        S.op("pe", lambda e, bk=bk, kz=kz, g=g, kofs=kofs, c=c, b=b, hh=hh: e.matmul(
                            pbank[bk][:, hh * 128:(hh + 1) * 128], lhsT=kz[:, g, kofs:kofs + 128],
                            rhs=qT[:, c, b * 128:(b + 1) * 128], start=True, stop=True),
                            reads=[B_kT, B_qT[c]], writes=[PB[bk]])
                    S.op("dve", lambda e, bk=bk, r=r, g=g, pc=pc: e.scalar_tensor_tensor(
                        out=tmp[r], in0=pbank[bk][:, :], scalar=0.125, in1=swab[:, g, pc, :], op0=ALU.mult, op1=ALU.add),
                        reads=[PB[bk], B_const], writes=[Btmp[r]])
                    if use_flag and pc == 0:
                        S.op("act", lambda e, r=r: e.activation(out=ee[r], in_=tmp[r], func=AF.Exp, bias=flag_t[:, 0:1]),
                             reads=[Btmp[r], B_const], writes=[Bee[r]])
                    else:
                        S.op("act", lambda e, r=r: e.activation(out=ee[r], in_=tmp[r], func=AF.Exp), reads=[Btmp[r]], writes=[Bee[r]])
                    ebufs[pc] = r
                return parts, ebufs

            def swa_pv(b, g, parts, ebufs):
                ao_b = ao[b % 2]
                ps_o = pbank[4 + (g % 2)]
                Bpo = PB[4 + (g % 2)]
                for hh in range(4):
                    for n_, pc in enumerate(parts):
                        r = ebufs[pc]
                        S.op("pe", lambda e, ps_o=ps_o, hh=hh, r=r, b=b, pc=pc, g=g, n_=n_, np_=len(parts): e.matmul(
                            ps_o[:, hh * 72:hh * 72 + 66], lhsT=ee[r][:, hh * 128:(hh + 1) * 128], rhs=vx[:, b + pc, g, 0:66],
                            start=(n_ == 0), stop=(n_ == np_ - 1)),
                            reads=[Bee[r], B_vx[b + pc]], writes=[Bpo])
                dn = den[:, g * 4:(g + 1) * 4]
                po3 = ps_o[:, 0:288].rearrange("p (h d) -> p h d", d=72)
                S.op("dve", lambda e, dn=dn, po3=po3, g=g: e.tensor_tensor(
                    out=dn, in0=po3[:, :, 64], in1=esink_bc[:, g * 4:(g + 1) * 4], op=ALU.add),
                    reads=[Bpo, B_const], writes=[B_den])
                S.op("dve", lambda e, dn=dn: e.reciprocal(out=dn, in_=dn), reads=[B_den], writes=[B_den])
                S.op("dve", lambda e, dn=dn, po3=po3, g=g, ao_b=ao_b: e.tensor_tensor(
                    out=ao_b[:, g * 256:(g + 1) * 256].rearrange("p (h d) -> p h d", d=64), in0=po3[:, :, 0:64],
                    in1=bc3(dn, 64), op=ALU.mult),
                    reads=[Bpo, B_den], writes=[Bao_t[b % 2]])

            def swa_finish(b):
                ao_b = ao[b % 2]
                for k in range(8):
                    S.op("pe", lambda e, k=k, ao_b=ao_b: e.transpose(out=pbankh[1][:, k * 128:(k + 1) * 128],
                                                                     in_=ao_b[:, k * 128:(k + 1) * 128], identity=identb[:]),
                         reads=[Bao_t[b % 2], B_const], writes=[PBH[1]])
                S.op("act", lambda e, b=b: e.activation(out=aoT[:, :, b * 128:(b + 1) * 128], in_=v3(pbankh[1][:, :], 128), func=AF.Copy),
                     reads=[PBH[1]], writes=[B_ao[b]])

            items_ = [(b, g) for b in range(NB) for g in range(4)]
            pend = swa_scores(*items_[0])
            for k_, (b, g) in enumerate(items_):
                nxt = swa_scores(*items_[k_ + 1]) if k_ + 1 < len(items_) else None
                swa_pv(b, g, *pend)
                if g == 3:
                    swa_finish(b)
                pend = nxt
            resid_proj(lambda k, b: aoT[:, k, b * 128:(b + 1) * 128], 8,
                       lambda k, half: wout_t[:, k, half * 512:(half + 1) * 512], [B_wout], B_ao, banks=(0, 1))
            S.barrier()

        def final_out(ti, do_norm=True):
            AFa.reset(); ABa.reset()
            ss = smalls[:, 0:NB]
            rstd = smalls[:, 8:8 + NB]
            Bss = Buf("fss")
            junk = ABa.alloc(D)
            Bj = Buf("fjunk")
            ot = [AFa.alloc(D) for _ in range(2)]
            Bot = [Buf("ot0"), Buf("ot1")]
            if do_norm:
                for b in range(NB):
                    S.op("act", lambda e, b=b: e.activation(out=junk, in_=h_t[:, b, :], func=AF.Square, accum_out=ss[:, b:b + 1]),
                         reads=[B_h[b]], writes=[Bj, Bss])
                S.op("act", lambda e: e.activation(out=rstd, in_=ss, func=AF.Ln, scale=1.0 / D, bias=EPS), reads=[Bss], writes=[Bss])
                S.op("act", lambda e: e.activation(out=rstd, in_=rstd, func=AF.Exp, scale=-0.5), reads=[Bss], writes=[Bss])
            for b in range(NB):
                t0 = ti * TT + b * 128
                if do_norm:
                    o_ = ot[b % 2]
                    S.op("dve", lambda e, b=b, o_=o_: e.scalar_tensor_tensor(out=o_, in0=h_t[:, b, :], scalar=rstd[:, b:b + 1],
                                                                            in1=nfin_bc[:], op0=ALU.mult, op1=ALU.mult),
                         reads=[B_h[b], Bss, B_const], writes=[Bot[b % 2]])
                    S.dma(lambda e, o_=o_, t0=t0: e.dma_start(out=out_d[t0:t0 + 128, :], in_=o_), reads=[Bot[b % 2]], final=True)
                else:
                    S.dma(lambda e, b=b, t0=t0: e.dma_start(out=out_d[t0:t0 + 128, :], in_=h_t[:, b, :]), reads=[B_h[b]], final=True)
            S.barrier()

        const_setup()
        S.barrier()
        if mode != "tile":
            weight_prep()
        if mode != "prep":
            stages = [s_ for s_ in ["delta", "ffn0", "ple0", "swa", "ffn1", "ple1"] if s_ in STAGES]
            for ti in range(n_tiles):
                for b in range(NB):
                    t0 = ti * TT + b * 128
                    S.dma(lambda e, b=b, t0=t0: e.dma_start(out=h_t[:, b, :], in_=x_d[t0:t0 + 128, :]), writes=[B_h[b]])
                done = False
                for sname in stages:
                    if stop_after == "prep":
                        done = True
                        break
                    if sname == "delta":
                        deltanet(ti)
                    elif sname == "ffn0":
                        conv_ffn(0)
                    elif sname == "ple0":
                        ple(0, ti)
                    elif sname == "swa":
                        swa(ti)
                    elif sname == "ffn1":
                        conv_ffn(1)
                    elif sname == "ple1":
                        ple(1, ti)
                    if stop_after == sname:
                        done = True
                        break
                final_out(ti, do_norm=not done)
            if mode == "tile":
                S.barrier()
                for (t_, dst_) in ((halo_a, ha_out), (halo_f, hf_out), (S_st, S_out), (kT_prev, kp_out), (vx_prev, vp_out)):
                    S.dma(lambda e, t_=t_, dst_=dst_: e.dma_start(out=dst_, in_=t_[:]), final=True)
        S.emit()
    return nc


_PREP_IN = ["norm_mix", "norm_ffn", "norm_ple", "a_w_in", "a_norm", "a_w_out", "b_w_in", "b_w_out",
            "f_w_up", "f_w_down", "ple_w_proj", "ple_w_gate"]
_TILE_PARAMS = ["norm_final", "a_conv", "a_log", "a_dt_bias", "b_sinks", "f_conv"]
_SQUEEZE = {"a_w_in", "a_conv", "a_log", "a_dt_bias", "a_norm", "a_w_out", "b_w_in", "b_sinks", "b_w_out"}
_WB = ["Wa_in", "Wa_ba", "Wa_out", "Wb_in", "Wb_out", "Wf_up0", "Wf_up1", "Wf_dn0", "Wf_dn1",
       "Wp_pr0", "Wp_pr1", "Wp_gt0", "Wp_gt1"]


def _prm(inputs, n):
    a = np.asarray(inputs[n], dtype=np.float32)
    if n in _SQUEEZE:
        a = a[0]
    return np.ascontiguousarray(a)


def make_in_maps(inputs, n_cores=8):
    shared = {n: _prm(inputs, n) for n in _PREP_IN + _TILE_PARAMS}
    x = np.asarray(inputs["x"], dtype=np.float32)
    p = np.asarray(inputs["p"], dtype=np.float32)
    maps = []
    for c in range(n_cores):
        m = dict(shared)
        m["x"] = np.ascontiguousarray(x[c])
        m["p"] = np.ascontiguousarray(p[:, c])
        maps.append(m)
    return maps


def kernel_tiles(inputs, n_cores=8, n_tiles=NT):
    import ml_dtypes
    bf = ml_dtypes.bfloat16
    nc_p = build_program(mode="prep")
    res = run_bass_kernel_spmd(nc_p, [{n: _prm(inputs, n) for n in _PREP_IN}], core_ids=[0])
    wb = {n: np.asarray(res.results[0][n]) for n in _WB}
    params = {n: _prm(inputs, n) for n in _TILE_PARAMS}
    x = np.asarray(inputs["x"], dtype=np.float32)
    p = np.asarray(inputs["p"], dtype=np.float32)
    nc_t = build_program(mode="tile")
    st = [{"S_in": np.zeros((128, 8, 128), np.float32), "ha_in": np.zeros((128, 24, 3), np.float32),
           "hf_in": np.zeros((128, 2, 44, 2), np.float32), "kp_in": np.zeros((128, 2, 4, 128), bf),
           "vp_in": np.zeros((128, 4, 72), bf)} for _ in range(n_cores)]
    out = np.zeros((n_cores, n_tiles * TT, D), np.float32)
    for ti in range(n_tiles):
        flag = np.full((128, 1), -BIG if ti == 0 else 0.0, np.float32)
        maps = []
        for c in range(n_cores):
            m = dict(wb)
            m.update(params)
            m.update(st[c])
            m["flag_in"] = flag
            m["x"] = np.ascontiguousarray(x[c, ti * TT:(ti + 1) * TT])
            m["p"] = np.ascontiguousarray(p[:, c, ti * TT:(ti + 1) * TT])
            maps.append(m)
        r = run_bass_kernel_spmd(nc_t, maps, core_ids=list(range(n_cores)))
        for c in range(n_cores):
            rc = r.results[c]
            out[c, ti * TT:(ti + 1) * TT] = np.asarray(rc["out"], dtype=np.float32)
            st[c] = {"S_in": np.asarray(rc["S_out"]), "ha_in": np.asarray(rc["ha_out"]), "hf_in": np.asarray(rc["hf_out"]),
                     "kp_in": np.asarray(rc["kp_out"]), "vp_in": np.asarray(rc["vp_out"])}
    return out


def kernel(**inputs):
    nc = build_program(mode="fused")
    maps = make_in_maps(inputs)
    res = run_bass_kernel_spmd(nc, maps, core_ids=list(range(8)))
    return np.stack([np.asarray(r["out"], dtype=np.float32) for r in res.results], axis=0)
```

```python
import contextlib
import numpy as np
import concourse.bass as bass
import concourse.mybir as mybir
from concourse.bass_utils import run_bass_kernel_spmd
from concourse.alu_op_type import AluOpType as ALU

F32 = mybir.dt.float32
BF16 = mybir.dt.bfloat16
F32R = mybir.dt.float32r
AF = mybir.ActivationFunctionType

T_SEQ = 4096
D = 1024
TT = 512
NB = TT // 128
NT = T_SEQ // TT
DFF = 2816
NFF = DFF // 128
EPS = 1e-6
BIG = 30000.0
import os
DBG = float(os.environ.get('K_DBG', '99'))
STAGES = os.environ.get('K_STAGES', 'delta,ffn0,ple0,swa,ffn1,ple1').split(',')


class Buf:
    __slots__ = ("name", "w", "r", "excl")

    def __init__(self, name="", excl=False):
        self.name = name
        self.w = None
        self.r = []
        self.excl = excl


class Sched:
    ENGS = ("pe", "act", "dve", "pool", "sp")

    def __init__(self, nc, n_dma_sems=32, same_engine_sync=True):
        self.nc = nc
        self.ops = {e: [] for e in self.ENGS}
        self.count = {e: 0 for e in self.ENGS}
        self.waited = {e: {} for e in self.ENGS}
        self.pending = {e: [] for e in self.ENGS}
        self.n_dma = n_dma_sems
        self.dma_uses = [0] * n_dma_sems
        self.dma_next = 0
        self.same_engine_sync = same_engine_sync
        self.final_tokens = []

    def _deps(self, reads, writes, eng=None):
        deps = []
        for b in reads:
            if b.w is not None:
                deps.append(b.w)
            if b.excl:
                deps.extend(t for t in b.r if t[0] != eng)
        for b in writes:
            if b.w is not None:
                deps.append(b.w)
            deps.extend(b.r)
        return deps

    def _commit(self, tok, reads, writes):
        for b in reads:
            b.r.append(tok)
            if len(b.r) > 48:
                best = {}
                for k, v in b.r:
                    if best.get(k, 0) < v:
                        best[k] = v
                b.r = list(best.items())
        for b in writes:
            b.w = tok
            b.r = []

    def _waits(self, eng, deps):
        out = []
        wd = self.waited[eng]
        best = {}
        for k, v in deps:
            if best.get(k, 0) < v:
                best[k] = v
        for k, v in best.items():
            if k == eng and (eng == "pe" or not self.same_engine_sync):
                continue
            if wd.get(k, 0) >= v:
                continue
            wd[k] = v
            out.append((k, v))
        return out

    def op(self, eng, fn, reads=(), writes=()):
        deps = self._deps(reads, writes, eng)
        if self.pending[eng]:
            deps = deps + self.pending[eng]
            self.pending[eng] = []
        waits = self._waits(eng, deps)
        self.count[eng] += 1
        tok = (eng, self.count[eng])
        self.ops[eng].append((waits, fn, eng, 1))
        self._commit(tok, reads, writes)
        return tok

    def dma(self, fn, reads=(), writes=(), queue="sp", final=False):
        deps = self._deps(reads, writes)
        if self.pending[queue]:
            deps = deps + self.pending[queue]
            self.pending[queue] = []
        slot = self.dma_next
        self.dma_next = (self.dma_next + 1) % self.n_dma
        key = ("dma", slot)
        if self.dma_uses[slot] > 0:
            deps.append((key, 16 * self.dma_uses[slot]))
        waits = self._waits(queue, deps)
        self.dma_uses[slot] += 1
        tok = (key, 16 * self.dma_uses[slot])
        self.ops[queue].append((waits, fn, key, 16))
        self._commit(tok, reads, writes)
        if final:
            self.final_tokens.append(tok)
        return tok

    def barrier(self):
        toks = [(e, self.count[e]) for e in self.ENGS if self.count[e] > 0 and e != "sp"]
        for i in range(self.n_dma):
            if self.dma_uses[i] > 0:
                toks.append((("dma", i), 16 * self.dma_uses[i]))
        for e in self.ENGS:
            self.pending[e] = self.pending[e] + toks

    def emit(self):
        nc = self.nc
        with contextlib.ExitStack() as st:
            sems = {}
            for e in self.ENGS:
                sems[e] = st.enter_context(nc.semaphore("s_" + e))
            for i in range(self.n_dma):
                sems[("dma", i)] = st.enter_context(nc.semaphore("s_dma%d" % i))
            fw = self._waits("sp", list(self.final_tokens))
            block = st.enter_context(nc.Block())

            def run(engname, engobj, extra_waits=()):
                for waits, fn, inc_key, inc_val in self.ops[engname]:
                    for k, v in waits:
                        engobj.wait_ge(sems[k], v)
                    ins = fn(engobj)
                    ins.then_inc(sems[inc_key], inc_val)
                for k, v in extra_waits:
                    engobj.wait_ge(sems[k], v)

            @block.tensor
            def _(e):
                run("pe", e)

            @block.scalar
            def _(e):
                run("act", e)

            @block.vector
            def _(e):
                run("dve", e)

            @block.gpsimd
            def _(e):
                run("pool", e)

            @block.sync
            def _(e):
                run("sp", e, fw)


class Arena:
    def __init__(self, ap, ncols):
        self.ap = ap
        self.n = ncols
        self.pos = 0

    def reset(self):
        self.pos = 0

    def alloc(self, cols):
        cols_al = (cols + 7) // 8 * 8
        assert self.pos + cols_al <= self.n, ("arena overflow", self.pos, cols, self.n)
        a = self.ap[:, self.pos:self.pos + cols]
        self.pos += cols_al
        return a


def bc3(ap2, n):
    return ap2.rearrange("p (h o) -> p h o", o=1).to_broadcast([ap2.shape[0], ap2.shape[1], n])


def v3(ap, inner):
    return ap.rearrange("p (a b) -> p a b", b=inner)


def build_program(stop_after=None, n_tiles=NT, mode="fused"):
    nc = bass.Bass("TRN2", target_bir_lowering=False)
    if mode == "tile":
        n_tiles = 1
    T_LOC = T_SEQ if mode == "fused" else TT

    def din(name, shape, dt=F32):
        return nc.dram_tensor(name, shape, dt, kind="ExternalInput").ap()

    def dout(name, shape, dt=F32):
        return nc.dram_tensor(name, shape, dt, kind="ExternalOutput").ap()

    if mode != "prep":
        x_d = din("x", [T_LOC, D])
        p_d = din("p", [2, T_LOC, 256])
        norm_final = din("norm_final", [D])
        a_conv = din("a_conv", [4, 3072])
        a_log = din("a_log", [8])
        a_dt_bias = din("a_dt_bias", [8])
        b_sinks = din("b_sinks", [16])
        f_conv = din("f_conv", [2, 3, 2 * DFF])
        out_d = dout("out", [T_LOC, D])
    if mode != "tile":
        norm_mix = din("norm_mix", [2, D])
        norm_ffn = din("norm_ffn", [2, D])
        norm_ple = din("norm_ple", [2, D])
        a_w_in = din("a_w_in", [D, 4112])
        a_norm = din("a_norm", [128])
        a_w_out = din("a_w_out", [D, D])
        b_w_in = din("b_w_in", [D, 1536])
        b_w_out = din("b_w_out", [D, D])
        f_w_up = din("f_w_up", [2, D, 2 * DFF])
        f_w_down = din("f_w_down", [2, DFF, D])
        ple_w_proj = din("ple_w_proj", [2, 256, D])
        ple_w_gate = din("ple_w_gate", [2, D, D])

    def dscr(name, shape):
        if mode == "prep":
            return dout(name, shape, BF16)
        if mode == "tile":
            return din(name, shape, BF16)
        return nc.dram_tensor(name, shape, BF16).ap()

    Wa_in = dscr("Wa_in", [D, 4096])
    Wa_ba = dscr("Wa_ba", [D, 16])
    Wa_out = dscr("Wa_out", [D, D])
    Wb_in = dscr("Wb_in", [D, 1024 + 512 + 256])
    Wb_out = dscr("Wb_out", [D, D])
    Wf_up = [dscr("Wf_up%d" % l, [D, 2 * DFF]) for l in range(2)]
    Wf_dn = [dscr("Wf_dn%d" % l, [DFF, D]) for l in range(2)]
    Wp_pr = [dscr("Wp_pr%d" % l, [256, D]) for l in range(2)]
    Wp_gt = [dscr("Wp_gt%d" % l, [D, D]) for l in range(2)]
    if mode == "tile":
        S_in = din("S_in", [128, 8, 128]); S_out = dout("S_out", [128, 8, 128])
        ha_in = din("ha_in", [128, 24, 3]); ha_out = dout("ha_out", [128, 24, 3])
        hf_in = din("hf_in", [128, 2, 44, 2]); hf_out = dout("hf_out", [128, 2, 44, 2])
        kp_in = din("kp_in", [128, 2, 4, 128], BF16); kp_out = dout("kp_out", [128, 2, 4, 128], BF16)
        vp_in = din("vp_in", [128, 4, 72], BF16); vp_out = dout("vp_out", [128, 4, 72], BF16)
        flag_in = din("flag_in", [128, 1])

    S = Sched(nc)
    st = contextlib.ExitStack()
    with st:
        def sb(name, shape, dt):
            return st.enter_context(nc.sbuf_tensor(name, shape, dt))

        h_t = sb("h", [128, NB, D], F32)
        hnT = sb("hnT", [128, 8, TT], BF16)
        identb = sb("identb", [128, 128], BF16)
        identf = sb("identf", [128, 128], F32)
        onesf = sb("onesf", [128, 128], F32)
        triU = sb("triU", [128, 128], F32)
        mposS = sb("mposS", [128, 128], F32)
        mnegI = sb("mnegI", [128, 128], F32)
        nrm = sb("nrm", [128, 7, 8], F32)
        anorm_c = sb("anorm_c", [128, 1], F32)
        nfin_bc = sb("nfin_bc", [128, D], F32)
        aconv_c = sb("aconv_c", [128, 24, 4], F32)
        fconv_c = sb("fconv_c", [128, 2, 44, 3], F32)
        negA_bc = sb("negA_bc", [128, 8], F32)
        dtb_bc = sb("dtb_bc", [128, 8], F32)
        esink_bc = sb("esink_bc", [128, 16], F32)
        halo_a = sb("halo_a", [128, 24, 3], F32)
        halo_f = sb("halo_f", [128, 2, 44, 2], F32)
        S_st = sb("S_st", [128, 8, 128], F32)
        Sb_st = sb("Sb_st", [128, 8, 128], BF16)
        kT_prev = sb("kT_prev", [128, 2, 4, 128], BF16)
        vx_prev = sb("vx_prev", [128, 4, 72], BF16)
        swab = sb("swab", [128, 4, 2, 512], F32)
        wst = [sb("wst%d" % i, [128, 8, 256], BF16) for i in range(3)]
        wout_t = sb("wout_t", [128, 8, D], BF16)
        smalls = sb("smalls", [128, 512], F32)
        arena_f_t = sb("arena_f", [128, 8960], F32)
        neu_t = sb("neu", [128, 4, 6, 128], F32R)
        arena_b_t = sb("arena_b", [128, 31488], BF16)
        AFa = Arena(arena_f_t[:], 8960)
        ABa = Arena(arena_b_t[:], 31488)

        pbank = [st.enter_context(nc.psum_tensor("pb%d" % i, [128, 512], F32)) for i in range(6)]
        pbankh = [st.enter_context(nc.psum_tensor("pbh%d" % i, [128, 1024], BF16)) for i in range(2)]
        pbank_bf = [pbank[i].bitcast(BF16) for i in range(6)]
        PB = [Buf("pb%d" % i, excl=True) for i in range(6)]
        PBH = [Buf("pbh%d" % i, excl=True) for i in range(2)]

        B_h = [Buf("h%d" % b) for b in range(NB)]
        B_hnT = Buf("hnT")
        B_const = Buf("const")
        B_wst = [Buf("wst%d" % i) for i in range(3)]
        B_wout = Buf("wout")
        B_S = [Buf("S%d" % i) for i in range(8)]
        B_Sb = [Buf("Sb%d" % i) for i in range(8)]
        B_halo_a = [Buf("haloa%d" % i) for i in range(24)]
        B_halo_f = [[Buf("halof%d_%d" % (l, i)) for i in range(44)] for l in range(2)]
        B_kprev = Buf("kprev")
        B_vprev = Buf("vprev")
        wst_rr = [0]

        flag_t = sb("flag_t", [128, 1], F32)

        def const_setup():
            rd, wr = [B_const], [B_const]

            def ld(out_ap, in_ap):
                S.dma(lambda e: e.dma_start(out=out_ap, in_=in_ap, allow_slow_non_contiguous=True), writes=wr)
            if mode != "tile":
                for i, (src, l) in enumerate([(norm_mix, 0), (norm_mix, 1), (norm_ffn, 0), (norm_ffn, 1),
                                              (norm_ple, 0), (norm_ple, 1)]):
                    ld(nrm[:, i, :], src[l].rearrange("(k p) -> p k", p=128))
                ld(anorm_c[:], a_norm.rearrange("(p o) -> p o", o=1))
            if mode == "prep":
                return
            S.op("pool", lambda e: e.memset(identf[:], 0.0), writes=wr)
            S.op("pool", lambda e: e.affine_select(out=identf[:], in_=identf[:], pattern=[[-1, 128]],
                                                   compare_op=ALU.not_equal, fill=1.0, base=0, channel_multiplier=1),
                 reads=rd, writes=wr)
            S.op("pool", lambda e: e.tensor_copy(out=identb[:], in_=identf[:]), reads=rd, writes=wr)
            S.op("pool", lambda e: e.memset(onesf[:], 1.0), writes=wr)
            S.op("pool", lambda e: e.affine_select(out=triU[:], in_=onesf[:], pattern=[[1, 128]],
                                                   compare_op=ALU.is_ge, fill=0.0, base=0, channel_multiplier=-1),
                 reads=rd, writes=wr)
            S.op("pool", lambda e: e.memset(mposS[:], 0.0), writes=wr)
            S.op("pool", lambda e: e.affine_select(out=mposS[:], in_=mposS[:], pattern=[[-1, 128]],
                                                   compare_op=ALU.is_gt, fill=BIG, base=0, channel_multiplier=1),
                 reads=rd, writes=wr)
            S.op("pool", lambda e: e.memset(mnegI[:], 0.0), writes=wr)
            S.op("pool", lambda e: e.affine_select(out=mnegI[:], in_=mnegI[:], pattern=[[1, 128]],
                                                   compare_op=ALU.is_ge, fill=-BIG, base=0, channel_multiplier=-1),
                 reads=rd, writes=wr)
            if mode == "tile":
                for (t_, src_) in ((halo_a, ha_in), (halo_f, hf_in), (S_st, S_in), (kT_prev, kp_in), (vx_prev, vp_in),
                                   (flag_t, flag_in)):
                    S.dma(lambda e, t_=t_, src_=src_: e.dma_start(out=t_[:], in_=src_), writes=wr)
                S.op("pool", lambda e: e.tensor_copy(out=Sb_st[:], in_=S_st[:]), reads=rd, writes=wr)
            else:
                for t_ in (halo_a, halo_f, S_st):
                    S.op("pool", lambda e, t_=t_: e.memset(t_[:], 0.0), writes=wr)
                S.op("pool", lambda e: e.memset(Sb_st[:], 0.0), writes=wr)
                S.op("pool", lambda e: e.memset(kT_prev[:], 0.0), writes=wr)
                S.op("pool", lambda e: e.memset(vx_prev[:], 0.0), writes=wr)
                S.op("pool", lambda e: e.memset(flag_t[:], 0.0), writes=wr)
            for g in range(4):
                for hh in range(4):
                    head = 4 * g + hh
                    slope = 2.0 ** (-8.0 * (head + 1) / 16.0)
                    for pc in range(2):
                        dst = swab[:, g, pc, hh * 128:(hh + 1) * 128]
                        S.op("pool", lambda e, dst=dst, pc=pc: e.iota(dst, [[1, 128]], base=(128 if pc == 0 else 0),
                                                                      channel_multiplier=-1,
                                                                      allow_small_or_imprecise_dtypes=True),
                             reads=rd, writes=wr)
                        S.op("pool", lambda e, dst=dst, slope=slope: e.tensor_scalar(out=dst, in0=dst, scalar1=-slope,
                                                                                    scalar2=None, op0=ALU.mult),
                             reads=rd, writes=wr)
                        if pc == 0:
                            S.op("pool", lambda e, dst=dst: e.affine_select(out=dst, in_=dst, pattern=[[-1, 128]],
                                                                            compare_op=ALU.is_gt, fill=-BIG, base=0,
                                                                            channel_multiplier=1),
                                 reads=rd, writes=wr)
                        else:
                            S.op("pool", lambda e, dst=dst: e.affine_select(out=dst, in_=dst, pattern=[[1, 128]],
                                                                            compare_op=ALU.is_ge, fill=-BIG, base=0,
                                                                            channel_multiplier=-1),
                                 reads=rd, writes=wr)
            ld(nfin_bc[:], bass.AP(norm_final.tensor, 0, [[0, 128], [1, D]]))
            for i in range(4):
                ld(aconv_c[:, :, i], a_conv[i].rearrange("(c p) -> p c", p=128))
            for l in range(2):
                for i in range(3):
                    ld(fconv_c[:, l, :, i], f_conv[l, i].rearrange("(c p) -> p c", p=128))
            ld(negA_bc[:], bass.AP(a_log.tensor, 0, [[0, 128], [1, 8]]))
            ld(dtb_bc[:], bass.AP(a_dt_bias.tensor, 0, [[0, 128], [1, 8]]))
            ld(esink_bc[:], bass.AP(b_sinks.tensor, 0, [[0, 128], [1, 16]]))
            S.op("act", lambda e: e.activation(out=negA_bc[:], in_=negA_bc[:], func=AF.Exp), reads=rd, writes=wr)
            S.op("dve", lambda e: e.tensor_scalar(out=negA_bc[:], in0=negA_bc[:], scalar1=-1.0, scalar2=None,
                                                  op0=ALU.mult), reads=rd, writes=wr)
            S.op("act", lambda e: e.activation(out=esink_bc[:], in_=esink_bc[:], func=AF.Exp), reads=rd, writes=wr)

        def weight_prep():
            AFa.reset(); ABa.reset()
            nst = 4
            PW = 2048
            stg_f = [AFa.alloc(PW) for _ in range(nst)]
            stg_b = [ABa.alloc(PW) for _ in range(nst)]
            Bf = [Buf("stgf%d" % i) for i in range(nst)]
            Bb = [Buf("stgb%d" % i) for i in range(nst)]
            engs = ["dve", "act"]
            pieces = []

            def conv_mat(src, dst, nrows, c0, ncols, d0, scale_fn):
                for k in range(nrows // 128):
                    for cc in range(0, ncols, PW):
                        n = min(PW, ncols - cc)
                        pieces.append((src[k * 128:(k + 1) * 128, c0 + cc:c0 + cc + n],
                                       dst[k * 128:(k + 1) * 128, d0 + cc:d0 + cc + n], n,
                                       scale_fn(k) if scale_fn else None))

            def emit_pieces():
                LA = nst - 1
                npc = len(pieces)
                for t in range(npc + LA):
                    if t < npc:
                        src_ap, _, ncols, _ = pieces[t]
                        i = t % nst
                        S.dma(lambda e, i=i, ncols=ncols, src_ap=src_ap: e.dma_start(out=stg_f[i][:, 0:ncols], in_=src_ap),
                              writes=[Bf[i]])
                    u = t - LA
                    if u >= 0:
                        _, dst_ap, ncols, scale_ap = pieces[u]
                        i = u % nst
                        sf, sbb = stg_f[i][:, 0:ncols], stg_b[i][:, 0:ncols]
                        eng = engs[u % 2]
                        rds = [Bf[i]] + ([B_const] if scale_ap is not None else [])
                        if eng == "act":
                            if scale_ap is None:
                                S.op("act", lambda e, sf=sf, sbb=sbb: e.activation(out=sbb, in_=sf, func=AF.Copy), reads=rds, writes=[Bb[i]])
                            else:
                                S.op("act", lambda e, sf=sf, sbb=sbb, scale_ap=scale_ap: e.activation(out=sbb, in_=sf, func=AF.Copy, scale=scale_ap),
                                     reads=rds, writes=[Bb[i]])
                        else:
                            if scale_ap is None:
                                S.op("dve", lambda e, sf=sf, sbb=sbb: e.tensor_copy(out=sbb, in_=sf), reads=rds, writes=[Bb[i]])
                            else:
                                S.op("dve", lambda e, sf=sf, sbb=sbb, scale_ap=scale_ap: e.tensor_scalar(out=sbb, in0=sf, scalar1=scale_ap,
                                                                                                      scalar2=None, op0=ALU.mult),
                                     reads=rds, writes=[Bb[i]])
                        S.dma(lambda e, dst_ap=dst_ap, sbb=sbb: e.dma_start(out=dst_ap, in_=sbb), reads=[Bb[i]], final=(mode == "prep"))

            conv_mat(a_w_in, Wa_in, D, 0, 4096, 0, lambda k: nrm[:, 0, k:k + 1])
            conv_mat(a_w_in, Wa_ba, D, 4096, 16, 0, lambda k: nrm[:, 0, k:k + 1])
            conv_mat(a_w_out, Wa_out, D, 0, D, 0, lambda k: anorm_c[:, 0:1])
            conv_mat(b_w_in, Wb_in, D, 0, 1024, 0, lambda k: nrm[:, 1, k:k + 1])
            for g in range(4):
                for dup in range(2):
                    conv_mat(b_w_in, Wb_in, D, 1024 + g * 64, 64, 1024 + g * 128 + dup * 64, lambda k: nrm[:, 1, k:k + 1])
            conv_mat(b_w_in, Wb_in, D, 1280, 256, 1536, lambda k: nrm[:, 1, k:k + 1])
            conv_mat(b_w_out, Wb_out, D, 0, D, 0, None)
            for l in range(2):
                conv_mat(f_w_up[l], Wf_up[l], D, 0, 2 * DFF, 0, lambda k, l=l: nrm[:, 2 + l, k:k + 1])
                conv_mat(f_w_down[l], Wf_dn[l], DFF, 0, D, 0, None)
                conv_mat(ple_w_proj[l], Wp_pr[l], 256, 0, D, 0, None)
                conv_mat(ple_w_gate[l], Wp_gt[l], D, 0, D, 0, lambda k, l=l: nrm[:, 4 + l, k:k + 1])
            emit_pieces()
            S.barrier()

        def load_wst(Wd, c0, ncols):
            i = wst_rr[0] % 3
            wst_rr[0] += 1
            dst = wst[i][:, :, 0:ncols]
            src = Wd[:, c0:c0 + ncols].rearrange("(k p) n -> p k n", p=128)
            S.dma(lambda e: e.dma_start(out=dst, in_=src), writes=[B_wst[i]])
            return i

        def norm_transpose():
            ss = smalls[:, 0:NB]
            rstd = smalls[:, 8:8 + NB]
            Bss = Buf("ss")
            junk = ABa.alloc(D)
            Bj = Buf("junk")
            xs = [ABa.alloc(D) for _ in range(2)]
            Bxs = [Buf("xs0"), Buf("xs1")]
            for b in range(NB):
                S.op("act", lambda e, b=b: e.activation(out=junk, in_=h_t[:, b, :], func=AF.Square,
                                                        accum_out=ss[:, b:b + 1]),
                     reads=[B_h[b]], writes=[Bj, Bss])
            S.op("act", lambda e: e.activation(out=rstd, in_=ss, func=AF.Ln, scale=1.0 / D, bias=EPS),
                 reads=[Bss], writes=[Bss])
            S.op("act", lambda e: e.activation(out=rstd, in_=rstd, func=AF.Exp, scale=-0.5), reads=[Bss], writes=[Bss])
            for b in range(NB):
                x_ = xs[b % 2]
                S.op("dve", lambda e, b=b, x_=x_: e.tensor_scalar(out=x_, in0=h_t[:, b, :], scalar1=rstd[:, b:b + 1],
                                                                  scalar2=None, op0=ALU.mult),
                     reads=[B_h[b], Bss], writes=[Bxs[b % 2]])
                pt = pbankh[b % 2]
                for k in range(8):
                    S.op("pe", lambda e, k=k, x_=x_, pt=pt: e.transpose(out=pt[:, k * 128:(k + 1) * 128],
                                                                        in_=x_[:, k * 128:(k + 1) * 128],
                                                                        identity=identb[:]),
                         reads=[Bxs[b % 2], B_const], writes=[PBH[b % 2]])
                eng = "dve" if b % 2 == 0 else "act"
                dst = hnT[:, :, b * 128:(b + 1) * 128]
                src = v3(pt[:, :], 128)
                if eng == "dve":
                    S.op("dve", lambda e, dst=dst, src=src: e.tensor_copy(out=dst, in_=src),
                         reads=[PBH[b % 2]], writes=[B_hnT])
                else:
                    S.op("act", lambda e, dst=dst, src=src: e.activation(out=dst, in_=src, func=AF.Copy),
                         reads=[PBH[b % 2]], writes=[B_hnT])

        def proj_fm(Wd, c0, nchunks, consume, banks=(0, 1, 2, 3)):
            per = 2
            bi = 0
            for cb in range(0, nchunks, per):
                n = min(per, nchunks - cb)
                wi = load_wst(Wd, c0 + cb * 128, n * 128)
                for j in range(n):
                    bk = banks[bi % len(banks)]
                    bi += 1
                    for k in range(8):
                        S.op("pe", lambda e, k=k, j=j, wi=wi, bk=bk: e.matmul(
                            pbank[bk][:, :], lhsT=wst[wi][:, k, j * 128:(j + 1) * 128], rhs=hnT[:, k, :],
                            start=(k == 0), stop=(k == 7)),
                            reads=[B_wst[wi], B_hnT], writes=[PB[bk]])
                    consume(cb + j, pbank[bk], PB[bk])

        def resid_proj(lhs_fn, nk, w_fn, w_bufs, lhs_bufs, banks=(4, 5)):
            for b in range(NB):
                for half in range(2):
                    bk = banks[half]
                    for k in range(nk):
                        S.op("pe", lambda e, b=b, k=k, half=half, bk=bk: e.matmul(
                            pbank[bk][:, :], lhsT=lhs_fn(k, b), rhs=w_fn(k, half), start=(k == 0), stop=(k == nk - 1)),
                            reads=list(w_bufs) + list(lhs_bufs), writes=[PB[bk]])
                    hs = h_t[:, b, half * 512:(half + 1) * 512]
                    S.op("dve", lambda e, hs=hs, bk=bk: e.tensor_tensor(out=hs, in0=pbank[bk][:, :], in1=hs, op=ALU.add),
                         reads=[PB[bk], B_h[b]], writes=[B_h[b]])

        def conv_from_psum(items, KW, act_fix=False):
            for it in items:
                S.op("act", lambda e, it=it: e.activation(out=it["ac"], in_=it["ps"][:, 0:TT], func=AF.Copy, scale=it["w"](KW - 1)),
                     reads=[it["Bps"], B_const], writes=[it["Bac"]])
            if act_fix:
                for j in range(KW - 1):
                    s_ = KW - 1 - j
                    for col in range(s_):
                        for it in items:
                            S.op("act", lambda e, it=it, j=j, col=col: e.activation(
                                out=it["ac"][:, col:col + 1], in_=it["halo"][:, j + col:j + col + 1], func=AF.Identity,
                                scale=it["w"](j), bias=it["ac"][:, col:col + 1]),
                                reads=[it["Bhalo"], it["Bac"], B_const], writes=[it["Bac"]])
                for it in items:
                    S.op("act", lambda e, it=it: e.activation(out=it["halo"], in_=it["ps"][:, TT - (KW - 1):TT], func=AF.Copy),
                         reads=[it["Bps"]], writes=[it["Bhalo"]])
            for j in range(KW - 1):
                s_ = KW - 1 - j
                for it in items:
                    S.op("dve", lambda e, it=it, j=j, s_=s_: e.scalar_tensor_tensor(
                        out=it["ac"][:, s_:TT], in0=it["ps"][:, 0:TT - s_], scalar=it["w"](j), in1=it["ac"][:, s_:TT],
                        op0=ALU.mult, op1=ALU.add), reads=[it["Bps"], it["Bac"], B_const], writes=[it["Bac"]])
            if act_fix:
                return
            for j in range(KW - 1):
                s_ = KW - 1 - j
                for it in items:
                    S.op("dve", lambda e, it=it, j=j, s_=s_: e.scalar_tensor_tensor(
                        out=it["ac"][:, 0:s_], in0=it["halo"][:, j:KW - 1], scalar=it["w"](j), in1=it["ac"][:, 0:s_],
                        op0=ALU.mult, op1=ALU.add), reads=[it["Bhalo"], it["Bac"], B_const], writes=[it["Bac"]])
            for it in items:
                S.op("dve", lambda e, it=it: e.tensor_copy(out=it["halo"], in_=it["ps"][:, TT - (KW - 1):TT]),
                     reads=[it["Bps"]], writes=[it["Bhalo"]])

        def deltanet(ti):
            AFa.reset(); ABa.reset()
            norm_transpose()
            qkvT = v3(ABa.alloc(24 * TT), TT)
            zsT = v3(ABa.alloc(8 * TT), TT)
            ogT = v3(ABa.alloc(8 * TT), TT)
            B_qkv = [Buf("qkv%d" % c) for c in range(24)]
            B_zs = Buf("zs")
            B_og = [Buf("og%d" % b) for b in range(NB)]
            ba = v3(AFa.alloc(NB * 16), 16)
            B_ba = Buf("ba")
            pre = [AFa.alloc(TT + 3) for _ in range(3)]
            acc = [AFa.alloc(TT) for _ in range(3)]
            Bpre = [Buf("pre%d" % i) for i in range(3)]
            Bacc = [Buf("acc%d" % i) for i in range(3)]

            S.dma(lambda e: e.dma_start(out=wout_t[:], in_=Wa_out.rearrange("(k p) n -> p k n", p=128)), writes=[B_wout])

            def consume(c, ps, Bps):
                if c < 24:
                    i = c % 3
                    ac = acc[i]
                    conv_from_psum([dict(ps=ps, Bps=Bps, w=(lambda j, c=c: aconv_c[:, c, j:j + 1]), halo=halo_a[:, c, :],
                                         Bhalo=B_halo_a[c], ac=ac, Bac=Bacc[i])], 4)
                    S.op("act", lambda e: e.activation(out=qkvT[:, c, :], in_=ac, func=AF.Silu),
                         reads=[Bacc[i]], writes=[B_qkv[c]])
                else:
                    S.op("act", lambda e: e.activation(out=zsT[:, c - 24, :], in_=ps[:, :], func=AF.Silu),
                         reads=[Bps], writes=[B_zs])

            proj_fm(Wa_in, 0, 32, consume)
            if DBG <= 1:
                S.barrier(); return

            wba = ABa.alloc(8 * 16)
            wba3 = v3(wba, 16)
            B_wba = Buf("wba")
            S.dma(lambda e: e.dma_start(out=wba3, in_=Wa_ba.rearrange("(k p) n -> p k n", p=128)), writes=[B_wba])
            for b in range(NB):
                for k in range(8):
                    S.op("pe", lambda e, b=b, k=k: e.matmul(pbank[4][:, b * 16:(b + 1) * 16],
                                                            lhsT=hnT[:, k, b * 128:(b + 1) * 128], rhs=wba3[:, k, :],
                                                            start=(k == 0), stop=(k == 7)),
                         reads=[B_hnT, B_wba], writes=[PB[4]])
            S.op("dve", lambda e: e.tensor_copy(out=ba, in_=v3(pbank[4][:, 0:NB * 16], 16)), reads=[PB[4]], writes=[B_ba])

            if DBG <= 2:
                S.barrier(); return

            def hb(n, dt):
                return [(AFa if dt == F32 else ABa).alloc(n) for _ in range(4)]
            Kn = hb(128, BF16); kb = hb(128, BF16); kd = hb(128, BF16)
            Qn = hb(128, BF16); qd = hb(128, BF16); bv = hb(128, BF16)
            KQT = hb(384, BF16)
            qkTm = hb(128, BF16); TTb = hb(128, BF16); nwkT = hb(128, BF16); usb = hb(128, BF16)
            dg = hb(128, F32); tL = hb(128, F32); tU = hb(128, F32); EL = hb(128, F32); EU = hb(128, F32)
            PTa_r, Pa_r, PTb_r, Pb_r, Xa_r, Xb_r = [[neu_t[:, i_, kk_, :] for i_ in range(4)] for kk_ in range(6)]
            PPa_r = [neu_t[:, i_, 0:2, :] for i_ in range(4)]
            PPb_r = [neu_t[:, i_, 2:4, :] for i_ in range(4)]
            PTa = [t_.bitcast(F32) for t_ in PTa_r]
            names = "Kn kb kd Qn qd bv KQT qkTm TTb nwkT usb dg tL tU EL EU PTa PTb Pa Pb Xa Xb".split()
            BH = [{n: Buf(n + str(i)) for n in names} for i in range(4)]
            sq = v3(AFa.alloc(16 * 128), 128)
            B_sq = Buf("sq")
            on_all = v3(ABa.alloc(8 * 128), 128)
            B_on = Buf("on")
            junk2 = AFa.alloc(128)
            B_j2 = Buf("junk2")
            B_g = Buf("gate")
            B_sso = Buf("sso")
            NG = NB * 8
            Gall = AFa.alloc(14 * NG + NB * 16)

            def gq(i):
                return Gall[:, i * NG:(i + 1) * NG]
            (beta_a, nbeta_a, g_a, gcs_a, ngcs_a, Eg_a, Ekd_a, Egl_a, c_kb_a, c_kd_a, c_qn_a, c_qd_a, tmp1, tmp2) = [
                gq(i) for i in range(14)]
            rn_a = Gall[:, 14 * NG:14 * NG + NB * 16]
            rn3 = v3(rn_a, 16)
            rnq3, rnk3 = rn3[:, :, 0:8], rn3[:, :, 8:16]

            def q3(x):
                return v3(x, 8)
            sso = smalls[:, 64:72]
            rso = smalls[:, 72:80]
            scale_q = 128.0 ** -0.5
            gr, gw = [B_g], [B_g]
            dtb3 = dtb_bc[:].rearrange("p (o h) -> p o h", o=1).to_broadcast([128, NB, 8])
            negA3 = negA_bc[:].rearrange("p (o h) -> p o h", o=1).to_broadcast([128, NB, 8])
            S.op("act", lambda e: e.activation(out=q3(tmp1), in_=ba[:, :, 0:8], func=AF.Exp, scale=-1.0), reads=[B_ba], writes=gw)
            S.op("dve", lambda e: e.tensor_scalar(out=tmp1, in0=tmp1, scalar1=1.0, scalar2=None, op0=ALU.add), reads=gr, writes=gw)
            S.op("dve", lambda e: e.reciprocal(out=beta_a, in_=tmp1), reads=gr, writes=gw)
            S.op("dve", lambda e: e.tensor_scalar(out=nbeta_a, in0=beta_a, scalar1=-1.0, scalar2=None, op0=ALU.mult), reads=gr, writes=gw)
            S.op("dve", lambda e: e.tensor_tensor(out=q3(tmp2), in0=ba[:, :, 8:16], in1=dtb3, op=ALU.add),
                 reads=[B_ba, B_const] + gr, writes=gw)
            S.op("act", lambda e: e.activation(out=tmp2, in_=tmp2, func=AF.Exp), reads=gr, writes=gw)
            S.op("act", lambda e: e.activation(out=tmp2, in_=tmp2, func=AF.Ln, bias=1.0), reads=gr, writes=gw)
            S.op("dve", lambda e: e.tensor_tensor(out=q3(g_a), in0=q3(tmp2), in1=negA3, op=ALU.mult), reads=[B_const] + gr, writes=gw)
            for b in range(NB):
                S.op("pe", lambda e, b=b: e.matmul(pbank[5][:, b * 8:(b + 1) * 8], lhsT=triU[:], rhs=g_a[:, b * 8:(b + 1) * 8],
                                                   start=True, stop=True), reads=[B_const] + gr, writes=[PB[5]])
                S.op("pe", lambda e, b=b: e.matmul(pbank[5][:, NG + b * 8:NG + (b + 1) * 8], lhsT=onesf[:],
                                                   rhs=g_a[:, b * 8:(b + 1) * 8], start=True, stop=True),
                     reads=[B_const] + gr, writes=[PB[5]])
            for b in range(NB):
                blk = slice(b * 128, (b + 1) * 128)
                S.op("act", lambda e, blk=blk: e.activation(out=sq, in_=qkvT[:, 0:16, blk], func=AF.Square),
                     reads=B_qkv[0:16], writes=[B_sq])
                for c in range(16):
                    S.op("pe", lambda e, c=c, b=b: e.matmul(pbank[5][:, 2 * NG + b * 16 + c:2 * NG + b * 16 + c + 1], lhsT=sq[:, c, :],
                                                            rhs=onesf[:, 0:1], start=True, stop=True),
                         reads=[B_sq, B_const], writes=[PB[5]])
            psg, psl, pss = pbank[5][:, 0:NG], pbank[5][:, NG:2 * NG], pbank[5][:, 2 * NG:2 * NG + NB * 16]
            S.op("act", lambda e: e.activation(out=gcs_a, in_=psg, func=AF.Copy), reads=[PB[5]] + gr, writes=gw)
            S.op("act", lambda e: e.activation(out=ngcs_a, in_=psg, func=AF.Copy, scale=-1.0), reads=[PB[5]] + gr, writes=gw)
            S.op("act", lambda e: e.activation(out=Eg_a, in_=psg, func=AF.Exp), reads=[PB[5]] + gr, writes=gw)
            S.op("act", lambda e: e.activation(out=Egl_a, in_=psl, func=AF.Exp), reads=[PB[5]] + gr, writes=gw)
            S.op("act", lambda e: e.activation(out=rn_a, in_=pss, func=AF.Ln, bias=EPS), reads=[PB[5]] + gr, writes=gw)
            S.op("act", lambda e: e.activation(out=rn_a, in_=rn_a, func=AF.Exp, scale=-0.5), reads=gr, writes=gw)
            S.op("dve", lambda e: e.tensor_tensor(out=tmp1, in0=psl, in1=gcs_a, op=ALU.subtract), reads=[PB[5]] + gr, writes=gw)
            S.op("act", lambda e: e.activation(out=Ekd_a, in_=tmp1, func=AF.Exp), reads=gr, writes=gw)
            S.op("dve", lambda e: e.tensor_tensor(out=q3(tmp2), in0=rnk3, in1=q3(beta_a), op=ALU.mult), reads=gr, writes=gw)
            S.op("dve", lambda e: e.tensor_tensor(out=c_kb_a, in0=tmp2, in1=Eg_a, op=ALU.mult), reads=gr, writes=gw)
            S.op("dve", lambda e: e.tensor_tensor(out=q3(c_kd_a), in0=rnk3, in1=q3(Ekd_a), op=ALU.mult), reads=gr, writes=gw)
            S.op("dve", lambda e: e.tensor_scalar(out=q3(c_qn_a), in0=rnq3, scalar1=scale_q, scalar2=None, op0=ALU.mult),
                 reads=gr, writes=gw)
            S.op("dve", lambda e: e.tensor_tensor(out=c_qd_a, in0=c_qn_a, in1=Eg_a, op=ALU.mult), reads=gr, writes=gw)

            def do_block(b):
                blk = slice(b * 128, (b + 1) * 128)
                beta_, nbeta, gcs, ngcs, Egl, c_kb, c_kd, c_qn, c_qd = [x[:, b * 8:(b + 1) * 8] for x in (
                    beta_a, nbeta_a, gcs_a, ngcs_a, Egl_a, c_kb_a, c_kd_a, c_qn_a, c_qd_a)]
                rnk = rn_a[:, b * 16 + 8:b * 16 + 16]
                if DBG <= 3:
                    S.barrier(); return
                for grp in range(2):
                    heads = [grp * 4 + i for i in range(4)]
                    ob = 4 + grp
                    for i, hh in enumerate(heads):
                        S.op("act", lambda e, i=i, hh=hh: e.activation(out=dg[i], in_=identf[:], func=AF.Copy, scale=gcs[:, hh:hh + 1]),
                             reads=[B_const] + gr, writes=[BH[i]["dg"]])
                    for i, hh in enumerate(heads):
                        S.op("pe", lambda e, i=i: e.matmul(pbank[i][:, 256:384], lhsT=onesf[:], rhs=dg[i], start=True, stop=True),
                             reads=[BH[i]["dg"], B_const], writes=[PB[i]])
                    for i, hh in enumerate(heads):
                        B = BH[i]
                        S.op("dve", lambda e, i=i: e.tensor_tensor(out=tL[i], in0=pbank[i][:, 256:384], in1=mposS[:], op=ALU.add),
                             reads=[PB[i], B_const], writes=[B["tL"]])
                        S.op("dve", lambda e, i=i: e.tensor_tensor(out=tU[i], in0=pbank[i][:, 256:384], in1=mnegI[:], op=ALU.add),
                             reads=[PB[i], B_const], writes=[B["tU"]])
                        S.op("act", lambda e, i=i, hh=hh: e.activation(out=EL[i], in_=tL[i], func=AF.Exp, scale=-1.0,
                                                                       bias=gcs[:, hh:hh + 1]),
                             reads=[B["tL"]] + gr, writes=[B["EL"]])
                        S.op("act", lambda e, i=i, hh=hh: e.activation(out=EU[i], in_=tU[i], func=AF.Exp,
                                                                       bias=ngcs[:, hh:hh + 1]),
                             reads=[B["tU"]] + gr, writes=[B["EU"]])
                    for i, hh in enumerate(heads):
                        pt = pbankh[i // 2]
                        for j, c in enumerate((8 + hh, hh, 16 + hh)):
                            o_ = ((i % 2) * 3 + j) * 128
                            S.op("pe", lambda e, pt=pt, o_=o_, c=c, blk=blk: e.transpose(
                                out=pt[:, o_:o_ + 128], in_=qkvT[:, c, blk], identity=identb[:]),
                                reads=[B_qkv[c], B_const], writes=[PBH[i // 2]])
                    if DBG <= 3.3:
                        S.barrier(); return
                    for urgent in (True, False):
                        for i, hh in enumerate(heads):
                            pt = pbankh[i // 2]
                            o_ = (i % 2) * 384
                            Kt, Qt, Vt = pt[:, o_:o_ + 128], pt[:, o_ + 128:o_ + 256], pt[:, o_ + 256:o_ + 384]
                            B = BH[i]
                            beng = "act" if i // 2 == 0 else "dve"
                            lst = (((Kn[i], Kt, rnk, "Kn"), (Qn[i], Qt, c_qn, "Qn"), (qd[i], Qt, c_qd, "qd")) if urgent else
                                   ((kb[i], Kt, c_kb, "kb"), (kd[i], Kt, c_kd, "kd"), (bv[i], Vt, beta_, "bv")))
                            for (dst, src, sc, nm) in lst:
                                sc1 = sc[:, hh:hh + 1]
                                if beng == "act":
                                    S.op("act", lambda e, dst=dst, src=src, sc1=sc1: e.activation(out=dst, in_=src, func=AF.Copy,
                                                                                                 scale=sc1),
                                         reads=[PBH[i // 2]] + gr, writes=[B[nm]])
                                else:
                                    S.op("dve", lambda e, dst=dst, src=src, sc1=sc1: e.tensor_scalar(
                                        out=dst, in0=src, scalar1=sc1, scalar2=None, op0=ALU.mult),
                                        reads=[PBH[i // 2]] + gr, writes=[B[nm]])
                    if DBG <= 3.6:
                        S.barrier(); return
                    for i, hh in enumerate(heads):
                        pv_ = pbank_bf[i]
                        B = BH[i]
                        for j, (src, nm) in enumerate(((Kn[i], "Kn"), (Qn[i], "Qn"), (qd[i], "qd"))):
                            S.op("pe", lambda e, pv_=pv_, j=j, src=src: e.transpose(out=pv_[:, j * 128:(j + 1) * 128], in_=src,
                                                                                  identity=identb[:]),
                                 reads=[B[nm], B_const], writes=[PB[i]])
                    for i, hh in enumerate(heads):
                        pv_ = pbank_bf[i]
                        if i % 2 == 1:
                            S.op("dve", lambda e, i=i, pv_=pv_: e.tensor_copy(out=KQT[i], in_=pv_[:, 0:384]),
                                 reads=[PB[i]], writes=[BH[i]["KQT"]])
                        else:
                            S.op("act", lambda e, i=i, pv_=pv_: e.activation(out=KQT[i], in_=pv_[:, 0:384], func=AF.Copy),
                                 reads=[PB[i]], writes=[BH[i]["KQT"]])
                    if DBG <= 4:
                        S.barrier(); return
                    for i, hh in enumerate(heads):
                        B = BH[i]
                        KnT, QnT = KQT[i][:, 0:128], KQT[i][:, 128:256]
                        S.op("pe", lambda e, i=i, KnT=KnT: e.matmul(pbank[i][:, 0:128], lhsT=KnT, rhs=KnT, start=True, stop=True),
                             reads=[B["KQT"]], writes=[PB[i]])
                        S.op("pe", lambda e, i=i, KnT=KnT, QnT=QnT: e.matmul(pbank[i][:, 128:256], lhsT=KnT, rhs=QnT,
                                                                           start=True, stop=True),
                             reads=[B["KQT"]], writes=[PB[i]])
                    for i, hh in enumerate(heads):
                        B = BH[i]
                        S.op("dve", lambda e, i=i, hh=hh: e.scalar_tensor_tensor(
                            out=PTa_r[i], in0=pbank[i][:, 0:128], scalar=nbeta[:, hh:hh + 1], in1=EL[i],
                            op0=ALU.mult, op1=ALU.mult), reads=[PB[i], B["EL"]] + gr, writes=[B["PTa"]])
                        S.op("dve", lambda e, i=i: e.tensor_tensor(out=qkTm[i], in0=pbank[i][:, 128:256], in1=EU[i], op=ALU.mult),
                             reads=[PB[i], B["EU"]], writes=[B["qkTm"]])
                    if DBG <= 5:
                        S.barrier(); return
                    for i, hh in enumerate(heads):
                        B = BH[i]
                        S.op("pe", lambda e, i=i: e.transpose(out=pbank[i][:, 0:128], in_=PTa[i], identity=identf[:]),
                             reads=[B["PTa"], B_const], writes=[PB[i]])
                        S.op("dve", lambda e, i=i: e.tensor_copy(out=Pa_r[i], in_=pbank[i][:, 0:128]),
                             reads=[PB[i]], writes=[B["Pa"]])
                        S.op("dve", lambda e, i=i: e.tensor_tensor(out=Xa_r[i], in0=pbank[i][:, 0:128], in1=identf[:], op=ALU.add),
                             reads=[PB[i], B_const], writes=[B["Xa"]])
                    cur = {"P": (Pa_r, "Pa"), "PT": (PTa_r, "PTa"), "X": (Xa_r, "Xa"), "PP": PPa_r}
                    alt = {"P": (Pb_r, "Pb"), "PT": (PTb_r, "PTb"), "X": (Xb_r, "Xb"), "PP": PPb_r}
                    for kk_ in range(1, 5):
                        last = (kk_ == 4)
                        (Pc, Pcn), (PTc, PTcn), (Xc, Xcn) = cur["P"], cur["PT"], cur["X"]
                        (Pn, Pnn), (PTn, PTnn), (Xn, Xnn) = alt["P"], alt["PT"], alt["X"]
                        PPn = alt["PP"]
                        for i, hh in enumerate(heads):
                            B = BH[i]
                            S.op("pe", lambda e, i=i, Pc=Pc, PTc=PTc: e.matmul(pbank[i][:, 0:128], lhsT=Pc[i], rhs=PTc[i],
                                                                             start=True, stop=True),
                                 reads=[B[Pcn], B[PTcn]], writes=[PB[i]])
                            if not last:
                                S.op("pe", lambda e, i=i, Pc=Pc, PTc=PTc: e.matmul(pbank[i][:, 128:256], lhsT=PTc[i], rhs=Pc[i],
                                                                                 start=True, stop=True),
                                     reads=[B[Pcn], B[PTcn]], writes=[PB[i]])
                        for i, hh in enumerate(heads):
                            B = BH[i]
                            if not last:
                                S.op("dve", lambda e, i=i, PPn=PPn: e.tensor_copy(out=PPn[i], in_=v3(pbank[i][:, 0:256], 128)),
                                     reads=[PB[i]], writes=[B[PTnn], B[Pnn]])
                            else:
                                S.op("dve", lambda e, i=i, PTn=PTn: e.tensor_copy(out=PTn[i], in_=pbank[i][:, 0:128]),
                                     reads=[PB[i]], writes=[B[PTnn]])
                        for i, hh in enumerate(heads):
                            B = BH[i]
                            S.op("pe", lambda e, i=i, PTn=PTn, Xc=Xc, ob=ob: e.matmul(pbank[ob][:, i * 128:(i + 1) * 128], lhsT=PTn[i],
                                                                                    rhs=Xc[i], start=True, stop=True),
                                 reads=[B[PTnn], B[Xcn]], writes=[PB[ob]])
                        for i, hh in enumerate(heads):
                            B = BH[i]
                            if not last:
                                S.op("dve", lambda e, i=i, Xn=Xn, Xc=Xc, ob=ob: e.tensor_tensor(
                                    out=Xn[i], in0=pbank[ob][:, i * 128:(i + 1) * 128], in1=Xc[i].bitcast(F32), op=ALU.add),
                                    reads=[PB[ob], B[Xcn]], writes=[B[Xnn]])
                            else:
                                S.op("dve", lambda e, i=i, Xc=Xc, ob=ob: e.tensor_tensor(
                                    out=TTb[i], in0=pbank[ob][:, i * 128:(i + 1) * 128], in1=Xc[i].bitcast(F32), op=ALU.add),
                                    reads=[PB[ob], B[Xcn]], writes=[B["TTb"]])
                        cur, alt = alt, cur
                    if DBG <= 6:
                        S.barrier(); return
                    for i, hh in enumerate(heads):
                        B = BH[i]
                        S.op("pe", lambda e, i=i: e.matmul(pbank[i][:, 0:128], lhsT=kb[i], rhs=TTb[i], start=True, stop=True),
                             reads=[B["kb"], B["TTb"]], writes=[PB[i]])
                        S.op("dve", lambda e, i=i: e.tensor_scalar(out=nwkT[i], in0=pbank[i][:, 0:128], scalar1=-1.0, scalar2=None,
                                                                    op0=ALU.mult),
                             reads=[PB[i]], writes=[B["nwkT"]])
                    for i, hh in enumerate(heads):
                        B = BH[i]
                        S.op("pe", lambda e, i=i: e.matmul(pbank[i][:, 128:256], lhsT=TTb[i], rhs=bv[i], start=True, stop=False),
                             reads=[B["TTb"], B["bv"]], writes=[PB[i]])
                        S.op("pe", lambda e, i=i, hh=hh: e.matmul(pbank[i][:, 128:256], lhsT=nwkT[i], rhs=Sb_st[:, hh, :],
                                                                 start=False, stop=True),
                             reads=[B["nwkT"], B_Sb[hh]], writes=[PB[i]])
                        S.op("dve", lambda e, i=i: e.tensor_copy(out=usb[i], in_=pbank[i][:, 128:256]),
                             reads=[PB[i]], writes=[B["usb"]])
                    for i, hh in enumerate(heads):
                        B = BH[i]
                        qdT = KQT[i][:, 256:384]
                        oc = slice(i * 128, (i + 1) * 128)
                        S.op("pe", lambda e, qdT=qdT, hh=hh, oc=oc, ob=ob: e.matmul(pbank[ob][:, oc], lhsT=qdT, rhs=Sb_st[:, hh, :],
                                                                                  start=True, stop=False),
                             reads=[B["KQT"], B_Sb[hh]], writes=[PB[ob]])
                        S.op("pe", lambda e, i=i, oc=oc, ob=ob: e.matmul(pbank[ob][:, oc], lhsT=qkTm[i], rhs=usb[i],
                                                                       start=False, stop=True),
                             reads=[B["qkTm"], B["usb"]], writes=[PB[ob]])
                        S.op("pe", lambda e, i=i: e.matmul(pbank[i][:, 256:384], lhsT=kd[i], rhs=usb[i], start=True, stop=True),
                             reads=[B["kd"], B["usb"]], writes=[PB[i]])
                        S.op("dve", lambda e, i=i, hh=hh: e.scalar_tensor_tensor(
                            out=S_st[:, hh, :], in0=S_st[:, hh, :], scalar=Egl[:, hh:hh + 1], in1=pbank[i][:, 256:384],
                            op0=ALU.mult, op1=ALU.add), reads=[PB[i], B_S[hh]] + gr, writes=[B_S[hh]])
                        S.op("act", lambda e, hh=hh: e.activation(out=Sb_st[:, hh, :], in_=S_st[:, hh, :], func=AF.Copy),
                             reads=[B_S[hh]], writes=[B_Sb[hh]])
                    for i, hh in enumerate(heads):
                        oc = slice(i * 128, (i + 1) * 128)
                        S.op("act", lambda e, oc=oc, ob=ob, hh=hh: e.activation(out=junk2, in_=pbank[ob][:, oc], func=AF.Square,
                                                                               accum_out=sso[:, hh:hh + 1]),
                             reads=[PB[ob]], writes=[B_j2, B_sso])
                if DBG <= 7:
                    S.barrier(); return
                S.op("act", lambda e: e.activation(out=rso, in_=sso, func=AF.Ln, scale=1.0 / 128, bias=EPS), reads=[B_sso], writes=[B_sso])
                S.op("act", lambda e: e.activation(out=rso, in_=rso, func=AF.Exp, scale=-0.5), reads=[B_sso], writes=[B_sso])
                for grp in range(2):
                    ob = 4 + grp
                    S.op("dve", lambda e, grp=grp, ob=ob: e.tensor_tensor(
                        out=on_all[:, grp * 4:(grp + 1) * 4, :], in0=v3(pbank[ob][:, :], 128),
                        in1=bc3(rso[:, grp * 4:(grp + 1) * 4], 128), op=ALU.mult),
                        reads=[PB[ob], B_sso], writes=[B_on])
                for hh in range(8):
                    S.op("pe", lambda e, hh=hh: e.transpose(out=pbankh[0][:, hh * 128:(hh + 1) * 128], in_=on_all[:, hh, :],
                                                            identity=identb[:]),
                         reads=[B_on, B_const], writes=[PBH[0]])
                S.op("dve", lambda e, blk=blk: e.tensor_tensor(out=ogT[:, :, blk], in0=v3(pbankh[0][:, :], 128),
                                                               in1=zsT[:, :, blk], op=ALU.mult),
                     reads=[PBH[0], B_zs], writes=[B_og[b]])
            for b in range(NB):
                do_block(b)
            resid_proj(lambda k, b: ogT[:, k, b * 128:(b + 1) * 128], 8,
                       lambda k, half: wout_t[:, k, half * 512:(half + 1) * 512], [B_wout], B_og)
            S.barrier()

        def conv_ffn(l):
            AFa.reset(); ABa.reset()
            norm_transpose()
            aT = v3(ABa.alloc(NFF * TT), TT)
            B_aT = [Buf("aT%d" % c) for c in range(NFF)]
            wdn = v3(ABa.alloc(NFF * 512), 512)
            B_wdn = Buf("wdn")

            def load_wdn(half):
                for (k0, k1) in ((0, 11), (11, 22)):
                    S.dma(lambda e, k0=k0, k1=k1: e.dma_start(
                        out=wdn[:, k0:k1, :],
                        in_=Wf_dn[l][k0 * 128:k1 * 128, half * 512:(half + 1) * 512].rearrange("(k p) n -> p k n", p=128)),
                        writes=[B_wdn])
            load_wdn(0)
            npre = 8
            acc = [AFa.alloc(TT) for _ in range(npre)]
            sg = [AFa.alloc(TT) for _ in range(4)]
            Bacc = [Buf("facc%d" % i) for i in range(npre)]
            Bsg = [Buf("sg%d" % i) for i in range(4)]
            cnt = [0]
            per = 2
            bi = 0
            for cb in range(0, NFF, per):
                n = min(per, NFF - cb)
                wg = load_wst(Wf_up[l], cb * 128, n * 128)
                wv = load_wst(Wf_up[l], DFF + cb * 128, n * 128)
                for j in range(n):
                    c = cb + j
                    items = []
                    for (wi, cg) in ((wg, c), (wv, NFF + c)):
                        bk = bi % 4
                        bi += 1
                        for k in range(8):
                            S.op("pe", lambda e, k=k, j=j, wi=wi, bk=bk: e.matmul(
                                pbank[bk][:, :], lhsT=wst[wi][:, k, j * 128:(j + 1) * 128], rhs=hnT[:, k, :],
                                start=(k == 0), stop=(k == 7)), reads=[B_wst[wi], B_hnT], writes=[PB[bk]])
                        i = cnt[0] % npre
                        cnt[0] += 1
                        items.append(dict(ps=pbank[bk], Bps=PB[bk], w=(lambda jj, cg=cg: fconv_c[:, l, cg, jj:jj + 1]),
                                          halo=halo_f[:, l, cg, :], Bhalo=B_halo_f[l][cg], ac=acc[i], Bac=Bacc[i]))
                    conv_from_psum(items, 3)
                    ag, Bag, av, Bav = items[0]["ac"], items[0]["Bac"], items[1]["ac"], items[1]["Bac"]
                    s_ = sg[c % 4]
                    S.op("act", lambda e, ag=ag, s_=s_: e.activation(out=s_, in_=ag, func=AF.Silu), reads=[Bag], writes=[Bsg[c % 4]])
                    S.op("pool", lambda e, c=c, av=av, s_=s_: e.tensor_tensor(out=aT[:, c, :], in0=s_, in1=av, op=ALU.mult),
                         reads=[Bsg[c % 4], Bav], writes=[B_aT[c]])
            for half in range(2):
                if half == 1:
                    load_wdn(1)
                for b in range(NB):
                    bk = 4 + (b % 2)
                    for k in range(NFF):
                        S.op("pe", lambda e, b=b, k=k, bk=bk: e.matmul(
                            pbank[bk][:, :], lhsT=aT[:, k, b * 128:(b + 1) * 128], rhs=wdn[:, k, :],
                            start=(k == 0), stop=(k == NFF - 1)), reads=[B_wdn, B_aT[k]], writes=[PB[bk]])
                    hs = h_t[:, b, half * 512:(half + 1) * 512]
                    S.op("dve", lambda e, hs=hs, bk=bk: e.tensor_tensor(out=hs, in0=pbank[bk][:, :], in1=hs, op=ALU.add),
                         reads=[PB[bk], B_h[b]], writes=[B_h[b]])
            S.barrier()

        def ple(l, ti):
            AFa.reset(); ABa.reset()
            norm_transpose()
            wpr = v3(ABa.alloc(2 * D), D)
            B_wpr = Buf("wpr")
            S.dma(lambda e: e.dma_start(out=wpr, in_=Wp_pr[l].rearrange("(k p) n -> p k n", p=128)), writes=[B_wpr])
            S.dma(lambda e: e.dma_start(out=wout_t[:], in_=Wp_gt[l].rearrange("(k p) n -> p k n", p=128)), writes=[B_wout])
            pin = [AFa.alloc(256) for _ in range(2)]
            pT = [ABa.alloc(256) for _ in range(2)]
            gate = [AFa.alloc(512) for _ in range(2)]
            Bpin = [Buf("pin0"), Buf("pin1")]
            BpT = [Buf("pT0"), Buf("pT1")]
            Bgate = [Buf("gate0"), Buf("gate1")]
            gi = 0
            for b in range(NB):
                i = b % 2
                t0 = ti * TT + b * 128
                S.dma(lambda e, i=i, t0=t0: e.dma_start(out=pin[i], in_=p_d[l, t0:t0 + 128, :]), writes=[Bpin[i]])
                for k in range(2):
                    S.op("pe", lambda e, i=i, k=k: e.transpose(out=pbank[4][:, k * 128:(k + 1) * 128],
                                                               in_=pin[i][:, k * 128:(k + 1) * 128], identity=identf[:]),
                         reads=[Bpin[i], B_const], writes=[PB[4]])
                S.op("act", lambda e, i=i: e.activation(out=pT[i], in_=pbank[4][:, 0:256], func=AF.Copy), reads=[PB[4]], writes=[BpT[i]])
                for half in range(2):
                    hs = slice(half * 512, (half + 1) * 512)
                    gbk = half
                    pbk = 2 + half
                    for k in range(8):
                        S.op("pe", lambda e, b=b, k=k, hs=hs, gbk=gbk: e.matmul(
                            pbank[gbk][:, :], lhsT=hnT[:, k, b * 128:(b + 1) * 128], rhs=wout_t[:, k, hs],
                            start=(k == 0), stop=(k == 7)), reads=[B_hnT, B_wout], writes=[PB[gbk]])
                    for k in range(2):
                        S.op("pe", lambda e, i=i, k=k, hs=hs, pbk=pbk: e.matmul(
                            pbank[pbk][:, :], lhsT=pT[i][:, k * 128:(k + 1) * 128], rhs=wpr[:, k, hs],
                            start=(k == 0), stop=(k == 1)), reads=[BpT[i], B_wpr], writes=[PB[pbk]])
                    gt = gate[gi % 2]
                    Bg = Bgate[gi % 2]
                    gi += 1
                    S.op("act", lambda e, gt=gt, gbk=gbk: e.activation(out=gt, in_=pbank[gbk][:, :], func=AF.Sigmoid),
                         reads=[PB[gbk]], writes=[Bg])
                    S.op("dve", lambda e, gt=gt, pbk=pbk: e.tensor_tensor(out=gt, in0=pbank[pbk][:, :], in1=gt, op=ALU.mult),
                         reads=[PB[pbk], Bg], writes=[Bg])
                    S.op("dve", lambda e, gt=gt, b=b, hs=hs: e.tensor_tensor(out=h_t[:, b, hs], in0=h_t[:, b, hs], in1=gt, op=ALU.add),
                         reads=[Bg, B_h[b]], writes=[B_h[b]])
            S.barrier()

        def swa(ti):
            AFa.reset(); ABa.reset()
            norm_transpose()
            qT = v3(ABa.alloc(8 * TT), TT)
            kT0 = v3(ABa.alloc(4 * (TT + 128)), TT + 128)
            kT1 = v3(ABa.alloc(4 * (TT + 128)), TT + 128)
            kTz = (kT0, kT1)
            vx = ABa.alloc((NB + 1) * 4 * 72).rearrange("p (b g d) -> p b g d", g=4, d=72)
            aoT = v3(ABa.alloc(8 * TT), TT)
            B_qT = [Buf("qT%d" % c) for c in range(8)]
            B_kT = Buf("kT")
            B_vx = [Buf("vx%d" % b) for b in range(NB + 1)]
            B_ao = [Buf("ao%d" % b) for b in range(NB)]
            S.dma(lambda e: e.dma_start(out=wout_t[:], in_=Wb_out.rearrange("(k p) n -> p k n", p=128)), writes=[B_wout])
            S.op("pool", lambda e: e.memset(kT0, 0.0), writes=[B_kT])
            S.op("pool", lambda e: e.memset(kT1, 0.0), writes=[B_kT])
            S.op("pool", lambda e: e.tensor_copy(out=kT0[:, :, 0:128], in_=kT_prev[:, 0, :, :]), reads=[B_kprev], writes=[B_kT])
            S.op("pool", lambda e: e.tensor_copy(out=kT1[:, :, 0:128], in_=kT_prev[:, 1, :, :]), reads=[B_kprev], writes=[B_kT])
            S.op("pool", lambda e: e.tensor_copy(out=vx[:, 0, :, :], in_=vx_prev[:]), reads=[B_vprev], writes=[B_vx[0]])

            def consume(c, ps, Bps):
                if c < 8:
                    S.op("act", lambda e: e.activation(out=qT[:, c, :], in_=ps[:, :], func=AF.Copy), reads=[Bps], writes=[B_qT[c]])
                else:
                    g = c - 8
                    S.op("dve", lambda e: e.tensor_copy(out=kT0[0:64, g, 128:128 + TT], in_=ps[0:64, :]), reads=[Bps], writes=[B_kT])
                    S.op("dve", lambda e: e.tensor_copy(out=kT1[64:128, g, 128:128 + TT], in_=ps[64:128, :]), reads=[Bps], writes=[B_kT])
            proj_fm(Wb_in, 0, 12, consume)
            wv = ABa.alloc(8 * 256)
            wv3 = v3(wv, 256)
            B_wv = Buf("wv")
            S.dma(lambda e: e.dma_start(out=wv3, in_=Wb_in[:, 1536:1792].rearrange("(k p) n -> p k n", p=128)), writes=[B_wv])
            for b in range(NB):
                for k in range(8):
                    S.op("pe", lambda e, b=b, k=k: e.matmul(pbank[4][:, 0:256], lhsT=hnT[:, k, b * 128:(b + 1) * 128],
                                                            rhs=wv3[:, k, :], start=(k == 0), stop=(k == 7)),
                         reads=[B_hnT, B_wv], writes=[PB[4]])
                S.op("pool", lambda e, b=b: e.memset(vx[:, b + 1, :, :], 1.0), writes=[B_vx[b + 1]])
                S.op("act", lambda e, b=b: e.activation(out=vx[:, b + 1, :, 0:64], in_=v3(pbank[4][:, 0:256], 64), func=AF.Copy),
                     reads=[PB[4]], writes=[B_vx[b + 1]])
            S.op("pool", lambda e: e.tensor_copy(out=kT_prev[:, 0, :, :], in_=kT0[:, :, TT:TT + 128]), reads=[B_kT], writes=[B_kprev])
            S.op("pool", lambda e: e.tensor_copy(out=kT_prev[:, 1, :, :], in_=kT1[:, :, TT:TT + 128]), reads=[B_kT], writes=[B_kprev])
            S.op("pool", lambda e: e.tensor_copy(out=vx_prev[:], in_=vx[:, NB, :, :]), reads=[B_vx[NB]], writes=[B_vprev])

            tmp = [AFa.alloc(512) for _ in range(4)]
            ee = [ABa.alloc(512) for _ in range(4)]
            Btmp = [Buf("stmp%d" % i) for i in range(4)]
            Bee = [Buf("see%d" % i) for i in range(4)]
            ao = [ABa.alloc(D) for _ in range(2)]
            Bao_t = [Buf("aot0"), Buf("aot1")]
            den = smalls[:, 300:332]
            B_den = Buf("den")
            ri = [0]

            def swa_scores(b, g):
                first = (mode == "fused" and ti == 0 and b == 0)
                use_flag = (mode == "tile" and b == 0)
                parts = ([] if first else [0]) + [1]
                ebufs = {}
                for pc in parts:
                    bk = (2 * g + pc) % 4
                    r = ri[0] % 4
                    ri[0] += 1
                    kofs = b * 128 + pc * 128
                    for hh in range(4):
                        head = 4 * g + hh
                        c, half = head // 2, head % 2
                        kz = kTz[half]
                        S.op("pe", lambda e, bk=bk, kz=kz, g=g, kofs=kofs, c=c, b=b, hh=hh: e.matmul(
                            pbank[bk][:, hh * 128:(hh + 1) * 128], lhsT=kz[:, g, kofs:kofs + 128],
                            rhs=qT[:, c, b * 128:(b + 1) * 128], start=True, stop=True),
                            reads=[B_kT, B_qT[c]], writes=[PB[bk]])
                    S.op("dve", lambda e, bk=bk, r=r, g=g, pc=pc: e.scalar_tensor_tensor(
                        out=tmp[r], in0=pbank[bk][:, :], scalar=0.125, in1=swab[:, g, pc, :], op0=ALU.mult, op1=ALU.add),
                        reads=[PB[bk], B_const], writes=[Btmp[r]])
                    if use_flag and pc == 0:
                        S.op("act", lambda e, r=r: e.activation(out=ee[r], in_=tmp[r], func=AF.Exp, bias=flag_t[:, 0:1]),
                             reads=[Btmp[r], B_const], writes=[Bee[r]])
                    else:
                        S.op("act", lambda e, r=r: e.activation(out=ee[r], in_=tmp[r], func=AF.Exp), reads=[Btmp[r]], writes=[Bee[r]])
                    ebufs[pc] = r
                return parts, ebufs

            def swa_pv(b, g, parts, ebufs):
                ao_b = ao[b % 2]
                ps_o = pbank[4 + (g % 2)]
                Bpo = PB[4 + (g % 2)]
                for hh in range(4):
                    for n_, pc in enumerate(parts):
                        r = ebufs[pc]
                        S.op("pe", lambda e, ps_o=ps_o, hh=hh, r=r, b=b, pc=pc, g=g, n_=n_, np_=len(parts): e.matmul(
                            ps_o[:, hh * 72:hh * 72 + 66], lhsT=ee[r][:, hh * 128:(hh + 1) * 128], rhs=vx[:, b + pc, g, 0:66],
                            start=(n_ == 0), stop=(n_ == np_ - 1)),
                            reads=[Bee[r], B_vx[b + pc]], writes=[Bpo])
                dn = den[:, g * 4:(g + 1) * 4]
                po3 = ps_o[:, 0:288].rearrange("p (h d) -> p h d", d=72)
                S.op("dve", lambda e, dn=dn, po3=po3, g=g: e.tensor_tensor(
                    out=dn, in0=po3[:, :, 64], in1=esink_bc[:, g * 4:(g + 1) * 4], op=ALU.add),
                    reads=[Bpo, B_const], writes=[B_den])
                S.op("dve", lambda e, dn=dn: e.reciprocal(out=dn, in_=dn), reads=[B_den], writes=[B_den])
                S.op("dve", lambda e, dn=dn, po3=po3, g=g, ao_b=ao_b: e.tensor_tensor(
                    out=ao_b[:, g * 256:(g + 1) * 256].rearrange("p (h d) -> p h d", d=64), in0=po3[:, :, 0:64],
                    in1=bc3(dn, 64), op=ALU.mult),
                    reads=[Bpo, B_den], writes=[Bao_t[b % 2]])

            def swa_finish(b):
                ao_b = ao[b % 2]
                for k in range(8):
                    S.op("pe", lambda e, k=k, ao_b=ao_b: e.transpose(out=pbankh[1][:, k * 128:(k + 1) * 128],
                                                                     in_=ao_b[:, k * 128:(k + 1) * 128], identity=identb[:]),
                         reads=[Bao_t[b % 2], B_const], writes=[PBH[1]])
                S.op("act", lambda e, b=b: e.activation(out=aoT[:, :, b * 128:(b + 1) * 128], in_=v3(pbankh[1][:, :], 128), func=AF.Copy),
                     reads=[PBH[1]], writes=[B_ao[b]])

            items_ = [(b, g) for b in range(NB) for g in range(4)]
            pend = swa_scores(*items_[0])
            for k_, (b, g) in enumerate(items_):
                nxt = swa_scores(*items_[k_ + 1]) if k_ + 1 < len(items_) else None
                swa_pv(b, g, *pend)
                if g == 3:
                    swa_finish(b)
                pend = nxt
            resid_proj(lambda k, b: aoT[:, k, b * 128:(b + 1) * 128], 8,
                       lambda k, half: wout_t[:, k, half * 512:(half + 1) * 512], [B_wout], B_ao, banks=(0, 1))
            S.barrier()

        def final_out(ti, do_norm=True):
            AFa.reset(); ABa.reset()
            ss = smalls[:, 0:NB]
            rstd = smalls[:, 8:8 + NB]
            Bss = Buf("fss")
            junk = ABa.alloc(D)
            Bj = Buf("fjunk")
            ot = [AFa.alloc(D) for _ in range(2)]
            Bot = [Buf("ot0"), Buf("ot1")]
            if do_norm:
                for b in range(NB):
                    S.op("act", lambda e, b=b: e.activation(out=junk, in_=h_t[:, b, :], func=AF.Square, accum_out=ss[:, b:b + 1]),
                         reads=[B_h[b]], writes=[Bj, Bss])
                S.op("act", lambda e: e.activation(out=rstd, in_=ss, func=AF.Ln, scale=1.0 / D, bias=EPS), reads=[Bss], writes=[Bss])
                S.op("act", lambda e: e.activation(out=rstd, in_=rstd, func=AF.Exp, scale=-0.5), reads=[Bss], writes=[Bss])
            for b in range(NB):
                t0 = ti * TT + b * 128
                if do_norm:
                    o_ = ot[b % 2]
                    S.op("dve", lambda e, b=b, o_=o_: e.scalar_tensor_tensor(out=o_, in0=h_t[:, b, :], scalar=rstd[:, b:b + 1],
                                                                            in1=nfin_bc[:], op0=ALU.mult, op1=ALU.mult),
                         reads=[B_h[b], Bss, B_const], writes=[Bot[b % 2]])
                    S.dma(lambda e, o_=o_, t0=t0: e.dma_start(out=out_d[t0:t0 + 128, :], in_=o_), reads=[Bot[b % 2]], final=True)
                else:
                    S.dma(lambda e, b=b, t0=t0: e.dma_start(out=out_d[t0:t0 + 128, :], in_=h_t[:, b, :]), reads=[B_h[b]], final=True)
            S.barrier()

        const_setup()
        S.barrier()
        if mode != "tile":
            weight_prep()
        if mode != "prep":
            stages = [s_ for s_ in ["delta", "ffn0", "ple0", "swa", "ffn1", "ple1"] if s_ in STAGES]
            for ti in range(n_tiles):
                for b in range(NB):
                    t0 = ti * TT + b * 128
                    S.dma(lambda e, b=b, t0=t0: e.dma_start(out=h_t[:, b, :], in_=x_d[t0:t0 + 128, :]), writes=[B_h[b]])
                done = False
                for sname in stages:
                    if stop_after == "prep":
                        done = True
                        break
                    if sname == "delta":
                        deltanet(ti)
                    elif sname == "ffn0":
                        conv_ffn(0)
                    elif sname == "ple0":
                        ple(0, ti)
                    elif sname == "swa":
                        swa(ti)
                    elif sname == "ffn1":
                        conv_ffn(1)
                    elif sname == "ple1":
                        ple(1, ti)
                    if stop_after == sname:
                        done = True
                        break
                final_out(ti, do_norm=not done)
            if mode == "tile":
                S.barrier()
                for (t_, dst_) in ((halo_a, ha_out), (halo_f, hf_out), (S_st, S_out), (kT_prev, kp_out), (vx_prev, vp_out)):
                    S.dma(lambda e, t_=t_, dst_=dst_: e.dma_start(out=dst_, in_=t_[:]), final=True)
        S.emit()
    return nc


_PREP_IN = ["norm_mix", "norm_ffn", "norm_ple", "a_w_in", "a_norm", "a_w_out", "b_w_in", "b_w_out",
            "f_w_up", "f_w_down", "ple_w_proj", "ple_w_gate"]
_TILE_PARAMS = ["norm_final", "a_conv", "a_log", "a_dt_bias", "b_sinks", "f_conv"]
_SQUEEZE = {"a_w_in", "a_conv", "a_log", "a_dt_bias", "a_norm", "a_w_out", "b_w_in", "b_sinks", "b_w_out"}
_WB = ["Wa_in", "Wa_ba", "Wa_out", "Wb_in", "Wb_out", "Wf_up0", "Wf_up1", "Wf_dn0", "Wf_dn1",
       "Wp_pr0", "Wp_pr1", "Wp_gt0", "Wp_gt1"]


def _prm(inputs, n):
    a = np.asarray(inputs[n], dtype=np.float32)
    if n in _SQUEEZE:
        a = a[0]
    return np.ascontiguousarray(a)


def make_in_maps(inputs, n_cores=8):
    shared = {n: _prm(inputs, n) for n in _PREP_IN + _TILE_PARAMS}
    x = np.asarray(inputs["x"], dtype=np.float32)
    p = np.asarray(inputs["p"], dtype=np.float32)
    maps = []
    for c in range(n_cores):
        m = dict(shared)
        m["x"] = np.ascontiguousarray(x[c])
        m["p"] = np.ascontiguousarray(p[:, c])
        maps.append(m)
    return maps


def kernel_tiles(inputs, n_cores=8, n_tiles=NT):
    import ml_dtypes
    bf = ml_dtypes.bfloat16
    nc_p = build_program(mode="prep")
    res = run_bass_kernel_spmd(nc_p, [{n: _prm(inputs, n) for n in _PREP_IN}], core_ids=[0])
    wb = {n: np.asarray(res.results[0][n]) for n in _WB}
    params = {n: _prm(inputs, n) for n in _TILE_PARAMS}
    x = np.asarray(inputs["x"], dtype=np.float32)
    p = np.asarray(inputs["p"], dtype=np.float32)
    nc_t = build_program(mode="tile")
    st = [{"S_in": np.zeros((128, 8, 128), np.float32), "ha_in": np.zeros((128, 24, 3), np.float32),
           "hf_in": np.zeros((128, 2, 44, 2), np.float32), "kp_in": np.zeros((128, 2, 4, 128), bf),
           "vp_in": np.zeros((128, 4, 72), bf)} for _ in range(n_cores)]
    out = np.zeros((n_cores, n_tiles * TT, D), np.float32)
    for ti in range(n_tiles):
        flag = np.full((128, 1), -BIG if ti == 0 else 0.0, np.float32)
        maps = []
        for c in range(n_cores):
            m = dict(wb)
            m.update(params)
            m.update(st[c])
            m["flag_in"] = flag
            m["x"] = np.ascontiguousarray(x[c, ti * TT:(ti + 1) * TT])
            m["p"] = np.ascontiguousarray(p[:, c, ti * TT:(ti + 1) * TT])
            maps.append(m)
        r = run_bass_kernel_spmd(nc_t, maps, core_ids=list(range(n_cores)))
        for c in range(n_cores):
            rc = r.results[c]
            out[c, ti * TT:(ti + 1) * TT] = np.asarray(rc["out"], dtype=np.float32)
            st[c] = {"S_in": np.asarray(rc["S_out"]), "ha_in": np.asarray(rc["ha_out"]), "hf_in": np.asarray(rc["hf_out"]),
                     "kp_in": np.asarray(rc["kp_out"]), "vp_in": np.asarray(rc["vp_out"])}
    return out


def kernel(**inputs):
    nc = build_program(mode="fused")
    maps = make_in_maps(inputs)
    res = run_bass_kernel_spmd(nc, maps, core_ids=list(range(8)))
    return np.stack([np.asarray(r["out"], dtype=np.float32) for r in res.results], axis=0)
```

```python
import contextlib
import numpy as np
import concourse.bass as bass
import concourse.mybir as mybir
from concourse.bass_utils import run_bass_kernel_spmd
from concourse.alu_op_type import AluOpType as ALU

F32 = mybir.dt.float32
BF16 = mybir.dt.bfloat16
F32R = mybir.dt.float32r
AF = mybir.ActivationFunctionType

T_SEQ = 4096
D = 1024
TT = 512
NB = TT // 128
NT = T_SEQ // TT
DFF = 2816
NFF = DFF // 128
EPS = 1e-6
BIG = 30000.0
import os
DBG = float(os.environ.get('K_DBG', '99'))
STAGES = os.environ.get('K_STAGES', 'delta,ffn0,ple0,swa,ffn1,ple1').split(',')


class Buf:
    __slots__ = ("name", "w", "r", "excl")

    def __init__(self, name="", excl=False):
        self.name = name
        self.w = None
        self.r = []
        self.excl = excl


class Sched:
    ENGS = ("pe", "act", "dve", "pool", "sp")

    def __init__(self, nc, n_dma_sems=32, same_engine_sync=True):
        self.nc = nc
        self.ops = {e: [] for e in self.ENGS}
        self.count = {e: 0 for e in self.ENGS}
        self.waited = {e: {} for e in self.ENGS}
        self.pending = {e: [] for e in self.ENGS}
        self.n_dma = n_dma_sems
        self.dma_uses = [0] * n_dma_sems
        self.dma_next = 0
        self.same_engine_sync = same_engine_sync
        self.final_tokens = []

    def _deps(self, reads, writes, eng=None):
        deps = []
        for b in reads:
            if b.w is not None:
                deps.append(b.w)
            if b.excl:
                deps.extend(t for t in b.r if t[0] != eng)
        for b in writes:
            if b.w is not None:
                deps.append(b.w)
            deps.extend(b.r)
        return deps

    def _commit(self, tok, reads, writes):
        for b in reads:
            b.r.append(tok)
            if len(b.r) > 48:
                best = {}
                for k, v in b.r:
                    if best.get(k, 0) < v:
                        best[k] = v
                b.r = list(best.items())
        for b in writes:
            b.w = tok
            b.r = []

    def _waits(self, eng, deps):
        out = []
        wd = self.waited[eng]
        best = {}
        for k, v in deps:
            if best.get(k, 0) < v:
                best[k] = v
        for k, v in best.items():
            if k == eng and (eng == "pe" or not self.same_engine_sync):
                continue
            if wd.get(k, 0) >= v:
                continue
            wd[k] = v
            out.append((k, v))
        return out

    def op(self, eng, fn, reads=(), writes=()):
        deps = self._deps(reads, writes, eng)
        if self.pending[eng]:
            deps = deps + self.pending[eng]
            self.pending[eng] = []
        waits = self._waits(eng, deps)
        self.count[eng] += 1
        tok = (eng, self.count[eng])
        self.ops[eng].append((waits, fn, eng, 1))
        self._commit(tok, reads, writes)
        return tok

    def dma(self, fn, reads=(), writes=(), queue="sp", final=False):
        deps = self._deps(reads, writes)
        if self.pending[queue]:
            deps = deps + self.pending[queue]
            self.pending[queue] = []
        slot = self.dma_next
        self.dma_next = (self.dma_next + 1) % self.n_dma
        key = ("dma", slot)
        if self.dma_uses[slot] > 0:
            deps.append((key, 16 * self.dma_uses[slot]))
        waits = self._waits(queue, deps)
        self.dma_uses[slot] += 1
        tok = (key, 16 * self.dma_uses[slot])
        self.ops[queue].append((waits, fn, key, 16))
        self._commit(tok, reads, writes)
        if final:
            self.final_tokens.append(tok)
        return tok

    def barrier(self):
        toks = [(e, self.count[e]) for e in self.ENGS if self.count[e] > 0 and e != "sp"]
        for i in range(self.n_dma):
            if self.dma_uses[i] > 0:
                toks.append((("dma", i), 16 * self.dma_uses[i]))
        for e in self.ENGS:
            self.pending[e] = self.pending[e] + toks

    def emit(self):
        nc = self.nc
        with contextlib.ExitStack() as st:
            sems = {}
            for e in self.ENGS:
                sems[e] = st.enter_context(nc.semaphore("s_" + e))
            for i in range(self.n_dma):
                sems[("dma", i)] = st.enter_context(nc.semaphore("s_dma%d" % i))
            fw = self._waits("sp", list(self.final_tokens))
            block = st.enter_context(nc.Block())

            def run(engname, engobj, extra_waits=()):
                for waits, fn, inc_key, inc_val in self.ops[engname]:
                    for k, v in waits:
                        engobj.wait_ge(sems[k], v)
                    ins = fn(engobj)
                    ins.then_inc(sems[inc_key], inc_val)
                for k, v in extra_waits:
                    engobj.wait_ge(sems[k], v)

            @block.tensor
            def _(e):
                run("pe", e)

            @block.scalar
            def _(e):
                run("act", e)

            @block.vector
            def _(e):
                run("dve", e)

            @block.gpsimd
            def _(e):
                run("pool", e)

            @block.sync
            def _(e):
                run("sp", e, fw)


class Arena:
    def __init__(self, ap, ncols):
        self.ap = ap
        self.n = ncols
        self.pos = 0

    def reset(self):
        self.pos = 0

    def alloc(self, cols):
        cols_al = (cols + 7) // 8 * 8
        assert self.pos + cols_al <= self.n, ("arena overflow", self.pos, cols, self.n)
        a = self.ap[:, self.pos:self.pos + cols]
        self.pos += cols_al
        return a


def bc3(ap2, n):
    return ap2.rearrange("p (h o) -> p h o", o=1).to_broadcast([ap2.shape[0], ap2.shape[1], n])


def v3(ap, inner):
    return ap.rearrange("p (a b) -> p a b", b=inner)


def build_program(stop_after=None, n_tiles=NT, mode="fused"):
    nc = bass.Bass("TRN2", target_bir_lowering=False)
    if mode == "tile":
        n_tiles = 1
    T_LOC = T_SEQ if mode == "fused" else TT

    def din(name, shape, dt=F32):
        return nc.dram_tensor(name, shape, dt, kind="ExternalInput").ap()

    def dout(name, shape, dt=F32):
        return nc.dram_tensor(name, shape, dt, kind="ExternalOutput").ap()

    if mode != "prep":
        x_d = din("x", [T_LOC, D])
        p_d = din("p", [2, T_LOC, 256])
        norm_final = din("norm_final", [D])
        a_conv = din("a_conv", [4, 3072])
        a_log = din("a_log", [8])
        a_dt_bias = din("a_dt_bias", [8])
        b_sinks = din("b_sinks", [16])
        f_conv = din("f_conv", [2, 3, 2 * DFF])
        out_d = dout("out", [T_LOC, D])
    if mode != "tile":
        norm_mix = din("norm_mix", [2, D])
        norm_ffn = din("norm_ffn", [2, D])
        norm_ple = din("norm_ple", [2, D])
        a_w_in = din("a_w_in", [D, 4112])
        a_norm = din("a_norm", [128])
        a_w_out = din("a_w_out", [D, D])
        b_w_in = din("b_w_in", [D, 1536])
        b_w_out = din("b_w_out", [D, D])
        f_w_up = din("f_w_up", [2, D, 2 * DFF])
        f_w_down = din("f_w_down", [2, DFF, D])
        ple_w_proj = din("ple_w_proj", [2, 256, D])
        ple_w_gate = din("ple_w_gate", [2, D, D])

    def dscr(name, shape):
        if mode == "prep":
            return dout(name, shape, BF16)
        if mode == "tile":
            return din(name, shape, BF16)
        return nc.dram_tensor(name, shape, BF16).ap()

    Wa_in = dscr("Wa_in", [D, 4096])
    Wa_ba = dscr("Wa_ba", [D, 16])
    Wa_out = dscr("Wa_out", [D, D])
    Wb_in = dscr("Wb_in", [D, 1024 + 512 + 256])
    Wb_out = dscr("Wb_out", [D, D])
    Wf_up = [dscr("Wf_up%d" % l, [D, 2 * DFF]) for l in range(2)]
    Wf_dn = [dscr("Wf_dn%d" % l, [DFF, D]) for l in range(2)]
    Wp_pr = [dscr("Wp_pr%d" % l, [256, D]) for l in range(2)]
    Wp_gt = [dscr("Wp_gt%d" % l, [D, D]) for l in range(2)]
    if mode == "tile":
        S_in = din("S_in", [128, 8, 128]); S_out = dout("S_out", [128, 8, 128])
        ha_in = din("ha_in", [128, 24, 3]); ha_out = dout("ha_out", [128, 24, 3])
        hf_in = din("hf_in", [128, 2, 44, 2]); hf_out = dout("hf_out", [128, 2, 44, 2])
        kp_in = din("kp_in", [128, 2, 4, 128], BF16); kp_out = dout("kp_out", [128, 2, 4, 128], BF16)
        vp_in = din("vp_in", [128, 4, 72], BF16); vp_out = dout("vp_out", [128, 4, 72], BF16)
        flag_in = din("flag_in", [128, 1])

    S = Sched(nc)
    st = contextlib.ExitStack()
    with st:
        def sb(name, shape, dt):
            return st.enter_context(nc.sbuf_tensor(name, shape, dt))

        h_t = sb("h", [128, NB, D], F32)
        hnT = sb("hnT", [128, 8, TT], BF16)
        identb = sb("identb", [128, 128], BF16)
        identf = sb("identf", [128, 128], F32)
        onesf = sb("onesf", [128, 128], F32)
        triU = sb("triU", [128, 128], F32)
        mposS = sb("mposS", [128, 128], F32)
        mnegI = sb("mnegI", [128, 128], F32)
        nrm = sb("nrm", [128, 7, 8], F32)
        anorm_c = sb("anorm_c", [128, 1], F32)
        nfin_bc = sb("nfin_bc", [128, D], F32)
        aconv_c = sb("aconv_c", [128, 24, 4], F32)
        fconv_c = sb("fconv_c", [128, 2, 44, 3], F32)
        negA_bc = sb("negA_bc", [128, 8], F32)
        dtb_bc = sb("dtb_bc", [128, 8], F32)
        esink_bc = sb("esink_bc", [128, 16], F32)
        halo_a = sb("halo_a", [128, 24, 3], F32)
        halo_f = sb("halo_f", [128, 2, 44, 2], F32)
        S_st = sb("S_st", [128, 8, 128], F32)
        Sb_st = sb("Sb_st", [128, 8, 128], BF16)
        kT_prev = sb("kT_prev", [128, 2, 4, 128], BF16)
        vx_prev = sb("vx_prev", [128, 4, 72], BF16)
        swab = sb("swab", [128, 4, 2, 512], F32)
        wst = [sb("wst%d" % i, [128, 8, 256], BF16) for i in range(3)]
        wout_t = sb("wout_t", [128, 8, D], BF16)
        smalls = sb("smalls", [128, 512], F32)
        arena_f_t = sb("arena_f", [128, 8960], F32)
        neu_t = sb("neu", [128, 4, 6, 128], F32R)
        arena_b_t = sb("arena_b", [128, 31488], BF16)
        AFa = Arena(arena_f_t[:], 8960)
        ABa = Arena(arena_b_t[:], 31488)

        pbank = [st.enter_context(nc.psum_tensor("pb%d" % i, [128, 512], F32)) for i in range(6)]
        pbankh = [st.enter_context(nc.psum_tensor("pbh%d" % i, [128, 1024], BF16)) for i in range(2)]
        pbank_bf = [pbank[i].bitcast(BF16) for i in range(6)]
        PB = [Buf("pb%d" % i, excl=True) for i in range(6)]
        PBH = [Buf("pbh%d" % i, excl=True) for i in range(2)]

        B_h = [Buf("h%d" % b) for b in range(NB)]
        B_hnT = Buf("hnT")
        B_const = Buf("const")
        B_wst = [Buf("wst%d" % i) for i in range(3)]
        B_wout = Buf("wout")
        B_S = [Buf("S%d" % i) for i in range(8)]
        B_Sb = [Buf("Sb%d" % i) for i in range(8)]
        B_halo_a = [Buf("haloa%d" % i) for i in range(24)]
        B_halo_f = [[Buf("halof%d_%d" % (l, i)) for i in range(44)] for l in range(2)]
        B_kprev = Buf("kprev")
        B_vprev = Buf("vprev")
        wst_rr = [0]

        flag_t = sb("flag_t", [128, 1], F32)

        def const_setup():
            rd, wr = [B_const], [B_const]

            def ld(out_ap, in_ap):
                S.dma(lambda e: e.dma_start(out=out_ap, in_=in_ap, allow_slow_non_contiguous=True), writes=wr)
            if mode != "tile":
                for i, (src, l) in enumerate([(norm_mix, 0), (norm_mix, 1), (norm_ffn, 0), (norm_ffn, 1),
                                              (norm_ple, 0), (norm_ple, 1)]):
                    ld(nrm[:, i, :], src[l].rearrange("(k p) -> p k", p=128))
                ld(anorm_c[:], a_norm.rearrange("(p o) -> p o", o=1))
            if mode == "prep":
                return
            S.op("pool", lambda e: e.memset(identf[:], 0.0), writes=wr)
            S.op("pool", lambda e: e.affine_select(out=identf[:], in_=identf[:], pattern=[[-1, 128]],
                                                   compare_op=ALU.not_equal, fill=1.0, base=0, channel_multiplier=1),
                 reads=rd, writes=wr)
            S.op("pool", lambda e: e.tensor_copy(out=identb[:], in_=identf[:]), reads=rd, writes=wr)
            S.op("pool", lambda e: e.memset(onesf[:], 1.0), writes=wr)
            S.op("pool", lambda e: e.affine_select(out=triU[:], in_=onesf[:], pattern=[[1, 128]],
                                                   compare_op=ALU.is_ge, fill=0.0, base=0, channel_multiplier=-1),
                 reads=rd, writes=wr)
            S.op("pool", lambda e: e.memset(mposS[:], 0.0), writes=wr)
            S.op("pool", lambda e: e.affine_select(out=mposS[:], in_=mposS[:], pattern=[[-1, 128]],
                                                   compare_op=ALU.is_gt, fill=BIG, base=0, channel_multiplier=1),
                 reads=rd, writes=wr)
            S.op("pool", lambda e: e.memset(mnegI[:], 0.0), writes=wr)
            S.op("pool", lambda e: e.affine_select(out=mnegI[:], in_=mnegI[:], pattern=[[1, 128]],
                                                   compare_op=ALU.is_ge, fill=-BIG, base=0, channel_multiplier=-1),
                 reads=rd, writes=wr)
            if mode == "tile":
                for (t_, src_) in ((halo_a, ha_in), (halo_f, hf_in), (S_st, S_in), (kT_prev, kp_in), (vx_prev, vp_in),
                                   (flag_t, flag_in)):
                    S.dma(lambda e, t_=t_, src_=src_: e.dma_start(out=t_[:], in_=src_), writes=wr)
                S.op("pool", lambda e: e.tensor_copy(out=Sb_st[:], in_=S_st[:]), reads=rd, writes=wr)
            else:
                for t_ in (halo_a, halo_f, S_st):
                    S.op("pool", lambda e, t_=t_: e.memset(t_[:], 0.0), writes=wr)
                S.op("pool", lambda e: e.memset(Sb_st[:], 0.0), writes=wr)
                S.op("pool", lambda e: e.memset(kT_prev[:], 0.0), writes=wr)
                S.op("pool", lambda e: e.memset(vx_prev[:], 0.0), writes=wr)
                S.op("pool", lambda e: e.memset(flag_t[:], 0.0), writes=wr)
            for g in range(4):
                for hh in range(4):
                    head = 4 * g + hh
                    slope = 2.0 ** (-8.0 * (head + 1) / 16.0)
                    for pc in range(2):
                        dst = swab[:, g, pc, hh * 128:(hh + 1) * 128]
                        S.op("pool", lambda e, dst=dst, pc=pc: e.iota(dst, [[1, 128]], base=(128 if pc == 0 else 0),
                                                                      channel_multiplier=-1,
                                                                      allow_small_or_imprecise_dtypes=True),
                             reads=rd, writes=wr)
                        S.op("pool", lambda e, dst=dst, slope=slope: e.tensor_scalar(out=dst, in0=dst, scalar1=-slope,
                                                                                    scalar2=None, op0=ALU.mult),
                             reads=rd, writes=wr)
                        if pc == 0:
                            S.op("pool", lambda e, dst=dst: e.affine_select(out=dst, in_=dst, pattern=[[-1, 128]],
                                                                            compare_op=ALU.is_gt, fill=-BIG, base=0,
                                                                            channel_multiplier=1),
                                 reads=rd, writes=wr)
                        else:
                            S.op("pool", lambda e, dst=dst: e.affine_select(out=dst, in_=dst, pattern=[[1, 128]],
                                                                            compare_op=ALU.is_ge, fill=-BIG, base=0,
                                                                            channel_multiplier=-1),
                                 reads=rd, writes=wr)
            ld(nfin_bc[:], bass.AP(norm_final.tensor, 0, [[0, 128], [1, D]]))
            for i in range(4):
                ld(aconv_c[:, :, i], a_conv[i].rearrange("(c p) -> p c", p=128))
            for l in range(2):
                for i in range(3):
                    ld(fconv_c[:, l, :, i], f_conv[l, i].rearrange("(c p) -> p c", p=128))
            ld(negA_bc[:], bass.AP(a_log.tensor, 0, [[0, 128], [1, 8]]))
            ld(dtb_bc[:], bass.AP(a_dt_bias.tensor, 0, [[0, 128], [1, 8]]))
            ld(esink_bc[:], bass.AP(b_sinks.tensor, 0, [[0, 128], [1, 16]]))
            S.op("act", lambda e: e.activation(out=negA_bc[:], in_=negA_bc[:], func=AF.Exp), reads=rd, writes=wr)
            S.op("dve", lambda e: e.tensor_scalar(out=negA_bc[:], in0=negA_bc[:], scalar1=-1.0, scalar2=None,
                                                  op0=ALU.mult), reads=rd, writes=wr)
            S.op("act", lambda e: e.activation(out=esink_bc[:], in_=esink_bc[:], func=AF.Exp), reads=rd, writes=wr)

        def weight_prep():
            AFa.reset(); ABa.reset()
            nst = 4
            PW = 2048
            stg_f = [AFa.alloc(PW) for _ in range(nst)]
            stg_b = [ABa.alloc(PW) for _ in range(nst)]
            Bf = [Buf("stgf%d" % i) for i in range(nst)]
            Bb = [Buf("stgb%d" % i) for i in range(nst)]
            engs = ["dve", "act"]
            pieces = []

            def conv_mat(src, dst, nrows, c0, ncols, d0, scale_fn):
                for k in range(nrows // 128):
                    for cc in range(0, ncols, PW):
                        n = min(PW, ncols - cc)
                        pieces.append((src[k * 128:(k + 1) * 128, c0 + cc:c0 + cc + n],
                                       dst[k * 128:(k + 1) * 128, d0 + cc:d0 + cc + n], n,
                                       scale_fn(k) if scale_fn else None))

            def emit_pieces():
                LA = nst - 1
                npc = len(pieces)
                for t in range(npc + LA):
                    if t < npc:
                        src_ap, _, ncols, _ = pieces[t]
                        i = t % nst
                        S.dma(lambda e, i=i, ncols=ncols, src_ap=src_ap: e.dma_start(out=stg_f[i][:, 0:ncols], in_=src_ap),
                              writes=[Bf[i]])
                    u = t - LA
                    if u >= 0:
                        _, dst_ap, ncols, scale_ap = pieces[u]
                        i = u % nst
                        sf, sbb = stg_f[i][:, 0:ncols], stg_b[i][:, 0:ncols]
                        eng = engs[u % 2]
                        rds = [Bf[i]] + ([B_const] if scale_ap is not None else [])
                        if eng == "act":
                            if scale_ap is None:
                                S.op("act", lambda e, sf=sf, sbb=sbb: e.activation(out=sbb, in_=sf, func=AF.Copy), reads=rds, writes=[Bb[i]])
                            else:
                                S.op("act", lambda e, sf=sf, sbb=sbb, scale_ap=scale_ap: e.activation(out=sbb, in_=sf, func=AF.Copy, scale=scale_ap),
                                     reads=rds, writes=[Bb[i]])
                        else:
                            if scale_ap is None:
                                S.op("dve", lambda e, sf=sf, sbb=sbb: e.tensor_copy(out=sbb, in_=sf), reads=rds, writes=[Bb[i]])
                            else:
                                S.op("dve", lambda e, sf=sf, sbb=sbb, scale_ap=scale_ap: e.tensor_scalar(out=sbb, in0=sf, scalar1=scale_ap,
                                                                                                      scalar2=None, op0=ALU.mult),
                                     reads=rds, writes=[Bb[i]])
                        S.dma(lambda e, dst_ap=dst_ap, sbb=sbb: e.dma_start(out=dst_ap, in_=sbb), reads=[Bb[i]], final=(mode == "prep"))

            conv_mat(a_w_in, Wa_in, D, 0, 4096, 0, lambda k: nrm[:, 0, k:k + 1])
            conv_mat(a_w_in, Wa_ba, D, 4096, 16, 0, lambda k: nrm[:, 0, k:k + 1])
            conv_mat(a_w_out, Wa_out, D, 0, D, 0, lambda k: anorm_c[:, 0:1])
            conv_mat(b_w_in, Wb_in, D, 0, 1024, 0, lambda k: nrm[:, 1, k:k + 1])
            for g in range(4):
                for dup in range(2):
                    conv_mat(b_w_in, Wb_in, D, 1024 + g * 64, 64, 1024 + g * 128 + dup * 64, lambda k: nrm[:, 1, k:k + 1])
            conv_mat(b_w_in, Wb_in, D, 1280, 256, 1536, lambda k: nrm[:, 1, k:k + 1])
            conv_mat(b_w_out, Wb_out, D, 0, D, 0, None)
            for l in range(2):
                conv_mat(f_w_up[l], Wf_up[l], D, 0, 2 * DFF, 0, lambda k, l=l: nrm[:, 2 + l, k:k + 1])
                conv_mat(f_w_down[l], Wf_dn[l], DFF, 0, D, 0, None)
                conv_mat(ple_w_proj[l], Wp_pr[l], 256, 0, D, 0, None)
                conv_mat(ple_w_gate[l], Wp_gt[l], D, 0, D, 0, lambda k, l=l: nrm[:, 4 + l, k:k + 1])
            emit_pieces()
            S.barrier()

        def load_wst(Wd, c0, ncols):
            i = wst_rr[0] % 3
            wst_rr[0] += 1
            dst = wst[i][:, :, 0:ncols]
            src = Wd[:, c0:c0 + ncols].rearrange("(k p) n -> p k n", p=128)
            S.dma(lambda e: e.dma_start(out=dst, in_=src), writes=[B_wst[i]])
            return i

        def norm_transpose():
            ss = smalls[:, 0:NB]
            rstd = smalls[:, 8:8 + NB]
            Bss = Buf("ss")
            junk = ABa.alloc(D)
            Bj = Buf("junk")
            xs = [ABa.alloc(D) for _ in range(2)]
            Bxs = [Buf("xs0"), Buf("xs1")]
            for b in range(NB):
                S.op("act", lambda e, b=b: e.activation(out=junk, in_=h_t[:, b, :], func=AF.Square,
                                                        accum_out=ss[:, b:b + 1]),
                     reads=[B_h[b]], writes=[Bj, Bss])
            S.op("act", lambda e: e.activation(out=rstd, in_=ss, func=AF.Ln, scale=1.0 / D, bias=EPS),
                 reads=[Bss], writes=[Bss])
            S.op("act", lambda e: e.activation(out=rstd, in_=rstd, func=AF.Exp, scale=-0.5), reads=[Bss], writes=[Bss])
            for b in range(NB):
                x_ = xs[b % 2]
                S.op("dve", lambda e, b=b, x_=x_: e.tensor_scalar(out=x_, in0=h_t[:, b, :], scalar1=rstd[:, b:b + 1],
                                                                  scalar2=None, op0=ALU.mult),
                     reads=[B_h[b], Bss], writes=[Bxs[b % 2]])
                pt = pbankh[b % 2]
                for k in range(8):
                    S.op("pe", lambda e, k=k, x_=x_, pt=pt: e.transpose(out=pt[:, k * 128:(k + 1) * 128],
                                                                        in_=x_[:, k * 128:(k + 1) * 128],
                                                                        identity=identb[:]),
                         reads=[Bxs[b % 2], B_const], writes=[PBH[b % 2]])
                eng = "dve" if b % 2 == 0 else "act"
                dst = hnT[:, :, b * 128:(b + 1) * 128]
                src = v3(pt[:, :], 128)
                if eng == "dve":
                    S.op("dve", lambda e, dst=dst, src=src: e.tensor_copy(out=dst, in_=src),
                         reads=[PBH[b % 2]], writes=[B_hnT])
                else:
                    S.op("act", lambda e, dst=dst, src=src: e.activation(out=dst, in_=src, func=AF.Copy),
                         reads=[PBH[b % 2]], writes=[B_hnT])

        def proj_fm(Wd, c0, nchunks, consume, banks=(0, 1, 2, 3)):
            per = 2
            bi = 0
            for cb in range(0, nchunks, per):
                n = min(per, nchunks - cb)
                wi = load_wst(Wd, c0 + cb * 128, n * 128)
                for j in range(n):
                    bk = banks[bi % len(banks)]
                    bi += 1
                    for k in range(8):
                        S.op("pe", lambda e, k=k, j=j, wi=wi, bk=bk: e.matmul(
                            pbank[bk][:, :], lhsT=wst[wi][:, k, j * 128:(j + 1) * 128], rhs=hnT[:, k, :],
                            start=(k == 0), stop=(k == 7)),
                            reads=[B_wst[wi], B_hnT], writes=[PB[bk]])
                    consume(cb + j, pbank[bk], PB[bk])

        def resid_proj(lhs_fn, nk, w_fn, w_bufs, lhs_bufs, banks=(4, 5)):
            for b in range(NB):
                for half in range(2):
                    bk = banks[half]
                    for k in range(nk):
                        S.op("pe", lambda e, b=b, k=k, half=half, bk=bk: e.matmul(
                            pbank[bk][:, :], lhsT=lhs_fn(k, b), rhs=w_fn(k, half), start=(k == 0), stop=(k == nk - 1)),
                            reads=list(w_bufs) + list(lhs_bufs), writes=[PB[bk]])
                    hs = h_t[:, b, half * 512:(half + 1) * 512]
                    S.op("dve", lambda e, hs=hs, bk=bk: e.tensor_tensor(out=hs, in0=pbank[bk][:, :], in1=hs, op=ALU.add),
                         reads=[PB[bk], B_h[b]], writes=[B_h[b]])

        def conv_from_psum(items, KW, act_fix=False):
            for it in items:
                S.op("act", lambda e, it=it: e.activation(out=it["ac"], in_=it["ps"][:, 0:TT], func=AF.Copy, scale=it["w"](KW - 1)),
                     reads=[it["Bps"], B_const], writes=[it["Bac"]])
            if act_fix:
                for j in range(KW - 1):
                    s_ = KW - 1 - j
                    for col in range(s_):
                        for it in items:
                            S.op("act", lambda e, it=it, j=j, col=col: e.activation(
                                out=it["ac"][:, col:col + 1], in_=it["halo"][:, j + col:j + col + 1], func=AF.Identity,
                                scale=it["w"](j), bias=it["ac"][:, col:col + 1]),
                                reads=[it["Bhalo"], it["Bac"], B_const], writes=[it["Bac"]])
                for it in items:
                    S.op("act", lambda e, it=it: e.activation(out=it["halo"], in_=it["ps"][:, TT - (KW - 1):TT], func=AF.Copy),
                         reads=[it["Bps"]], writes=[it["Bhalo"]])
            for j in range(KW - 1):
                s_ = KW - 1 - j
                for it in items:
                    S.op("dve", lambda e, it=it, j=j, s_=s_: e.scalar_tensor_tensor(
                        out=it["ac"][:, s_:TT], in0=it["ps"][:, 0:TT - s_], scalar=it["w"](j), in1=it["ac"][:, s_:TT],
                        op0=ALU.mult, op1=ALU.add), reads=[it["Bps"], it["Bac"], B_const], writes=[it["Bac"]])
            if act_fix:
                return
            for j in range(KW - 1):
                s_ = KW - 1 - j
                for it in items:
                    S.op("dve", lambda e, it=it, j=j, s_=s_: e.scalar_tensor_tensor(
                        out=it["ac"][:, 0:s_], in0=it["halo"][:, j:KW - 1], scalar=it["w"](j), in1=it["ac"][:, 0:s_],
                        op0=ALU.mult, op1=ALU.add), reads=[it["Bhalo"], it["Bac"], B_const], writes=[it["Bac"]])
            for it in items:
                S.op("dve", lambda e, it=it: e.tensor_copy(out=it["halo"], in_=it["ps"][:, TT - (KW - 1):TT]),
                     reads=[it["Bps"]], writes=[it["Bhalo"]])

        def deltanet(ti):
            AFa.reset(); ABa.reset()
            norm_transpose()
            qkvT = v3(ABa.alloc(24 * TT), TT)
            zsT = v3(ABa.alloc(8 * TT), TT)
            ogT = v3(ABa.alloc(8 * TT), TT)
            B_qkv = [Buf("qkv%d" % c) for c in range(24)]
            B_zs = Buf("zs")
            B_og = [Buf("og%d" % b) for b in range(NB)]
            ba = v3(AFa.alloc(NB * 16), 16)
            B_ba = Buf("ba")
            pre = [AFa.alloc(TT + 3) for _ in range(3)]
            acc = [AFa.alloc(TT) for _ in range(3)]
            Bpre = [Buf("pre%d" % i) for i in range(3)]
            Bacc = [Buf("acc%d" % i) for i in range(3)]

            S.dma(lambda e: e.dma_start(out=wout_t[:], in_=Wa_out.rearrange("(k p) n -> p k n", p=128)), writes=[B_wout])

            def consume(c, ps, Bps):
                if c < 24:
                    i = c % 3
                    ac = acc[i]
                    conv_from_psum([dict(ps=ps, Bps=Bps, w=(lambda j, c=c: aconv_c[:, c, j:j + 1]), halo=halo_a[:, c, :],
                                         Bhalo=B_halo_a[c], ac=ac, Bac=Bacc[i])], 4)
                    S.op("act", lambda e: e.activation(out=qkvT[:, c, :], in_=ac, func=AF.Silu),
                         reads=[Bacc[i]], writes=[B_qkv[c]])
                else:
                    S.op("act", lambda e: e.activation(out=zsT[:, c - 24, :], in_=ps[:, :], func=AF.Silu),
                         reads=[Bps], writes=[B_zs])

            proj_fm(Wa_in, 0, 32, consume)
            if DBG <= 1:
                S.barrier(); return

            wba = ABa.alloc(8 * 16)
            wba3 = v3(wba, 16)
            B_wba = Buf("wba")
            S.dma(lambda e: e.dma_start(out=wba3, in_=Wa_ba.rearrange("(k p) n -> p k n", p=128)), writes=[B_wba])
            for b in range(NB):
                for k in range(8):
                    S.op("pe", lambda e, b=b, k=k: e.matmul(pbank[4][:, b * 16:(b + 1) * 16],
                                                            lhsT=hnT[:, k, b * 128:(b + 1) * 128], rhs=wba3[:, k, :],
                                                            start=(k == 0), stop=(k == 7)),
                         reads=[B_hnT, B_wba], writes=[PB[4]])
            S.op("dve", lambda e: e.tensor_copy(out=ba, in_=v3(pbank[4][:, 0:NB * 16], 16)), reads=[PB[4]], writes=[B_ba])

            if DBG <= 2:
                S.barrier(); return

            def hb(n, dt):
                return [(AFa if dt == F32 else ABa).alloc(n) for _ in range(4)]
            Kn = hb(128, BF16); kb = hb(128, BF16); kd = hb(128, BF16)
            Qn = hb(128, BF16); qd = hb(128, BF16); bv = hb(128, BF16)
            KQT = hb(384, BF16)
            qkTm = hb(128, BF16); TTb = hb(128, BF16); nwkT = hb(128, BF16); usb = hb(128, BF16)
            dg = hb(128, F32); tL = hb(128, F32); tU = hb(128, F32); EL = hb(128, F32); EU = hb(128, F32)
            PTa_r, Pa_r, PTb_r, Pb_r, Xa_r, Xb_r = [[neu_t[:, i_, kk_, :] for i_ in range(4)] for kk_ in range(6)]
            PPa_r = [neu_t[:, i_, 0:2, :] for i_ in range(4)]
            PPb_r = [neu_t[:, i_, 2:4, :] for i_ in range(4)]
            PTa = [t_.bitcast(F32) for t_ in PTa_r]
            names = "Kn kb kd Qn qd bv KQT qkTm TTb nwkT usb dg tL tU EL EU PTa PTb Pa Pb Xa Xb".split()
            BH = [{n: Buf(n + str(i)) for n in names} for i in range(4)]
            sq = v3(AFa.alloc(16 * 128), 128)
            B_sq = Buf("sq")
            on_all = v3(ABa.alloc(8 * 128), 128)
            B_on = Buf("on")
            junk2 = AFa.alloc(128)
            B_j2 = Buf("junk2")
            B_g = Buf("gate")
            B_sso = Buf("sso")
            NG = NB * 8
            Gall = AFa.alloc(14 * NG + NB * 16)

            def gq(i):
                return Gall[:, i * NG:(i + 1) * NG]
            (beta_a, nbeta_a, g_a, gcs_a, ngcs_a, Eg_a, Ekd_a, Egl_a, c_kb_a, c_kd_a, c_qn_a, c_qd_a, tmp1, tmp2) = [
                gq(i) for i in range(14)]
            rn_a = Gall[:, 14 * NG:14 * NG + NB * 16]
            rn3 = v3(rn_a, 16)
            rnq3, rnk3 = rn3[:, :, 0:8], rn3[:, :, 8:16]

            def q3(x):
                return v3(x, 8)
            sso = smalls[:, 64:72]
            rso = smalls[:, 72:80]
            scale_q = 128.0 ** -0.5
            gr, gw = [B_g], [B_g]
            dtb3 = dtb_bc[:].rearrange("p (o h) -> p o h", o=1).to_broadcast([128, NB, 8])
            negA3 = negA_bc[:].rearrange("p (o h) -> p o h", o=1).to_broadcast([128, NB, 8])
            S.op("act", lambda e: e.activation(out=q3(tmp1), in_=ba[:, :, 0:8], func=AF.Exp, scale=-1.0), reads=[B_ba], writes=gw)
            S.op("dve", lambda e: e.tensor_scalar(out=tmp1, in0=tmp1, scalar1=1.0, scalar2=None, op0=ALU.add), reads=gr, writes=gw)
            S.op("dve", lambda e: e.reciprocal(out=beta_a, in_=tmp1), reads=gr, writes=gw)
            S.op("dve", lambda e: e.tensor_scalar(out=nbeta_a, in0=beta_a, scalar1=-1.0, scalar2=None, op0=ALU.mult), reads=gr, writes=gw)
            S.op("dve", lambda e: e.tensor_tensor(out=q3(tmp2), in0=ba[:, :, 8:16], in1=dtb3, op=ALU.add),
                 reads=[B_ba, B_const] + gr, writes=gw)
            S.op("act", lambda e: e.activation(out=tmp2, in_=tmp2, func=AF.Exp), reads=gr, writes=gw)
            S.op("act", lambda e: e.activation(out=tmp2, in_=tmp2, func=AF.Ln, bias=1.0), reads=gr, writes=gw)
            S.op("dve", lambda e: e.tensor_tensor(out=q3(g_a), in0=q3(tmp2), in1=negA3, op=ALU.mult), reads=[B_const] + gr, writes=gw)
            for b in range(NB):
                S.op("pe", lambda e, b=b: e.matmul(pbank[5][:, b * 8:(b + 1) * 8], lhsT=triU[:], rhs=g_a[:, b * 8:(b + 1) * 8],
                                                   start=True, stop=True), reads=[B_const] + gr, writes=[PB[5]])
                S.op("pe", lambda e, b=b: e.matmul(pbank[5][:, NG + b * 8:NG + (b + 1) * 8], lhsT=onesf[:],
                                                   rhs=g_a[:, b * 8:(b + 1) * 8], start=True, stop=True),
                     reads=[B_const] + gr, writes=[PB[5]])
            for b in range(NB):
                blk = slice(b * 128, (b + 1) * 128)
                S.op("act", lambda e, blk=blk: e.activation(out=sq, in_=qkvT[:, 0:16, blk], func=AF.Square),
                     reads=B_qkv[0:16], writes=[B_sq])
                for c in range(16):
                    S.op("pe", lambda e, c=c, b=b: e.matmul(pbank[5][:, 2 * NG + b * 16 + c:2 * NG + b * 16 + c + 1], lhsT=sq[:, c, :],
                                                            rhs=onesf[:, 0:1], start=True, stop=True),
                         reads=[B_sq, B_const], writes=[PB[5]])
            psg, psl, pss = pbank[5][:, 0:NG], pbank[5][:, NG:2 * NG], pbank[5][:, 2 * NG:2 * NG + NB * 16]
            S.op("act", lambda e: e.activation(out=gcs_a, in_=psg, func=AF.Copy), reads=[PB[5]] + gr, writes=gw)
            S.op("act", lambda e: e.activation(out=ngcs_a, in_=psg, func=AF.Copy, scale=-1.0), reads=[PB[5]] + gr, writes=gw)
            S.op("act", lambda e: e.activation(out=Eg_a, in_=psg, func=AF.Exp), reads=[PB[5]] + gr, writes=gw)
            S.op("act", lambda e: e.activation(out=Egl_a, in_=psl, func=AF.Exp), reads=[PB[5]] + gr, writes=gw)
            S.op("act", lambda e: e.activation(out=rn_a, in_=pss, func=AF.Ln, bias=EPS), reads=[PB[5]] + gr, writes=gw)
            S.op("act", lambda e: e.activation(out=rn_a, in_=rn_a, func=AF.Exp, scale=-0.5), reads=gr, writes=gw)
            S.op("dve", lambda e: e.tensor_tensor(out=tmp1, in0=psl, in1=gcs_a, op=ALU.subtract), reads=[PB[5]] + gr, writes=gw)
            S.op("act", lambda e: e.activation(out=Ekd_a, in_=tmp1, func=AF.Exp), reads=gr, writes=gw)
            S.op("dve", lambda e: e.tensor_tensor(out=q3(tmp2), in0=rnk3, in1=q3(beta_a), op=ALU.mult), reads=gr, writes=gw)
            S.op("dve", lambda e: e.tensor_tensor(out=c_kb_a, in0=tmp2, in1=Eg_a, op=ALU.mult), reads=gr, writes=gw)
            S.op("dve", lambda e: e.tensor_tensor(out=q3(c_kd_a), in0=rnk3, in1=q3(Ekd_a), op=ALU.mult), reads=gr, writes=gw)
            S.op("dve", lambda e: e.tensor_scalar(out=q3(c_qn_a), in0=rnq3, scalar1=scale_q, scalar2=None, op0=ALU.mult),
                 reads=gr, writes=gw)
            S.op("dve", lambda e: e.tensor_tensor(out=c_qd_a, in0=c_qn_a, in1=Eg_a, op=ALU.mult), reads=gr, writes=gw)

            def do_block(b):
                blk = slice(b * 128, (b + 1) * 128)
                beta_, nbeta, gcs, ngcs, Egl, c_kb, c_kd, c_qn, c_qd = [x[:, b * 8:(b + 1) * 8] for x in (
                    beta_a, nbeta_a, gcs_a, ngcs_a, Egl_a, c_kb_a, c_kd_a, c_qn_a, c_qd_a)]
                rnk = rn_a[:, b * 16 + 8:b * 16 + 16]
                if DBG <= 3:
                    S.barrier(); return
                for grp in range(2):
                    heads = [grp * 4 + i for i in range(4)]
                    ob = 4 + grp
                    for i, hh in enumerate(heads):
                        S.op("act", lambda e, i=i, hh=hh: e.activation(out=dg[i], in_=identf[:], func=AF.Copy, scale=gcs[:, hh:hh + 1]),
                             reads=[B_const] + gr, writes=[BH[i]["dg"]])
                    for i, hh in enumerate(heads):
                        pt = pbankh[i // 2]
                        for j, c in enumerate((8 + hh, hh, 16 + hh)):
                            o_ = ((i % 2) * 3 + j) * 128
                            S.op("pe", lambda e, pt=pt, o_=o_, c=c, blk=blk: e.transpose(
                                out=pt[:, o_:o_ + 128], in_=qkvT[:, c, blk], identity=identb[:]),
                                reads=[B_qkv[c], B_const], writes=[PBH[i // 2]])
                    if DBG <= 3.3:
                        S.barrier(); return
                    for urgent in (True, False):
                        for i, hh in enumerate(heads):
                            pt = pbankh[i // 2]
                            o_ = (i % 2) * 384
                            Kt, Qt, Vt = pt[:, o_:o_ + 128], pt[:, o_ + 128:o_ + 256], pt[:, o_ + 256:o_ + 384]
                            B = BH[i]
                            beng = "act" if i // 2 == 0 else "dve"
                            lst = (((Kn[i], Kt, rnk, "Kn"), (Qn[i], Qt, c_qn, "Qn"), (qd[i], Qt, c_qd, "qd")) if urgent else
                                   ((kb[i], Kt, c_kb, "kb"), (kd[i], Kt, c_kd, "kd"), (bv[i], Vt, beta_, "bv")))
                            for (dst, src, sc, nm) in lst:
                                sc1 = sc[:, hh:hh + 1]
                                if beng == "act":
                                    S.op("act", lambda e, dst=dst, src=src, sc1=sc1: e.activation(out=dst, in_=src, func=AF.Copy,
                                                                                                 scale=sc1),
                                         reads=[PBH[i // 2]] + gr, writes=[B[nm]])
                                else:
                                    S.op("dve", lambda e, dst=dst, src=src, sc1=sc1: e.tensor_scalar(
                                        out=dst, in0=src, scalar1=sc1, scalar2=None, op0=ALU.mult),
                                        reads=[PBH[i // 2]] + gr, writes=[B[nm]])
                    if DBG <= 3.6:
                        S.barrier(); return
                    for i, hh in enumerate(heads):
                        pv_ = pbank_bf[i]
                        B = BH[i]
                        for j, (src, nm) in enumerate(((Kn[i], "Kn"), (Qn[i], "Qn"), (qd[i], "qd"))):
                            S.op("pe", lambda e, pv_=pv_, j=j, src=src: e.transpose(out=pv_[:, j * 128:(j + 1) * 128], in_=src,
                                                                                  identity=identb[:]),
                                 reads=[B[nm], B_const], writes=[PB[i]])
                    for i, hh in enumerate(heads):
                        pv_ = pbank_bf[i]
                        if i % 2 == 1:
                            S.op("dve", lambda e, i=i, pv_=pv_: e.tensor_copy(out=KQT[i], in_=pv_[:, 0:384]),
                                 reads=[PB[i]], writes=[BH[i]["KQT"]])
                        else:
                            S.op("act", lambda e, i=i, pv_=pv_: e.activation(out=KQT[i], in_=pv_[:, 0:384], func=AF.Copy),
                                 reads=[PB[i]], writes=[BH[i]["KQT"]])
                    if DBG <= 4:
                        S.barrier(); return
                    for i, hh in enumerate(heads):
                        B = BH[i]
                        KnT, QnT = KQT[i][:, 0:128], KQT[i][:, 128:256]
                        S.op("pe", lambda e, i=i, KnT=KnT: e.matmul(pbank[i][:, 0:128], lhsT=KnT, rhs=KnT, start=True, stop=True),
                             reads=[B["KQT"]], writes=[PB[i]])
                        S.op("pe", lambda e, i=i, KnT=KnT, QnT=QnT: e.matmul(pbank[i][:, 128:256], lhsT=KnT, rhs=QnT,
                                                                           start=True, stop=True),
                             reads=[B["KQT"]], writes=[PB[i]])
                        S.op("pe", lambda e, i=i: e.matmul(pbank[i][:, 256:384], lhsT=onesf[:], rhs=dg[i], start=True, stop=True),
                             reads=[B["dg"], B_const], writes=[PB[i]])
                    for i, hh in enumerate(heads):
                        B = BH[i]
                        S.op("dve", lambda e, i=i: e.tensor_tensor(out=tL[i], in0=pbank[i][:, 256:384], in1=mposS[:], op=ALU.add),
                             reads=[PB[i], B_const], writes=[B["tL"]])
                        S.op("dve", lambda e, i=i: e.tensor_tensor(out=tU[i], in0=pbank[i][:, 256:384], in1=mnegI[:], op=ALU.add),
                             reads=[PB[i], B_const], writes=[B["tU"]])
                        S.op("act", lambda e, i=i, hh=hh: e.activation(out=EL[i], in_=tL[i], func=AF.Exp, scale=-1.0,
                                                                       bias=gcs[:, hh:hh + 1]),
                             reads=[B["tL"]] + gr, writes=[B["EL"]])
                        S.op("act", lambda e, i=i, hh=hh: e.activation(out=EU[i], in_=tU[i], func=AF.Exp,
                                                                       bias=ngcs[:, hh:hh + 1]),
                             reads=[B["tU"]] + gr, writes=[B["EU"]])
                    for i, hh in enumerate(heads):
                        B = BH[i]
                        S.op("dve", lambda e, i=i, hh=hh: e.scalar_tensor_tensor(
                            out=PTa_r[i], in0=pbank[i][:, 0:128], scalar=nbeta[:, hh:hh + 1], in1=EL[i],
                            op0=ALU.mult, op1=ALU.mult), reads=[PB[i], B["EL"]] + gr, writes=[B["PTa"]])
                        S.op("dve", lambda e, i=i: e.tensor_tensor(out=qkTm[i], in0=pbank[i][:, 128:256], in1=EU[i], op=ALU.mult),
                             reads=[PB[i], B["EU"]], writes=[B["qkTm"]])
                    if DBG <= 5:
                        S.barrier(); return
                    for i, hh in enumerate(heads):
                        B = BH[i]
                        S.op("pe", lambda e, i=i: e.transpose(out=pbank[i][:, 0:128], in_=PTa[i], identity=identf[:]),
                             reads=[B["PTa"], B_const], writes=[PB[i]])
                        S.op("dve", lambda e, i=i: e.tensor_copy(out=Pa_r[i], in_=pbank[i][:, 0:128]),
                             reads=[PB[i]], writes=[B["Pa"]])
                        S.op("dve", lambda e, i=i: e.tensor_tensor(out=Xa_r[i], in0=pbank[i][:, 0:128], in1=identf[:], op=ALU.add),
                             reads=[PB[i], B_const], writes=[B["Xa"]])
                    cur = {"P": (Pa_r, "Pa"), "PT": (PTa_r, "PTa"), "X": (Xa_r, "Xa"), "PP": PPa_r}
                    alt = {"P": (Pb_r, "Pb"), "PT": (PTb_r, "PTb"), "X": (Xb_r, "Xb"), "PP": PPb_r}
                    for kk_ in range(1, 5):
                        last = (kk_ == 4)
                        (Pc, Pcn), (PTc, PTcn), (Xc, Xcn) = cur["P"], cur["PT"], cur["X"]
                        (Pn, Pnn), (PTn, PTnn), (Xn, Xnn) = alt["P"], alt["PT"], alt["X"]
                        PPn = alt["PP"]
                        for i, hh in enumerate(heads):
                            B = BH[i]
                            S.op("pe", lambda e, i=i, Pc=Pc, PTc=PTc: e.matmul(pbank[i][:, 0:128], lhsT=Pc[i], rhs=PTc[i],
                                                                             start=True, stop=True),
                                 reads=[B[Pcn], B[PTcn]], writes=[PB[i]])
                            if not last:
                                S.op("pe", lambda e, i=i, Pc=Pc, PTc=PTc: e.matmul(pbank[i][:, 128:256], lhsT=PTc[i], rhs=Pc[i],
                                                                                 start=True, stop=True),
                                     reads=[B[Pcn], B[PTcn]], writes=[PB[i]])
                        for i, hh in enumerate(heads):
                            B = BH[i]
                            if not last:
                                S.op("dve", lambda e, i=i, PPn=PPn: e.tensor_copy(out=PPn[i], in_=v3(pbank[i][:, 0:256], 128)),
                                     reads=[PB[i]], writes=[B[PTnn], B[Pnn]])
                            else:
                                S.op("dve", lambda e, i=i, PTn=PTn: e.tensor_copy(out=PTn[i], in_=pbank[i][:, 0:128]),
                                     reads=[PB[i]], writes=[B[PTnn]])
                        for i, hh in enumerate(heads):
                            B = BH[i]
                            S.op("pe", lambda e, i=i, PTn=PTn, Xc=Xc, ob=ob: e.matmul(pbank[ob][:, i * 128:(i + 1) * 128], lhsT=PTn[i],
                                                                                    rhs=Xc[i], start=True, stop=True),
                                 reads=[B[PTnn], B[Xcn]], writes=[PB[ob]])
                        for i, hh in enumerate(heads):
                            B = BH[i]
                            if not last:
                                S.op("dve", lambda e, i=i, Xn=Xn, Xc=Xc, ob=ob: e.tensor_tensor(
                                    out=Xn[i], in0=pbank[ob][:, i * 128:(i + 1) * 128], in1=Xc[i].bitcast(F32), op=ALU.add),
                                    reads=[PB[ob], B[Xcn]], writes=[B[Xnn]])
                            else:
                                S.op("dve", lambda e, i=i, Xc=Xc, ob=ob: e.tensor_tensor(
                                    out=TTb[i], in0=pbank[ob][:, i * 128:(i + 1) * 128], in1=Xc[i].bitcast(F32), op=ALU.add),
                                    reads=[PB[ob], B[Xcn]], writes=[B["TTb"]])
                        cur, alt = alt, cur
                    if DBG <= 6:
                        S.barrier(); return
                    for i, hh in enumerate(heads):
                        B = BH[i]
                        S.op("pe", lambda e, i=i: e.matmul(pbank[i][:, 0:128], lhsT=kb[i], rhs=TTb[i], start=True, stop=True),
                             reads=[B["kb"], B["TTb"]], writes=[PB[i]])
                        S.op("dve", lambda e, i=i: e.tensor_scalar(out=nwkT[i], in0=pbank[i][:, 0:128], scalar1=-1.0, scalar2=None,
                                                                    op0=ALU.mult),
                             reads=[PB[i]], writes=[B["nwkT"]])
                    for i, hh in enumerate(heads):
                        B = BH[i]
                        S.op("pe", lambda e, i=i: e.matmul(pbank[i][:, 128:256], lhsT=TTb[i], rhs=bv[i], start=True, stop=False),
                             reads=[B["TTb"], B["bv"]], writes=[PB[i]])
                        S.op("pe", lambda e, i=i, hh=hh: e.matmul(pbank[i][:, 128:256], lhsT=nwkT[i], rhs=Sb_st[:, hh, :],
                                                                 start=False, stop=True),
                             reads=[B["nwkT"], B_Sb[hh]], writes=[PB[i]])
                        S.op("dve", lambda e, i=i: e.tensor_copy(out=usb[i], in_=pbank[i][:, 128:256]),
                             reads=[PB[i]], writes=[B["usb"]])
                    for i, hh in enumerate(heads):
                        B = BH[i]
                        qdT = KQT[i][:, 256:384]
                        oc = slice(i * 128, (i + 1) * 128)
                        S.op("pe", lambda e, qdT=qdT, hh=hh, oc=oc, ob=ob: e.matmul(pbank[ob][:, oc], lhsT=qdT, rhs=Sb_st[:, hh, :],
                                                                                  start=True, stop=False),
                             reads=[B["KQT"], B_Sb[hh]], writes=[PB[ob]])
                        S.op("pe", lambda e, i=i, oc=oc, ob=ob: e.matmul(pbank[ob][:, oc], lhsT=qkTm[i], rhs=usb[i],
                                                                       start=False, stop=True),
                             reads=[B["qkTm"], B["usb"]], writes=[PB[ob]])
                        S.op("pe", lambda e, i=i: e.matmul(pbank[i][:, 256:384], lhsT=kd[i], rhs=usb[i], start=True, stop=True),
                             reads=[B["kd"], B["usb"]], writes=[PB[i]])
                        S.op("dve", lambda e, i=i, hh=hh: e.scalar_tensor_tensor(
                            out=S_st[:, hh, :], in0=S_st[:, hh, :], scalar=Egl[:, hh:hh + 1], in1=pbank[i][:, 256:384],
                            op0=ALU.mult, op1=ALU.add), reads=[PB[i], B_S[hh]] + gr, writes=[B_S[hh]])
                        S.op("act", lambda e, hh=hh: e.activation(out=Sb_st[:, hh, :], in_=S_st[:, hh, :], func=AF.Copy),
                             reads=[B_S[hh]], writes=[B_Sb[hh]])
                    for i, hh in enumerate(heads):
                        oc = slice(i * 128, (i + 1) * 128)
                        S.op("act", lambda e, oc=oc, ob=ob, hh=hh: e.activation(out=junk2, in_=pbank[ob][:, oc], func=AF.Square,
                                                                               accum_out=sso[:, hh:hh + 1]),
                             reads=[PB[ob]], writes=[B_j2, B_sso])
                if DBG <= 7:
                    S.barrier(); return
                S.op("act", lambda e: e.activation(out=rso, in_=sso, func=AF.Ln, scale=1.0 / 128, bias=EPS), reads=[B_sso], writes=[B_sso])
                S.op("act", lambda e: e.activation(out=rso, in_=rso, func=AF.Exp, scale=-0.5), reads=[B_sso], writes=[B_sso])
                for grp in range(2):
                    ob = 4 + grp
                    S.op("dve", lambda e, grp=grp, ob=ob: e.tensor_tensor(
                        out=on_all[:, grp * 4:(grp + 1) * 4, :], in0=v3(pbank[ob][:, :], 128),
                        in1=bc3(rso[:, grp * 4:(grp + 1) * 4], 128), op=ALU.mult),
                        reads=[PB[ob], B_sso], writes=[B_on])
                for hh in range(8):
                    S.op("pe", lambda e, hh=hh: e.transpose(out=pbankh[0][:, hh * 128:(hh + 1) * 128], in_=on_all[:, hh, :],
                                                            identity=identb[:]),
                         reads=[B_on, B_const], writes=[PBH[0]])
                S.op("dve", lambda e, blk=blk: e.tensor_tensor(out=ogT[:, :, blk], in0=v3(pbankh[0][:, :], 128),
                                                               in1=zsT[:, :, blk], op=ALU.mult),
                     reads=[PBH[0], B_zs], writes=[B_og[b]])
            for b in range(NB):
                do_block(b)
            resid_proj(lambda k, b: ogT[:, k, b * 128:(b + 1) * 128], 8,
                       lambda k, half: wout_t[:, k, half * 512:(half + 1) * 512], [B_wout], B_og)
            S.barrier()

        def conv_ffn(l):
            AFa.reset(); ABa.reset()
            norm_transpose()
            aT = v3(ABa.alloc(NFF * TT), TT)
            B_aT = [Buf("aT%d" % c) for c in range(NFF)]
            wdn = v3(ABa.alloc(NFF * 512), 512)
            wdn_b = v3(ABa.alloc(11 * 512), 512)
            B_wlo, B_whi, B_wb = Buf("wdn_lo"), Buf("wdn_hi"), Buf("wdn_b")

            def ld_wdn(dst, k0, k1, half, bufw):
                S.dma(lambda e: e.dma_start(
                    out=dst, in_=Wf_dn[l][k0 * 128:k1 * 128, half * 512:(half + 1) * 512].rearrange("(k p) n -> p k n", p=128)),
                    writes=[bufw])
            ld_wdn(wdn[:, 0:11, :], 0, 11, 0, B_wlo)
            ld_wdn(wdn[:, 11:22, :], 11, 22, 0, B_whi)
            ld_wdn(wdn_b[:, :, :], 0, 11, 1, B_wb)
            npre = 8
            acc = [AFa.alloc(TT) for _ in range(npre)]
            sg = [AFa.alloc(TT) for _ in range(4)]
            Bacc = [Buf("facc%d" % i) for i in range(npre)]
            Bsg = [Buf("sg%d" % i) for i in range(4)]
            cnt = [0]
            per = 2
            bi = 0
            for cb in range(0, NFF, per):
                n = min(per, NFF - cb)
                wg = load_wst(Wf_up[l], cb * 128, n * 128)
                wv = load_wst(Wf_up[l], DFF + cb * 128, n * 128)
                for j in range(n):
                    c = cb + j
                    items = []
                    for (wi, cg) in ((wg, c), (wv, NFF + c)):
                        bk = bi % 4
                        bi += 1
                        for k in range(8):
                            S.op("pe", lambda e, k=k, j=j, wi=wi, bk=bk: e.matmul(
                                pbank[bk][:, :], lhsT=wst[wi][:, k, j * 128:(j + 1) * 128], rhs=hnT[:, k, :],
                                start=(k == 0), stop=(k == 7)), reads=[B_wst[wi], B_hnT], writes=[PB[bk]])
                        i = cnt[0] % npre
                        cnt[0] += 1
                        items.append(dict(ps=pbank[bk], Bps=PB[bk], w=(lambda jj, cg=cg: fconv_c[:, l, cg, jj:jj + 1]),
                                          halo=halo_f[:, l, cg, :], Bhalo=B_halo_f[l][cg], ac=acc[i], Bac=Bacc[i]))
                    conv_from_psum(items, 3)
                    ag, Bag, av, Bav = items[0]["ac"], items[0]["Bac"], items[1]["ac"], items[1]["Bac"]
                    s_ = sg[c % 4]
                    S.op("act", lambda e, ag=ag, s_=s_: e.activation(out=s_, in_=ag, func=AF.Silu), reads=[Bag], writes=[Bsg[c % 4]])
                    S.op("pool", lambda e, c=c, av=av, s_=s_: e.tensor_tensor(out=aT[:, c, :], in0=s_, in1=av, op=ALU.mult),
                         reads=[Bsg[c % 4], Bav], writes=[B_aT[c]])
            for half in range(2):
                if half == 1:
                    ld_wdn(wdn[:, 11:22, :], 11, 22, 1, B_whi)
                for b in range(NB):
                    bk = 4 + (b % 2)
                    for k in range(NFF):
                        if k < 11:
                            rhs_, bw_ = (wdn[:, k, :], B_wlo) if half == 0 else (wdn_b[:, k, :], B_wb)
                        else:
                            rhs_, bw_ = wdn[:, k, :], B_whi
                        S.op("pe", lambda e, b=b, k=k, bk=bk, rhs_=rhs_: e.matmul(
                            pbank[bk][:, :], lhsT=aT[:, k, b * 128:(b + 1) * 128], rhs=rhs_,
                            start=(k == 0), stop=(k == NFF - 1)), reads=[bw_, B_aT[k]], writes=[PB[bk]])
                    hs = h_t[:, b, half * 512:(half + 1) * 512]
                    S.op("dve", lambda e, hs=hs, bk=bk: e.tensor_tensor(out=hs, in0=pbank[bk][:, :], in1=hs, op=ALU.add),
                         reads=[PB[bk], B_h[b]], writes=[B_h[b]])
            S.barrier()

        def ple(l, ti):
            AFa.reset(); ABa.reset()
            norm_transpose()
            wpr = v3(ABa.alloc(2 * D), D)
            B_wpr = Buf("wpr")
            S.dma(lambda e: e.dma_start(out=wpr, in_=Wp_pr[l].rearrange("(k p) n -> p k n", p=128)), writes=[B_wpr])
            S.dma(lambda e: e.dma_start(out=wout_t[:], in_=Wp_gt[l].rearrange("(k p) n -> p k n", p=128)), writes=[B_wout])
            pin = [AFa.alloc(256) for _ in range(2)]
            pT = [ABa.alloc(256) for _ in range(2)]
            gate = [AFa.alloc(512) for _ in range(2)]
            Bpin = [Buf("pin0"), Buf("pin1")]
            BpT = [Buf("pT0"), Buf("pT1")]
            Bgate = [Buf("gate0"), Buf("gate1")]
            gi = 0
            for b in range(NB):
                i = b % 2
                t0 = ti * TT + b * 128
                S.dma(lambda e, i=i, t0=t0: e.dma_start(out=pin[i], in_=p_d[l, t0:t0 + 128, :]), writes=[Bpin[i]])
                for k in range(2):
                    S.op("pe", lambda e, i=i, k=k: e.transpose(out=pbank[4][:, k * 128:(k + 1) * 128],
                                                               in_=pin[i][:, k * 128:(k + 1) * 128], identity=identf[:]),
                         reads=[Bpin[i], B_const], writes=[PB[4]])
                S.op("act", lambda e, i=i: e.activation(out=pT[i], in_=pbank[4][:, 0:256], func=AF.Copy), reads=[PB[4]], writes=[BpT[i]])
                for half in range(2):
                    hs = slice(half * 512, (half + 1) * 512)
                    gbk = half
                    pbk = 2 + half
                    for k in range(8):
                        S.op("pe", lambda e, b=b, k=k, hs=hs, gbk=gbk: e.matmul(
                            pbank[gbk][:, :], lhsT=hnT[:, k, b * 128:(b + 1) * 128], rhs=wout_t[:, k, hs],
                            start=(k == 0), stop=(k == 7)), reads=[B_hnT, B_wout], writes=[PB[gbk]])
                    for k in range(2):
                        S.op("pe", lambda e, i=i, k=k, hs=hs, pbk=pbk: e.matmul(
                            pbank[pbk][:, :], lhsT=pT[i][:, k * 128:(k + 1) * 128], rhs=wpr[:, k, hs],
                            start=(k == 0), stop=(k == 1)), reads=[BpT[i], B_wpr], writes=[PB[pbk]])
                    gt = gate[gi % 2]
                    Bg = Bgate[gi % 2]
                    gi += 1
                    S.op("act", lambda e, gt=gt, gbk=gbk: e.activation(out=gt, in_=pbank[gbk][:, :], func=AF.Sigmoid),
                         reads=[PB[gbk]], writes=[Bg])
                    S.op("dve", lambda e, gt=gt, pbk=pbk: e.tensor_tensor(out=gt, in0=pbank[pbk][:, :], in1=gt, op=ALU.mult),
                         reads=[PB[pbk], Bg], writes=[Bg])
                    S.op("dve", lambda e, gt=gt, b=b, hs=hs: e.tensor_tensor(out=h_t[:, b, hs], in0=h_t[:, b, hs], in1=gt, op=ALU.add),
                         reads=[Bg, B_h[b]], writes=[B_h[b]])
            S.barrier()

        def swa(ti):
            AFa.reset(); ABa.reset()
            norm_transpose()
            qT = v3(ABa.alloc(8 * TT), TT)
            kT0 = v3(ABa.alloc(4 * (TT + 128)), TT + 128)
            kT1 = v3(ABa.alloc(4 * (TT + 128)), TT + 128)
            kTz = (kT0, kT1)
            vx = ABa.alloc((NB + 1) * 4 * 72).rearrange("p (b g d) -> p b g d", g=4, d=72)
            aoT = v3(ABa.alloc(8 * TT), TT)
            B_qT = [Buf("qT%d" % c) for c in range(8)]
            B_kT = Buf("kT")
            B_vx = [Buf("vx%d" % b) for b in range(NB + 1)]
            B_ao = [Buf("ao%d" % b) for b in range(NB)]
            S.dma(lambda e: e.dma_start(out=wout_t[:], in_=Wb_out.rearrange("(k p) n -> p k n", p=128)), writes=[B_wout])
            S.op("pool", lambda e: e.memset(kT0, 0.0), writes=[B_kT])
            S.op("pool", lambda e: e.memset(kT1, 0.0), writes=[B_kT])
            S.op("pool", lambda e: e.tensor_copy(out=kT0[:, :, 0:128], in_=kT_prev[:, 0, :, :]), reads=[B_kprev], writes=[B_kT])
            S.op("pool", lambda e: e.tensor_copy(out=kT1[:, :, 0:128], in_=kT_prev[:, 1, :, :]), reads=[B_kprev], writes=[B_kT])
            S.op("pool", lambda e: e.tensor_copy(out=vx[:, 0, :, :], in_=vx_prev[:]), reads=[B_vprev], writes=[B_vx[0]])

            def consume(c, ps, Bps):
                if c < 8:
                    S.op("act", lambda e: e.activation(out=qT[:, c, :], in_=ps[:, :], func=AF.Copy), reads=[Bps], writes=[B_qT[c]])
                else:
                    g = c - 8
                    S.op("dve", lambda e: e.tensor_copy(out=kT0[0:64, g, 128:128 + TT], in_=ps[0:64, :]), reads=[Bps], writes=[B_kT])
                    S.op("dve", lambda e: e.tensor_copy(out=kT1[64:128, g, 128:128 + TT], in_=ps[64:128, :]), reads=[Bps], writes=[B_kT])
            proj_fm(Wb_in, 0, 12, consume)
            wv = ABa.alloc(8 * 256)
            wv3 = v3(wv, 256)
            B_wv = Buf("wv")
            S.dma(lambda e: e.dma_start(out=wv3, in_=Wb_in[:, 1536:1792].rearrange("(k p) n -> p k n", p=128)), writes=[B_wv])
            for b in range(NB):
                for k in range(8):
                    S.op("pe", lambda e, b=b, k=k: e.matmul(pbank[4][:, 0:256], lhsT=hnT[:, k, b * 128:(b + 1) * 128],
                                                            rhs=wv3[:, k, :], start=(k == 0), stop=(k == 7)),
                         reads=[B_hnT, B_wv], writes=[PB[4]])
                S.op("pool", lambda e, b=b: e.memset(vx[:, b + 1, :, :], 1.0), writes=[B_vx[b + 1]])
                S.op("act", lambda e, b=b: e.activation(out=vx[:, b + 1, :, 0:64], in_=v3(pbank[4][:, 0:256], 64), func=AF.Copy),
                     reads=[PB[4]], writes=[B_vx[b + 1]])
            S.op("pool", lambda e: e.tensor_copy(out=kT_prev[:, 0, :, :], in_=kT0[:, :, TT:TT + 128]), reads=[B_kT], writes=[B_kprev])
            S.op("pool", lambda e: e.tensor_copy(out=kT_prev[:, 1, :, :], in_=kT1[:, :, TT:TT + 128]), reads=[B_kT], writes=[B_kprev])
            S.op("pool", lambda e: e.tensor_copy(out=vx_prev[:], in_=vx[:, NB, :, :]), reads=[B_vx[NB]], writes=[B_vprev])

            tmp = [AFa.alloc(512) for _ in range(4)]
            ee = [ABa.alloc(512) for _ in range(4)]
            Btmp = [Buf("stmp%d" % i) for i in range(4)]
            Bee = [Buf("see%d" % i) for i in range(4)]
            ao = [ABa.alloc(D) for _ in range(2)]
            Bao_t = [Buf("aot0"), Buf("aot1")]
            den = smalls[:, 300:332]
            B_den = Buf("den")
            ri = [0]

            def swa_scores(b, g):
                first = (mode == "fused" and ti == 0 and b == 0)
                use_flag = (mode == "tile" and b == 0)
                parts = ([] if first else [0]) + [1]
                ebufs = {}
                for pc in parts:
                    bk = (2 * g + pc) % 4
                    r = ri[0] % 4
                    ri[0] += 1
                    kofs = b * 128 + pc * 128
                    for hh in range(4):
                        head = 4 * g + hh
                        c, half = head // 2, head % 2
                        kz = kTz[half]
                        S.op("pe", lambda e, bk=bk, kz=kz, g=g, kofs=kofs, c=c, b=b, hh=hh: e.matmul(
                            pbank[bk][:, hh * 128:(hh + 1) * 128], lhsT=kz[:, g, kofs:kofs + 128],
                            rhs=qT[:, c, b * 128:(b + 1) * 128], start=True, stop=True),
                            reads=[B_kT, B_qT[c]], writes=[PB[bk]])
                    S.op("dve", lambda e, bk=bk, r=r, g=g, pc=pc: e.scalar_tensor_tensor(
                        out=tmp[r], in0=pbank[bk][:, :], scalar=0.125, in1=swab[:, g, pc, :], op0=ALU.mult, op1=ALU.add),
                        reads=[PB[bk], B_const], writes=[Btmp[r]])
                    if use_flag and pc == 0:
                        S.op("act", lambda e, r=r: e.activation(out=ee[r], in_=tmp[r], func=AF.Exp, bias=flag_t[:, 0:1]),
                             reads=[Btmp[r], B_const], writes=[Bee[r]])
                    else:
                        S.op("act", lambda e, r=r: e.activation(out=ee[r], in_=tmp[r], func=AF.Exp), reads=[Btmp[r]], writes=[Bee[r]])
                    ebufs[pc] = r
                return parts, ebufs

            def swa_pv(b, g, parts, ebufs):
                ao_b = ao[b % 2]
                ps_o = pbank[4 + (g % 2)]
                Bpo = PB[4 + (g % 2)]
                for hh in range(4):
                    for n_, pc in enumerate(parts):
                        r = ebufs[pc]
                        S.op("pe", lambda e, ps_o=ps_o, hh=hh, r=r, b=b, pc=pc, g=g, n_=n_, np_=len(parts): e.matmul(
                            ps_o[:, hh * 72:hh * 72 + 66], lhsT=ee[r][:, hh * 128:(hh + 1) * 128], rhs=vx[:, b + pc, g, 0:66],
                            start=(n_ == 0), stop=(n_ == np_ - 1)),
                            reads=[Bee[r], B_vx[b + pc]], writes=[Bpo])
                dn = den[:, g * 4:(g + 1) * 4]
                po3 = ps_o[:, 0:288].rearrange("p (h d) -> p h d", d=72)
                S.op("dve", lambda e, dn=dn, po3=po3, g=g: e.tensor_tensor(
                    out=dn, in0=po3[:, :, 64], in1=esink_bc[:, g * 4:(g + 1) * 4], op=ALU.add),
                    reads=[Bpo, B_const], writes=[B_den])
                S.op("dve", lambda e, dn=dn: e.reciprocal(out=dn, in_=dn), reads=[B_den], writes=[B_den])
                S.op("dve", lambda e, dn=dn, po3=po3, g=g, ao_b=ao_b: e.tensor_tensor(
                    out=ao_b[:, g * 256:(g + 1) * 256].rearrange("p (h d) -> p h d", d=64), in0=po3[:, :, 0:64],
                    in1=bc3(dn, 64), op=ALU.mult),
                    reads=[Bpo, B_den], writes=[Bao_t[b % 2]])

            def swa_finish(b):
                ao_b = ao[b % 2]
                for k in range(8):
                    S.op("pe", lambda e, k=k, ao_b=ao_b: e.transpose(out=pbankh[1][:, k * 128:(k + 1) * 128],
                                                                     in_=ao_b[:, k * 128:(k + 1) * 128], identity=identb[:]),
                         reads=[Bao_t[b % 2], B_const], writes=[PBH[1]])
                S.op("act", lambda e, b=b: e.activation(out=aoT[:, :, b * 128:(b + 1) * 128], in_=v3(pbankh[1][:, :], 128), func=AF.Copy),
                     reads=[PBH[1]], writes=[B_ao[b]])

            items_ = [(b, g) for b in range(NB) for g in range(4)]
            pend = swa_scores(*items_[0])
            for k_, (b, g) in enumerate(items_):
                nxt = swa_scores(*items_[k_ + 1]) if k_ + 1 < len(items_) else None
                swa_pv(b, g, *pend)
                if g == 3:
                    swa_finish(b)
                pend = nxt
            resid_proj(lambda k, b: aoT[:, k, b * 128:(b + 1) * 128], 8,
                       lambda k, half: wout_t[:, k, half * 512:(half + 1) * 512], [B_wout], B_ao, banks=(0, 1))
            S.barrier()

        def final_out(ti, do_norm=True):
            AFa.reset(); ABa.reset()
            ss = smalls[:, 0:NB]
            rstd = smalls[:, 8:8 + NB]
            Bss = Buf("fss")
            junk = ABa.alloc(D)
            Bj = Buf("fjunk")
            ot = [AFa.alloc(D) for _ in range(2)]
            Bot = [Buf("ot0"), Buf("ot1")]
            if do_norm:
                for b in range(NB):
                    S.op("act", lambda e, b=b: e.activation(out=junk, in_=h_t[:, b, :], func=AF.Square, accum_out=ss[:, b:b + 1]),
                         reads=[B_h[b]], writes=[Bj, Bss])
                S.op("act", lambda e: e.activation(out=rstd, in_=ss, func=AF.Ln, scale=1.0 / D, bias=EPS), reads=[Bss], writes=[Bss])
                S.op("act", lambda e: e.activation(out=rstd, in_=rstd, func=AF.Exp, scale=-0.5), reads=[Bss], writes=[Bss])
            for b in range(NB):
                t0 = ti * TT + b * 128
                if do_norm:
                    o_ = ot[b % 2]
                    S.op("dve", lambda e, b=b, o_=o_: e.scalar_tensor_tensor(out=o_, in0=h_t[:, b, :], scalar=rstd[:, b:b + 1],
                                                                            in1=nfin_bc[:], op0=ALU.mult, op1=ALU.mult),
                         reads=[B_h[b], Bss, B_const], writes=[Bot[b % 2]])
                    S.dma(lambda e, o_=o_, t0=t0: e.dma_start(out=out_d[t0:t0 + 128, :], in_=o_), reads=[Bot[b % 2]], final=True)
                else:
                    S.dma(lambda e, b=b, t0=t0: e.dma_start(out=out_d[t0:t0 + 128, :], in_=h_t[:, b, :]), reads=[B_h[b]], final=True)
            S.barrier()

        const_setup()
        S.barrier()
        if mode != "tile":
            weight_prep()
        if mode != "prep":
            stages = [s_ for s_ in ["delta", "ffn0", "ple0", "swa", "ffn1", "ple1"] if s_ in STAGES]
            for ti in range(n_tiles):
                for b in range(NB):
                    t0 = ti * TT + b * 128
                    S.dma(lambda e, b=b, t0=t0: e.dma_start(out=h_t[:, b, :], in_=x_d[t0:t0 + 128, :]), writes=[B_h[b]])
                done = False
                for sname in stages:
                    if stop_after == "prep":
                        done = True
                        break
                    if sname == "delta":
                        deltanet(ti)
                    elif sname == "ffn0":
                        conv_ffn(0)
                    elif sname == "ple0":
                        ple(0, ti)
                    elif sname == "swa":
                        swa(ti)
                    elif sname == "ffn1":
                        conv_ffn(1)
                    elif sname == "ple1":
                        ple(1, ti)
                    if stop_after == sname:
                        done = True
                        break
                final_out(ti, do_norm=not done)
            if mode == "tile":
                S.barrier()
                for (t_, dst_) in ((halo_a, ha_out), (halo_f, hf_out), (S_st, S_out), (kT_prev, kp_out), (vx_prev, vp_out)):
                    S.dma(lambda e, t_=t_, dst_=dst_: e.dma_start(out=dst_, in_=t_[:]), final=True)
        S.emit()
    return nc


_PREP_IN = ["norm_mix", "norm_ffn", "norm_ple", "a_w_in", "a_norm", "a_w_out", "b_w_in", "b_w_out",
            "f_w_up", "f_w_down", "ple_w_proj", "ple_w_gate"]
_TILE_PARAMS = ["norm_final", "a_conv", "a_log", "a_dt_bias", "b_sinks", "f_conv"]
_SQUEEZE = {"a_w_in", "a_conv", "a_log", "a_dt_bias", "a_norm", "a_w_out", "b_w_in", "b_sinks", "b_w_out"}
_WB = ["Wa_in", "Wa_ba", "Wa_out", "Wb_in", "Wb_out", "Wf_up0", "Wf_up1", "Wf_dn0", "Wf_dn1",
       "Wp_pr0", "Wp_pr1", "Wp_gt0", "Wp_gt1"]


def _prm(inputs, n):
    a = np.asarray(inputs[n], dtype=np.float32)
    if n in _SQUEEZE:
        a = a[0]
    return np.ascontiguousarray(a)


def make_in_maps(inputs, n_cores=8):
    shared = {n: _prm(inputs, n) for n in _PREP_IN + _TILE_PARAMS}
    x = np.asarray(inputs["x"], dtype=np.float32)
    p = np.asarray(inputs["p"], dtype=np.float32)
    maps = []
    for c in range(n_cores):
        m = dict(shared)
        m["x"] = np.ascontiguousarray(x[c])
        m["p"] = np.ascontiguousarray(p[:, c])
        maps.append(m)
    return maps


def kernel_tiles(inputs, n_cores=8, n_tiles=NT):
    import ml_dtypes
    bf = ml_dtypes.bfloat16
    nc_p = build_program(mode="prep")
    res = run_bass_kernel_spmd(nc_p, [{n: _prm(inputs, n) for n in _PREP_IN}], core_ids=[0])
    wb = {n: np.asarray(res.results[0][n]) for n in _WB}
    params = {n: _prm(inputs, n) for n in _TILE_PARAMS}
    x = np.asarray(inputs["x"], dtype=np.float32)
    p = np.asarray(inputs["p"], dtype=np.float32)
    nc_t = build_program(mode="tile")
    st = [{"S_in": np.zeros((128, 8, 128), np.float32), "ha_in": np.zeros((128, 24, 3), np.float32),
           "hf_in": np.zeros((128, 2, 44, 2), np.float32), "kp_in": np.zeros((128, 2, 4, 128), bf),
           "vp_in": np.zeros((128, 4, 72), bf)} for _ in range(n_cores)]
    out = np.zeros((n_cores, n_tiles * TT, D), np.float32)
    for ti in range(n_tiles):
        flag = np.full((128, 1), -BIG if ti == 0 else 0.0, np.float32)
        maps = []
        for c in range(n_cores):
            m = dict(wb)
            m.update(params)
            m.update(st[c])
            m["flag_in"] = flag
            m["x"] = np.ascontiguousarray(x[c, ti * TT:(ti + 1) * TT])
            m["p"] = np.ascontiguousarray(p[:, c, ti * TT:(ti + 1) * TT])
            maps.append(m)
        r = run_bass_kernel_spmd(nc_t, maps, core_ids=list(range(n_cores)))
        for c in range(n_cores):
            rc = r.results[c]
            out[c, ti * TT:(ti + 1) * TT] = np.asarray(rc["out"], dtype=np.float32)
            st[c] = {"S_in": np.asarray(rc["S_out"]), "ha_in": np.asarray(rc["ha_out"]), "hf_in": np.asarray(rc["hf_out"]),
                     "kp_in": np.asarray(rc["kp_out"]), "vp_in": np.asarray(rc["vp_out"])}
    return out


def kernel(**inputs):
    nc = build_program(mode="fused")
    maps = make_in_maps(inputs)
    res = run_bass_kernel_spmd(nc, maps, core_ids=list(range(8)))
    return np.stack([np.asarray(r["out"], dtype=np.float32) for r in res.results], axis=0)
```
